# Optimizing a Trainium2 kernel written in Bass

```python
import math
import jax, jax.numpy as jnp
from jax import lax
import numpy as np

D_MODEL = 1024
BATCH = 8
SEQ = 2048
DEPTH = 4

MEM_LEN = 256
RNN_WIDTH = D_MODEL
RNN_BLOCKS = 4
RNN_BLOCK = RNN_WIDTH // RNN_BLOCKS
CONV_WIDTH = 4
LRU_C = 8.0
HEAD_DIM = 64
N_Q_HEADS = D_MODEL // HEAD_DIM
N_KV_HEADS = 2
GROUP = N_Q_HEADS // N_KV_HEADS
ATTN_WIDTH = N_Q_HEADS * HEAD_DIM
KV_WIDTH = N_KV_HEADS * HEAD_DIM
WINDOW = 128
BLOCK = 128
ROPE_THETA = 500000.0
ROT_DIM = HEAD_DIM // 4
IN_COLS = 2 * RNN_WIDTH + ATTN_WIDTH + 2 * KV_WIDTH + 2 * D_MODEL
CROSS_HEADS = 4
CROSS_HEAD_DIM = D_MODEL // CROSS_HEADS
CROSS_WIDTH = CROSS_HEADS * CROSS_HEAD_DIM
D_FF = -(-8 * D_MODEL // (3 * 256)) * 256
LN_EPS = 1e-5
DEEPNORM_ALPHA = (2 * DEPTH) ** 0.25
DEEPNORM_BETA = (8 * DEPTH) ** -0.25
NEG_INF = -1e30

kernel_name = "hawk_swa_sink_hybrid_deepnorm_trunk"


def layer_norm(x, g, b):
    xf = x.astype(jnp.float32)
    mu = jnp.mean(xf, axis=-1, keepdims=True)
    var = jnp.mean(jnp.square(xf - mu), axis=-1, keepdims=True)
    y = (xf - mu) * lax.rsqrt(var + LN_EPS)
    return (y * g.astype(jnp.float32) + b.astype(jnp.float32)).astype(x.dtype)


def rope_tables(seq_len):
    pos = jnp.arange(seq_len, dtype=jnp.float32)
    inv_freq = ROPE_THETA ** (-jnp.arange(0, ROT_DIM, 2, dtype=jnp.float32) / ROT_DIM)
    ang = pos[:, None] * inv_freq[None, :]
    return jnp.cos(ang), jnp.sin(ang)


def apply_partial_rope(t, cos, sin):
    half = ROT_DIM // 2
    c = cos[None, :, None, :].astype(t.dtype)
    s = sin[None, :, None, :].astype(t.dtype)
    t1, t2, rest = t[..., :half], t[..., half:ROT_DIM], t[..., ROT_DIM:]
    return jnp.concatenate([t1 * c - t2 * s, t2 * c + t1 * s, rest], axis=-1)


def rglru_branch(xr, gr, conv_w, conv_b, w_rg, b_rg, w_ig, b_ig, lru_lambda):
    B, S, _ = xr.shape
    xp = jnp.pad(xr, ((0, 0), (CONV_WIDTH - 1, 0), (0, 0)))
    xc = conv_b
    for k in range(CONV_WIDTH):
        xc = xc + xp[:, k:k + S] * conv_w[k]
    xb = xc.reshape(B, S, RNN_BLOCKS, RNN_BLOCK)
    r = jax.nn.sigmoid(jnp.einsum('bsnc,ncd->bsnd', xb, w_rg).reshape(B, S, RNN_WIDTH) + b_rg)
    i = jax.nn.sigmoid(jnp.einsum('bsnc,ncd->bsnd', xb, w_ig).reshape(B, S, RNN_WIDTH) + b_ig)
    log_a = -LRU_C * r.astype(jnp.float32) * jax.nn.softplus(-lru_lambda.astype(jnp.float32))
    a = jnp.exp(log_a)
    mult = jnp.sqrt(-jnp.expm1(2.0 * log_a))
    b_in = mult * (i * xc).astype(jnp.float32)

    def combine(lhs, rhs):
        a1, b1 = lhs
        a2, b2 = rhs
        return a1 * a2, a2 * b1 + b2

    _, h = lax.associative_scan(combine, (a, b_in), axis=1)
    return h.astype(xr.dtype) * jax.nn.gelu(gr)


def swa_sink_branch(q, k, v, sinks, cos, sin):
    B, S, _ = q.shape
    NB = S // BLOCK
    q = apply_partial_rope(q.reshape(B, S, N_Q_HEADS, HEAD_DIM), cos, sin)
    k = apply_partial_rope(k.reshape(B, S, N_KV_HEADS, HEAD_DIM), cos, sin)
    v = v.reshape(B, S, N_KV_HEADS, HEAD_DIM)
    qb = q.reshape(B, NB, BLOCK, N_KV_HEADS, GROUP, HEAD_DIM)

    def band(t):
        tp = jnp.pad(t, ((0, 0), (BLOCK, 0), (0, 0), (0, 0))).reshape(B, NB + 1, BLOCK, N_KV_HEADS, HEAD_DIM)
        return jnp.concatenate([tp[:, :-1], tp[:, 1:]], axis=2)

    kb, vb = band(k), band(v)
    scores = jnp.einsum('bnqhgd,bnjhd->bnhgqj', qb, kb).astype(jnp.float32) * (HEAD_DIM ** -0.5)
    blk = jnp.arange(NB)[:, None, None]
    qpos = blk * BLOCK + jnp.arange(BLOCK)[None, :, None]
    kpos = (blk - 1) * BLOCK + jnp.arange(2 * BLOCK)[None, None, :]
    valid = (kpos <= qpos) & (kpos > qpos - WINDOW) & (kpos >= 0)
    scores = jnp.where(valid[None, :, None, None], scores, NEG_INF)
    sink = sinks.astype(jnp.float32).reshape(N_KV_HEADS, GROUP)[None, None, :, :, None, None]
    sink = jnp.broadcast_to(sink, scores.shape[:-1] + (1,))
    probs = jax.nn.softmax(jnp.concatenate([scores, sink], axis=-1), axis=-1)[..., :-1]
    out = jnp.einsum('bnhgqj,bnjhd->bnqhgd', probs.astype(vb.dtype), vb)
    return out.reshape(B, S, ATTN_WIDTH)


def hybrid_mixer(u, w_in, conv_w, conv_b, w_rg, b_rg, w_ig, b_ig, lru_lambda,
                 w_br_rnn, w_br_attn, sinks, w_out, cos, sin):
    widths = (RNN_WIDTH, RNN_WIDTH, ATTN_WIDTH, KV_WIDTH, KV_WIDTH, D_MODEL, D_MODEL)
    points = np.cumsum(widths)[:-1].tolist()
    proj = u @ w_in
    xr, gr, q, k, v, g_rnn, g_attn = jnp.split(proj, points, axis=-1)
    y_rnn = rglru_branch(xr, gr, conv_w, conv_b, w_rg, b_rg, w_ig, b_ig, lru_lambda)
    y_attn = swa_sink_branch(q, k, v, sinks, cos, sin)
    merged = jax.nn.sigmoid(g_rnn) * (y_rnn @ w_br_rnn) + jax.nn.sigmoid(g_attn) * (y_attn @ w_br_attn)
    return merged @ w_out


def cross_attention(u, mem, cq_w, ckv_w, co_w):
    B, S, _ = u.shape
    M = mem.shape[1]
    q = (u @ cq_w).reshape(B, S, CROSS_HEADS, CROSS_HEAD_DIM)
    k, v = jnp.split(mem @ ckv_w, 2, axis=-1)
    k = k.reshape(B, M, CROSS_HEADS, CROSS_HEAD_DIM)
    v = v.reshape(B, M, CROSS_HEADS, CROSS_HEAD_DIM)
    s = jnp.einsum('bshd,bmhd->bhsm', q, k).astype(jnp.float32) * (CROSS_HEAD_DIM ** -0.5)
    p = jax.nn.softmax(s, axis=-1)
    o = jnp.einsum('bhsm,bmhd->bshd', p.astype(v.dtype), v).reshape(B, S, CROSS_WIDTH)
    return o @ co_w


def swiglu(u, wi, wo):
    gate, up = jnp.split(u @ wi, 2, axis=-1)
    return (jax.nn.silu(gate) * up) @ wo


def setup_inputs(seed: int = 0) -> dict:
    key = jax.random.key(seed)
    ks = jax.random.split(key, 26)
    L = DEPTH
    f32 = jnp.float32

    def nrm(k, shape, scale):
        return jax.random.normal(k, shape, f32) * scale

    u = jax.random.uniform(ks[9], (L, RNN_WIDTH), f32, 0.9, 0.999)
    p = u ** (1.0 / LRU_C)
    lru_lambda = jnp.log(p) - jnp.log1p(-p)
    return {
        "x": nrm(ks[0], (BATCH, SEQ, D_MODEL), 1.0),
        "mem": nrm(ks[1], (BATCH, MEM_LEN, D_MODEL), 1.0),
        "w_in": nrm(ks[2], (L, D_MODEL, IN_COLS), D_MODEL ** -0.5),
        "conv_w": nrm(ks[3], (L, CONV_WIDTH, RNN_WIDTH), CONV_WIDTH ** -0.5),
        "conv_b": nrm(ks[4], (L, RNN_WIDTH), 0.01),
        "w_rg": nrm(ks[5], (L, RNN_BLOCKS, RNN_BLOCK, RNN_BLOCK), RNN_BLOCK ** -0.5),
        "b_rg": nrm(ks[6], (L, RNN_WIDTH), 0.01),
        "w_ig": nrm(ks[7], (L, RNN_BLOCKS, RNN_BLOCK, RNN_BLOCK), RNN_BLOCK ** -0.5),
        "b_ig": nrm(ks[8], (L, RNN_WIDTH), 0.01),
        "lru_lambda": lru_lambda,
        "w_br_rnn": nrm(ks[10], (L, RNN_WIDTH, D_MODEL), RNN_WIDTH ** -0.5),
        "w_br_attn": nrm(ks[11], (L, ATTN_WIDTH, D_MODEL), ATTN_WIDTH ** -0.5),
        "sinks": nrm(ks[12], (L, N_Q_HEADS), 0.5),
        "w_out": nrm(ks[13], (L, D_MODEL, D_MODEL), DEEPNORM_BETA * D_MODEL ** -0.5),
        "ln1_g": 1.0 + nrm(ks[14], (L, D_MODEL), 0.02),
        "ln1_b": nrm(ks[15], (L, D_MODEL), 0.02),
        "cq_w": nrm(ks[16], (L, D_MODEL, CROSS_WIDTH), D_MODEL ** -0.5),
        "ckv_w": nrm(ks[17], (L, D_MODEL, 2 * CROSS_WIDTH), D_MODEL ** -0.5),
        "co_w": nrm(ks[18], (L, CROSS_WIDTH, D_MODEL), DEEPNORM_BETA * CROSS_WIDTH ** -0.5),
        "ln2_g": 1.0 + nrm(ks[19], (L, D_MODEL), 0.02),
        "ln2_b": nrm(ks[20], (L, D_MODEL), 0.02),
        "ffn_wi": nrm(ks[21], (L, D_MODEL, 2 * D_FF), D_MODEL ** -0.5),
        "ffn_wo": nrm(ks[22], (L, D_FF, D_MODEL), DEEPNORM_BETA * D_FF ** -0.5),
        "ln3_g": 1.0 + nrm(ks[23], (L, D_MODEL), 0.02),
        "ln3_b": nrm(ks[24], (L, D_MODEL), 0.02),
    }


def reference(x, mem, w_in, conv_w, conv_b, w_rg, b_rg, w_ig, b_ig, lru_lambda,
              w_br_rnn, w_br_attn, sinks, w_out, ln1_g, ln1_b,
              cq_w, ckv_w, co_w, ln2_g, ln2_b,
              ffn_wi, ffn_wo, ln3_g, ln3_b):
    cos, sin = rope_tables(x.shape[1])
    h = x
    for l in range(DEPTH):
        mix = hybrid_mixer(h, w_in[l], conv_w[l], conv_b[l], w_rg[l], b_rg[l], w_ig[l], b_ig[l],
                           lru_lambda[l], w_br_rnn[l], w_br_attn[l], sinks[l], w_out[l], cos, sin)
        h = layer_norm(DEEPNORM_ALPHA * h + mix, ln1_g[l], ln1_b[l])
        h = layer_norm(DEEPNORM_ALPHA * h + cross_attention(h, mem, cq_w[l], ckv_w[l], co_w[l]),
                       ln2_g[l], ln2_b[l])
        h = layer_norm(DEEPNORM_ALPHA * h + swiglu(h, ffn_wi[l], ffn_wo[l]), ln3_g[l], ln3_b[l])
    return h
```

```python
import math
_KCUT = 99
LN_G_ENG = "dve"
from contextlib import ExitStack

import numpy as np
import concourse.bass as bass
import concourse.mybir as mybir
from concourse.bass_utils import run_bass_kernel_spmd

F32 = mybir.dt.float32
BF16 = mybir.dt.bfloat16
AF = mybir.ActivationFunctionType
ALU = mybir.AluOpType

D = 1024
KC = 8
MEM = 256
DFF = 2816
NFC = 22
IN_COLS = 5376
ALPHA = 8.0 ** 0.25
LN_EPS = 1e-5
ROPE_THETA = 500000.0
EPOCH = 16000
NEPOCH = 8
NDMA = 24
WSLOT = 4096
NWSLOT = 4
NV = 112

V_CONVW = 0
V_CONVB = 32
V_BRG = 40
V_BIG = 48
V_LAM = 56


class Tile:
    __slots__ = ("w", "rs", "excl")

    def __init__(self, fence=None, excl=False):
        self.w = None
        self.rs = dict(fence) if fence else {}
        self.excl = excl


class Eng:
    def __init__(self, name, h, key, sems, self_sync):
        self.name = name
        self.h = h
        self.key = key
        self.sems = sems
        self.seq = 0
        self.waited = {}
        self.self_sync = self_sync


class MK:
    def __init__(self, nc, st, self_sync=True):
        self.nc = nc
        self.E = {}
        for name, h, ss in (("pe", nc.tensor, False), ("act", nc.scalar, self_sync),
                            ("dve", nc.vector, self_sync), ("pool", nc.gpsimd, self_sync),
                            ("sp", nc.sync, False)):
            sems = [st.enter_context(nc.semaphore(f"s_{name}{i}")) for i in range(NEPOCH if name != "sp" else 1)]
            self.E[name] = Eng(name, h, ("e", name), sems, ss)
        self.dsem = [st.enter_context(nc.semaphore(f"s_dma{i}")) for i in range(NDMA)]
        self.dtot = [0] * NDMA
        self.drr = {"pool": 0, "sp": 0}
        self.dbase = {"pool": (0, NDMA // 2), "sp": (NDMA // 2, NDMA - NDMA // 2)}
        self.fence = {}
        self.nops = 0
        self.marks = []

    def mark(self, name):
        self.marks.append((name, self.E['pe'].seq))

    def tile(self):
        return Tile(self.fence)

    def tiles(self, *shape):
        if len(shape) == 1:
            return [self.tile() for _ in range(shape[0])]
        return [self.tiles(*shape[1:]) for _ in range(shape[0])]

    def set_fence(self):
        f = {}
        for e in self.E.values():
            if e.seq > 0:
                f[e.key] = e.seq
        self.fence = f

    def _sem(self, key, val):
        if key[0] == "e":
            e = self.E[key[1]]
            ep = (val - 1) // EPOCH
            return e.sems[ep], val - ep * EPOCH
        return self.dsem[key[1]], val

    def _waits(self, E, reads, writes, extra=None):
        need = {}
        for t in reads:
            if t.w is not None:
                k, v = t.w
                if need.get(k, 0) < v:
                    need[k] = v
            if t.excl:
                for k, v in t.rs.items():
                    if k != E.key and need.get(k, 0) < v:
                        need[k] = v
        for t in writes:
            if t.w is not None:
                k, v = t.w
                if need.get(k, 0) < v:
                    need[k] = v
            for k, v in t.rs.items():
                if need.get(k, 0) < v:
                    need[k] = v
        if extra:
            for k, v in extra:
                if need.get(k, 0) < v:
                    need[k] = v
        for k, v in need.items():
            if k == E.key and not E.self_sync:
                continue
            if E.waited.get(k, 0) >= v:
                continue
            E.waited[k] = v
            sem, sv = self._sem(k, v)
            E.h.wait_ge(sem, sv)

    def _mark(self, dep, reads, writes):
        k, v = dep
        for t in reads:
            if t.rs.get(k, 0) < v:
                t.rs[k] = v
        for t in writes:
            t.w = dep
            t.rs = {}

    def op(self, eng, fn, reads=(), writes=()):
        E = self.E[eng]
        self._waits(E, reads, writes)
        ins = fn(E.h)
        E.seq += 1
        ep = (E.seq - 1) // EPOCH
        assert ep < len(E.sems), f"too many instructions on {eng}"
        ins.then_inc(E.sems[ep], 1)
        self._mark((E.key, E.seq), reads, writes)
        self.nops += 1
        return ins

    def dma(self, queue, out, in_, reads=(), writes=()):
        Q = self.E[queue]
        base, cnt = self.dbase[queue]
        i = base + self.drr[queue] % cnt
        self.drr[queue] += 1
        extra = [(("d", i), self.dtot[i])] if self.dtot[i] > 0 else None
        self._waits(Q, reads, writes, extra)
        ins = Q.h.dma_start(out=out, in_=in_)
        ins.then_inc(self.dsem[i], 16)
        self.dtot[i] += 16
        self._mark((("d", i), self.dtot[i]), reads, writes)
        return ins

    def dma_multi(self, queue, pairs, reads=(), writes=()):
        Q = self.E[queue]
        base, cnt = self.dbase[queue]
        i = base + self.drr[queue] % cnt
        self.drr[queue] += 1
        extra = [(("d", i), self.dtot[i])] if self.dtot[i] > 0 else None
        self._waits(Q, reads, writes, extra)
        for out, in_ in pairs:
            ins = Q.h.dma_start(out=out, in_=in_)
            ins.then_inc(self.dsem[i], 16)
            self.dtot[i] += 16
        self._mark((("d", i), self.dtot[i]), reads, writes)

    def wait_all(self, eng, tiles):
        E = self.E[eng]
        self._waits(E, tiles, ())


def _fm(v):
    return np.ascontiguousarray(v.reshape(KC, 128).T)


def _consts(S):
    c = np.zeros((128, 128 + 512 + 512 + 128), np.float32)
    c[:, 0:128] = np.eye(128, dtype=np.float32)
    p = np.arange(128)[:, None]
    f = np.arange(128)[None, :]
    cur = np.where(p <= f, 0.0, -30000.0).astype(np.float32)
    prev = np.where(p > f, 0.0, -30000.0).astype(np.float32)
    c[:, 128:640] = np.tile(cur, (1, 4))
    c[:, 640:1152] = np.tile(prev, (1, 4))
    pm = np.zeros((128, 128), np.float32)
    for m in range(128):
        j = m % 64
        if j < 8:
            pm[m + 8, m] = 1.0
        elif j < 16:
            pm[m - 8, m] = 1.0
    c[:, 1152:1280] = pm
    pos = np.arange(S, dtype=np.float32)
    inv = (np.float32(ROPE_THETA) ** (-(np.arange(0, 16, 2, dtype=np.float32)) / np.float32(16))).astype(np.float32)
    ang = (pos[None, :] * inv[:, None]).astype(np.float32)
    cs = np.cos(ang.astype(np.float64)).astype(np.float32)
    sn = np.sin(ang.astype(np.float64)).astype(np.float32)
    C = np.ones((128, S), np.float32)
    Sg = np.zeros((128, S), np.float32)
    for m in range(128):
        j = m % 64
        if j < 8:
            C[m] = cs[j]
            Sg[m] = -sn[j]
        elif j < 16:
            C[m] = cs[j - 8]
            Sg[m] = sn[j - 8]
    return c, np.stack([C, Sg], 0)


ARENA_BYTES = 77 * 1024


def build_program(S=2048, DEPTH=4, TS=1024, subs=("mixer", "cross", "ffn"), self_sync=True, mix_stage=7):
    NSEG = S // TS
    NT = TS // 128
    NG = TS // 512
    NTT = S // 128
    nc = bass.Bass("TRN2", target_bir_lowering=False)
    L = DEPTH

    def din(name, shape):
        return nc.dram_tensor(name, list(shape), F32, kind="ExternalInput").ap()

    x_d = din("x", [S, D])
    mem_d = din("mem", [MEM, D])
    w_in_d = din("w_in", [L, D, IN_COLS])
    wk2_d = din("wk2", [L, D, 256])
    w_rg_d = din("w_rg", [L, 4, 256, 256])
    w_ig_d = din("w_ig", [L, 4, 256, 256])
    w_brr_d = din("w_br_rnn", [L, D, D])
    w_bra_d = din("w_br_attn", [L, D, D])
    w_out_d = din("w_out", [L, D, D])
    cq_d = din("cq_w", [L, D, D])
    ckv_d = din("ckv_w", [L, D, 2 * D])
    co_d = din("co_w", [L, D, D])
    wi_d = din("ffn_wi", [L, D, 2 * DFF])
    wo_d = din("ffn_wo", [L, DFF, D])
    vecs_d = din("vecs", [L, 128, NV])
    lnrow_d = din("lnrow", [L, 6, D])
    sinkcol_d = din("sinkcol", [L, 128, 8])
    consts_d = din("consts", [128, 1280])
    rope_d = din("rope", [2, 128, S])
    y_d = nc.dram_tensor("y", [S, D], F32, kind="ExternalOutput").ap()

    with ExitStack() as st:
        mk = MK(nc, st, self_sync=self_sync)

        def sb(name, shape, dt):
            return st.enter_context(nc.sbuf_tensor("sb_" + name, list(shape), dt))

        H = sb("H", [128, NTT, D], F32)
        HT = sb("HT", [128, KC, TS], BF16)
        ident = sb("ident", [128, 128], BF16)
        maskc = sb("maskc", [128, 512], BF16)
        maskp = sb("maskp", [128, 512], BF16)
        pmT = sb("pmT", [128, 128], BF16)
        ones = sb("ones", [128, 128], BF16)
        onesA = sb("onesA", [128, 128], BF16)
        onesB = sb("onesB", [128, 128], BF16)
        vecs = sb("vecs", [128, NV], F32)
        lv = sb("lv", [128, 64], F32)
        memT = sb("memT", [128, KC, MEM], BF16)
        KCT = sb("KCT", [128, KC, MEM], BF16)
        VC = sb("VC", [128, 2, D], BF16)
        wring = sb("wring", [128, NWSLOT, WSLOT], BF16)
        small = sb("small", [128, 64], F32)
        epsc = sb("epsc", [128, 4], F32)
        sinkt = sb("sinkt", [128, 8], F32)
        CONVST = sb("convst", [128, KC, 4], BF16)
        SCANST = sb("scanst", [128, KC], F32)
        KPREV = sb("kprev", [128, 2, 2, 128], BF16)
        VPREV = sb("vprev", [128, 2, 2, 128], BF16)
        ARENA = sb("arena", [128, ARENA_BYTES // 2], BF16)
        PS = st.enter_context(nc.psum_tensor("PS", [128, 8, 512], F32))

        def av(off, shape, dt):
            n = 1
            for d_ in shape:
                n *= d_
            nb = n * (4 if dt == F32 else 2)
            assert off % 4 == 0 and off + nb <= ARENA_BYTES, (off, nb)
            a = ARENA[:, off // 2:(off + nb) // 2]
            if dt == F32:
                a = a.bitcast(F32)
            if len(shape) == 2:
                a = a.rearrange("p (a b) -> p a b", a=shape[0])
            elif len(shape) == 3:
                a = a.rearrange("p (a b c) -> p a b c", a=shape[0], b=shape[1])
            return a

        t_H = mk.tiles(NTT)
        t_HT = mk.tiles(KC, NG)
        t_const = mk.tile()
        t_vecs = mk.tile()
        t_lv = mk.tile()
        t_memT = mk.tile()
        t_KCT = mk.tiles(KC)
        t_VC = mk.tiles(2)
        t_w = mk.tiles(NWSLOT)
        t_ps = [Tile(excl=True) for _ in range(8)]
        t_small = mk.tile()
        t_small4 = mk.tiles(4)
        t_convst = mk.tiles(KC)
        t_scanst = mk.tiles(KC)
        t_kprev = mk.tiles(2)
        t_vprev = mk.tiles(2)
        ps_rr = [0]
        w_rr = [0]

        def psum(n=1):
            i = ps_rr[0] % 8
            if n == 2 and i % 2 == 1:
                i = (i + 1) % 8
            ps_rr[0] = i + n
            return list(range(i, i + n))

        def wslot():
            i = w_rr[0] % NWSLOT
            w_rr[0] += 1
            return i

        def wload(slot, pairs):
            mk.dma_multi("pool", pairs, writes=[t_w[slot]])

        def wview(slot, k, n, off=0):
            return wring[:, slot, off:off + k * n].rearrange("p (k n) -> p k n", k=k)

        def wsrc(dram2d, c0, n, r0=0, k=KC):
            return dram2d[r0:r0 + k * 128, c0:c0 + n].rearrange("(k p) n -> p k n", p=128)

        def mm(bk, lhsT, rhs, start, stop, reads, cols=None):
            out = PS[:, bk, :] if cols is None else PS[:, bk, cols[0]:cols[1]]
            mk.op("pe", lambda e: e.matmul(out, lhsT=lhsT, rhs=rhs, start=start, stop=stop),
                  reads=reads, writes=[t_ps[bk]])

        def act(out, in_, func, reads, writes, bias=None, scale=None):
            kw = {}
            if bias is not None:
                kw["bias"] = bias
            if scale is not None:
                kw["scale"] = scale
            mk.op("act", lambda e: e.activation(out=out, in_=in_, func=func, **kw), reads=reads, writes=writes)

        def tt(out, in0, in1, op, reads, writes, eng="dve"):
            mk.op(eng, lambda e: e.tensor_tensor(out=out, in0=in0, in1=in1, op=op), reads=reads, writes=writes)

        def stt(out, in0, scalar, in1, op0, op1, reads, writes):
            mk.op("dve", lambda e: e.scalar_tensor_tensor(out=out, in0=in0, scalar=scalar, in1=in1, op0=op0, op1=op1),
                  reads=reads, writes=writes)

        def ts(out, in0, s1, s2, op0, op1, reads, writes):
            if s2 is None:
                mk.op("dve", lambda e: e.tensor_scalar(out=out, in0=in0, scalar1=s1, scalar2=None, op0=op0),
                      reads=reads, writes=writes)
            else:
                mk.op("dve", lambda e: e.tensor_scalar(out=out, in0=in0, scalar1=s1, scalar2=s2, op0=op0, op1=op1),
                      reads=reads, writes=writes)

        def cp(out, in_, reads, writes, eng="dve"):
            mk.op(eng, lambda e: e.tensor_copy(out=out, in_=in_), reads=reads, writes=writes)

        def memset(ap, val, writes, eng="dve"):
            mk.op(eng, lambda e: e.memset(ap, val), writes=writes)

        mk.dma_multi("pool", [(ident[:, :], consts_d[:, 0:128]), (maskc[:, :], consts_d[:, 128:640]),
                              (maskp[:, :], consts_d[:, 640:1152]), (pmT[:, :], consts_d[:, 1152:1280])],
                     writes=[t_const])
        memset(ones[:, :], 1.0, [t_const])
        memset(onesA[:, :], 0.0, [t_const])
        memset(onesB[:, :], 0.0, [t_const])
        memset(onesA[:, 0:64], 1.0, [t_const])
        memset(onesB[:, 64:128], 1.0, [t_const])
        memset(epsc[:, 0:1], LN_EPS, [t_const])
        memset(epsc[:, 1:2], 1.0, [t_const])
        memset(epsc[:, 2:3], math.log(0.5), [t_const])

        xv = x_d.rearrange("(t p) d -> p t d", p=128)
        for t0 in range(0, NTT, 4):
            mk.dma("sp", H[:, t0:t0 + 4, :], xv[:, t0:t0 + 4, :], writes=t_H[t0:t0 + 4])

        def transpose_into(src_of, n_tiles, dst_of, t_src_of, t_dst_of, hb_off, groups=None, t_hb=None, all_act=False):
            hb = av(hb_off, [2, D], BF16)
            if t_hb is None:
                t_hb = mk.tiles(2)
            for g0 in range(0, n_tiles, 4):
                if groups is not None and g0 // 4 not in groups:
                    continue
                nq = min(4, n_tiles - g0)
                banks = []
                for q in range(nq):
                    ti = g0 + q
                    b = q % 2
                    act(hb[:, b, :], src_of(ti), AF.Copy, [t_src_of(ti)], [t_hb[b]])
                    for c in range(KC):
                        if q == 0:
                            banks.append(psum()[0])
                        mm(banks[c], hb[:, b, c * 128:(c + 1) * 128], ident[:, :], True, True, [t_hb[b], t_const],
                           cols=(q * 128, (q + 1) * 128))
                for c in range(KC):
                    bk = banks[c]
                    dst, tdst = dst_of(c, g0, nq), t_dst_of(c, g0)
                    if c % 2 == 0 or all_act:
                        act(dst, PS[:, bk, 0:nq * 128], AF.Copy, [t_ps[bk]], [tdst])
                    else:
                        cp(dst, PS[:, bk, 0:nq * 128], [t_ps[bk]], [tdst])

        def make_HT(seg, hb_off, groups=None, t_hb=None, all_act=False):
            transpose_into(lambda ti: H[:, seg * NT + ti, :], NT,
                           lambda c, g0, nq: HT[:, c, g0 * 128:(g0 + nq) * 128],
                           lambda ti: t_H[seg * NT + ti], lambda c, g0: t_HT[c][g0 // 4], hb_off, groups, t_hb, all_act)

        def ln_parts(gt, lng, lnb, t_lng, t_lnb):
            so = (gt % 4) * 16
            tsm = t_small4[gt % 4]

            def A():
                for hf in range(2):
                    mk.op("dve", lambda e, hf=hf: e.bn_stats(out=small[:, so + hf * 6:so + hf * 6 + 6],
                                                             in_=H[:, gt, hf * 512:(hf + 1) * 512]),
                          reads=[t_H[gt]], writes=[tsm])
                mk.op("dve", lambda e: e.bn_aggr(out=small[:, so + 12:so + 14], in_=small[:, so:so + 12]),
                      reads=[tsm], writes=[tsm])
                act(small[:, so + 14:so + 15], small[:, so + 13:so + 14], AF.Ln, [tsm, t_const], [tsm],
                    bias=epsc[:, 0:1], scale=1.0)
                act(small[:, so + 14:so + 15], small[:, so + 14:so + 15], AF.Exp, [tsm], [tsm], scale=-0.5)

            def B1():
                stt(small[:, so + 15:so + 16], small[:, so + 12:so + 13], -1.0, small[:, so + 14:so + 15], ALU.mult, ALU.mult,
                    [tsm], [tsm])
                act(H[:, gt, :], H[:, gt, :], AF.Identity, [tsm, t_H[gt]], [t_H[gt]],
                    bias=small[:, so + 15:so + 16], scale=small[:, so + 14:so + 15])

            def B2():
                tt(H[:, gt, :], H[:, gt, :], lng, ALU.mult, [t_lng, t_H[gt]], [t_H[gt]], eng=LN_G_ENG)
                tt(H[:, gt, :], H[:, gt, :], lnb, ALU.add, [t_lnb, t_H[gt]], [t_H[gt]])
            return A, B1, B2

        def out_proj_ln(l, seg, w_d, srcT, t_srcT, nk, ln_idx, ln_off, rebuild=True):
            mk.mark('outproj')
            lng = av(ln_off, [D], F32)
            lnb = av(ln_off + 4096, [D], F32)
            t_lng, t_lnb = mk.tile(), mk.tile()
            mk.dma("sp", lng, lnrow_d[l, ln_idx, :].partition_broadcast(128), writes=[t_lng])
            mk.dma("sp", lnb, lnrow_d[l, ln_idx + 1, :].partition_broadcast(128), writes=[t_lnb])
            hbt = mk.tiles(2)
            pend = None
            stages = []

            def ln_issue(gt):
                A, B1, B2 = ln_parts(gt, lng, lnb, t_lng, t_lnb)
                A()
                stages.append((B1, B2))
                n = len(stages)
                if n >= 2:
                    stages[n - 2][0]()
                if n >= 3:
                    stages[n - 3][1]()

            def ln_flush():
                n = len(stages)
                if n >= 1:
                    stages[n - 1][0]()
                if n >= 2:
                    stages[n - 2][1]()
                if n >= 1:
                    stages[n - 1][1]()

            for g in range(NG):
                for hf in range(2):
                    sl = []
                    for k0 in range(0, nk, 8):
                        kk = min(8, nk - k0)
                        s_ = wslot()
                        wload(s_, [(wview(s_, kk, 512), wsrc(w_d, hf * 512, 512, r0=k0 * 128, k=kk))])
                        sl.append((s_, k0, kk))
                    for q in range(4):
                        tk = g * 4 + q
                        gt = seg * NT + tk
                        bk = psum()[0]
                        for (s_, k0, kk) in sl:
                            for k in range(kk):
                                kg = k0 + k
                                mm(bk, srcT[:, kg, tk * 128:(tk + 1) * 128], wview(s_, kk, 512)[:, k, :],
                                   kg == 0, kg == nk - 1, [t_srcT[kg][g], t_w[s_]])
                        stt(H[:, gt, hf * 512:(hf + 1) * 512], H[:, gt, hf * 512:(hf + 1) * 512], ALPHA, PS[:, bk, :],
                            ALU.mult, ALU.add, [t_ps[bk], t_H[gt]], [t_H[gt]])
                        if hf == 1:
                            ln_issue(gt)
                if rebuild:
                    if pend is not None:
                        pend()
                    pend = (lambda g=g: make_HT(seg, ln_off + 8192, groups=[g], t_hb=hbt, all_act=True))
            ln_flush()
            if pend is not None:
                pend()

        def ffn(l, seg):
            mk.mark('ffn')
            mk.set_fence()
            actT = av(0, [NFC, TS], BF16)
            sg = av(45056, [2, 512], F32)
            t_act = mk.tiles(NFC, NG)
            t_sg = mk.tiles(2)
            r = 0
            for fb in range(6):
                ncol = 512 if fb < 5 else 256
                ncc = ncol // 128
                s_g = wslot()
                wload(s_g, [(wview(s_g, 8, ncol), wsrc(wi_d[l], fb * 512, ncol))])
                s_u = wslot()
                wload(s_u, [(wview(s_u, 8, ncol), wsrc(wi_d[l], DFF + fb * 512, ncol))])
                for cc in range(ncc):
                    c = fb * 4 + cc
                    for g in range(NG):
                        bg = psum()[0]
                        bu = psum()[0]
                        for k in range(KC):
                            mm(bg, wview(s_g, 8, ncol)[:, k, cc * 128:(cc + 1) * 128], HT[:, k, g * 512:(g + 1) * 512],
                               k == 0, k == KC - 1, [t_w[s_g], t_HT[k][g]])
                        for k in range(KC):
                            mm(bu, wview(s_u, 8, ncol)[:, k, cc * 128:(cc + 1) * 128], HT[:, k, g * 512:(g + 1) * 512],
                               k == 0, k == KC - 1, [t_w[s_u], t_HT[k][g]])
                        b = r % 2
                        r += 1
                        act(sg[:, b, :], PS[:, bg, :], AF.Silu, [t_ps[bg]], [t_sg[b]])
                        tt(actT[:, c, g * 512:(g + 1) * 512], PS[:, bu, :], sg[:, b, :], ALU.mult,
                           [t_ps[bu], t_sg[b]], [t_act[c][g]])
            nxt = l * NSEG + seg + 1
            if nxt < L * NSEG:
                make_HT(nxt % NSEG, 61440)
            out_proj_ln(l, seg, wo_d[l], actT, t_act, NFC, 4, 49152, rebuild=False)
            mk.set_fence()

        def prep_mem():
            mk.set_fence()
            mf = av(0, [2, D], F32)
            t_mf = mk.tiles(2)
            mk.dma("sp", mf, mem_d.rearrange("(t p) d -> p t d", p=128), writes=t_mf)
            transpose_into(lambda ti: mf[:, ti, :], 2, lambda c, g0, nq: memT[:, c, g0 * 128:(g0 + nq) * 128],
                           lambda ti: t_mf[ti], lambda c, g0: t_memT, 8192)
            mk.set_fence()

        def cross_kv(l):
            for ob in range(2):
                s_ = wslot()
                wload(s_, [(wview(s_, 8, 512), wsrc(ckv_d[l], ob * 512, 512))])
                for cc in range(4):
                    c = ob * 4 + cc
                    bk = psum()[0]
                    for k in range(KC):
                        mm(bk, wview(s_, 8, 512)[:, k, cc * 128:(cc + 1) * 128], memT[:, k, :], k == 0, k == KC - 1,
                           [t_w[s_], t_memT], cols=(0, MEM))
                    act(KCT[:, c, :], PS[:, bk, 0:MEM], AF.Copy, [t_ps[bk]], [t_KCT[c]])
            for hf in range(2):
                s_ = wslot()
                wload(s_, [(wview(s_, 8, 512), wsrc(ckv_d[l], D + hf * 512, 512))])
                for mt in range(2):
                    bk = psum()[0]
                    for k in range(KC):
                        mm(bk, memT[:, k, mt * 128:(mt + 1) * 128], wview(s_, 8, 512)[:, k, :], k == 0, k == KC - 1,
                           [t_w[s_], t_memT])
                    cp(VC[:, mt, hf * 512:(hf + 1) * 512], PS[:, bk, :], [t_ps[bk]], [t_VC[mt]])

        def cross(l, seg):
            mk.mark('cross')
            mk.set_fence()
            QCT = av(0, [KC, TS], BF16)
            OT = av(16384, [KC, TS], BF16)
            EE = av(32768, [4, 512], BF16)
            LG = av(36864, [2, 512], F32)
            t_Q = mk.tiles(KC, NG)
            t_O = mk.tiles(KC, NG)
            t_E = mk.tiles(4)
            t_LG = mk.tiles(2)
            for ob in range(2):
                s_ = wslot()
                wload(s_, [(wview(s_, 8, 512), wsrc(cq_d[l], ob * 512, 512))])
                for cc in range(4):
                    c = ob * 4 + cc
                    for g in range(NG):
                        bk = psum()[0]
                        for k in range(KC):
                            mm(bk, wview(s_, 8, 512)[:, k, cc * 128:(cc + 1) * 128], HT[:, k, g * 512:(g + 1) * 512],
                               k == 0, k == KC - 1, [t_w[s_], t_HT[k][g]])
                        if (cc + g) % 2 == 0:
                            act(QCT[:, c, g * 512:(g + 1) * 512], PS[:, bk, :], AF.Copy, [t_ps[bk]], [t_Q[c][g]])
                        else:
                            cp(QCT[:, c, g * 512:(g + 1) * 512], PS[:, bk, :], [t_ps[bk]], [t_Q[c][g]])
            r = 0
            for hh in range(4):
                for g in range(NG):
                    eb = (r % 2) * 2
                    r += 1
                    for mt in range(2):
                        bk = psum()[0]
                        for kc in range(2):
                            c = 2 * hh + kc
                            mm(bk, KCT[:, c, mt * 128:(mt + 1) * 128], QCT[:, c, g * 512:(g + 1) * 512], kc == 0, kc == 1,
                               [t_KCT[c], t_Q[c][g]])
                        act(EE[:, eb + mt, :], PS[:, bk, :], AF.Exp, [t_ps[bk]], [t_E[eb + mt]], scale=1.0 / 16.0)
                    bd = psum()[0]
                    for mt in range(2):
                        mm(bd, ones[:, :], EE[:, eb + mt, :], mt == 0, mt == 1, [t_const, t_E[eb + mt]])
                    bn = []
                    for kc in range(2):
                        c = 2 * hh + kc
                        bk = psum()[0]
                        bn.append(bk)
                        for mt in range(2):
                            mm(bk, VC[:, mt, c * 128:(c + 1) * 128], EE[:, eb + mt, :], mt == 0, mt == 1,
                               [t_VC[mt], t_E[eb + mt]])
                    lb = (r % 2)
                    act(LG[:, lb, :], PS[:, bd, :], AF.Ln, [t_ps[bd]], [t_LG[lb]])
                    act(LG[:, lb, :], LG[:, lb, :], AF.Exp, [t_LG[lb]], [t_LG[lb]], scale=-1.0)
                    for kc in range(2):
                        c = 2 * hh + kc
                        tt(OT[:, c, g * 512:(g + 1) * 512], PS[:, bn[kc], :], LG[:, lb, :], ALU.mult,
                           [t_ps[bn[kc]], t_LG[lb]], [t_O[c][g]])
            mk.set_fence()
            out_proj_ln(l, seg, co_d[l], OT, t_O, KC, 2, 0, rebuild=True)
            mk.set_fence()


        class Bump:
            def __init__(self, base):
                self.o = base

            def __call__(self, n):
                n = (n + 31) // 32 * 32
                r = self.o
                self.o += n
                assert self.o <= ARENA_BYTES, self.o
                return r

        def layer_prep(l):
            mk.dma("sp", sinkt[:, :], sinkcol_d[l], writes=[t_small])
            act(lv[:, 32:40], vecs[:, V_LAM:V_LAM + 8], AF.Exp, [t_vecs], [t_lv], scale=-1.0)
            act(lv[:, 32:40], lv[:, 32:40], AF.Ln, [t_lv, t_const], [t_lv], bias=epsc[:, 1:2], scale=1.0)
            ts(lv[:, 0:8], lv[:, 32:40], -4.0, None, ALU.mult, None, [t_lv], [t_lv])
            ts(lv[:, 8:16], vecs[:, V_BRG:V_BRG + 8], 0.5, None, ALU.mult, None, [t_vecs, t_lv], [t_lv])
            ts(lv[:, 16:24], vecs[:, V_BIG:V_BIG + 8], 0.5, None, ALU.mult, None, [t_vecs, t_lv], [t_lv])
            act(lv[:, 24:32], sinkt[:, :], AF.Exp, [t_small, t_lv], [t_lv])
            for c in range(KC):
                memset(CONVST[:, c, :], 0.0, [t_convst[c]])
                memset(SCANST[:, c:c + 1], 0.0, [t_scanst[c]])
            for gk in range(2):
                memset(KPREV[:, gk, :, :], 0.0, [t_kprev[gk]])
                memset(VPREV[:, gk, :, :], 0.0, [t_vprev[gk]])

        def rnn_phase(l, seg, YR, t_YR):
            mk.mark('rnn')
            o = Bump(32768)
            o2 = Bump(16384)
            DW = [av(o(2048), [2, 4, 128], BF16) for _ in range(2)]
            XRb = av(o(2 * (TS + 4) * 2), [2, TS + 4], BF16)
            XCb = av(o(2048), [2, 512], BF16)

            def two(a, b):
                return [av(a(4096), [2, 512], F32), av(b(4096), [2, 512], F32)]
            XC32 = two(o, o2)
            GL = two(o, o2)
            Rt = two(o, o2)
            It = two(o, o2)
            Aa = two(o, o)
            Mq = two(o, o)
            assert o2.o <= 32768
            t_DW = mk.tiles(2, 2)
            t_XR = mk.tiles(2, NG)
            t_XRh = mk.tiles(2)
            t_XCb = mk.tiles(2)
            t_XC32, t_GL, t_Rt, t_It, t_Aa, t_Mq = (mk.tiles(2, 2) for _ in range(6))
            its = [(n, g) for n in range(4) for g in range(NG)]
            blk = {}

            def front(i):
                n, g = its[i]
                p = i % 2
                dw, tdw = DW[n % 2], t_DW[n % 2]
                if g == 0:
                    sA = wslot()
                    wA = wview(sA, 8, 512)
                    wload(sA, [(wA[:, :, 0:256], wsrc(w_in_d[l], n * 256, 256)),
                               (wA[:, :, 256:512], wsrc(w_in_d[l], 1024 + n * 256, 256))])
                    sB = wslot()
                    rgv = wview(sB, 2, 256)
                    igv = wview(sB, 2, 256, off=512)
                    wload(sB, [(rgv, w_rg_d[l, n].rearrange("(k p) n -> p k n", p=128)),
                               (igv, w_ig_d[l, n].rearrange("(k p) n -> p k n", p=128))])
                    blk[n] = (sA, wA, sB, rgv, igv)
                    for cc in range(2):
                        c = 2 * n + cc
                        for tap in range(4):
                            ts(dw[:, cc, tap, :], ident[:, :], vecs[:, V_CONVW + tap * 8 + c:V_CONVW + tap * 8 + c + 1], None,
                               ALU.mult, None, [t_const, t_vecs], [tdw[cc]])
                        cp(XRb[:, cc, 0:3], CONVST[:, c, 0:3], [t_convst[c]], [t_XRh[cc]])
                sA, wA, sB, rgv, igv = blk[n]
                for cc in range(2):
                    bx = psum()[0]
                    for k in range(KC):
                        mm(bx, wA[:, k, cc * 128:(cc + 1) * 128], HT[:, k, g * 512:(g + 1) * 512], k == 0, k == KC - 1,
                           [t_w[sA], t_HT[k][g]])
                    cp(XRb[:, cc, 3 + g * 512:3 + (g + 1) * 512], PS[:, bx, :], [t_ps[bx]], [t_XR[cc][g]])
                    bg = psum()[0]
                    for k in range(KC):
                        mm(bg, wA[:, k, 256 + cc * 128:256 + (cc + 1) * 128], HT[:, k, g * 512:(g + 1) * 512], k == 0,
                           k == KC - 1, [t_w[sA], t_HT[k][g]])
                    act(GL[p][:, cc, :], PS[:, bg, :], AF.Gelu_apprx_tanh, [t_ps[bg]], [t_GL[p][cc]])
                for cc in range(2):
                    c = 2 * n + cc
                    bc = psum()[0]
                    rd = [tdw[cc], t_XR[cc][g], t_XR[cc][g - 1] if g > 0 else t_XRh[cc]]
                    for tap in range(4):
                        mm(bc, dw[:, cc, tap, :], XRb[:, cc, g * 512 + tap:g * 512 + tap + 512], tap == 0, tap == 3, rd)
                    act(XCb[:, cc, :], PS[:, bc, :], AF.Identity, [t_ps[bc], t_vecs], [t_XCb[cc]],
                        bias=vecs[:, V_CONVB + c:V_CONVB + c + 1], scale=1.0)
                    ts(XC32[p][:, cc, :], PS[:, bc, :], vecs[:, V_CONVB + c:V_CONVB + c + 1], None, ALU.add, None,
                       [t_ps[bc], t_vecs], [t_XC32[p][cc]])
                brs, bis = [], []
                for co in range(2):
                    br = psum()[0]
                    for kc in range(2):
                        mm(br, rgv[:, kc, co * 128:(co + 1) * 128], XCb[:, kc, :], kc == 0, kc == 1, [t_w[sB], t_XCb[kc]])
                    bi = psum()[0]
                    for kc in range(2):
                        mm(bi, igv[:, kc, co * 128:(co + 1) * 128], XCb[:, kc, :], kc == 0, kc == 1, [t_w[sB], t_XCb[kc]])
                    brs.append(br)
                    bis.append(bi)
                for co in range(2):
                    c = 2 * n + co
                    act(Rt[p][:, co, :], PS[:, brs[co], :], AF.Tanh, [t_ps[brs[co]], t_lv], [t_Rt[p][co]],
                        bias=lv[:, 8 + c:9 + c], scale=0.5)
                    act(It[p][:, co, :], PS[:, bis[co], :], AF.Tanh, [t_ps[bis[co]], t_lv], [t_It[p][co]],
                        bias=lv[:, 16 + c:17 + c], scale=0.5)
                for co in range(2):
                    c = 2 * n + co
                    act(Aa[p][:, co, :], Rt[p][:, co, :], AF.Exp, [t_Rt[p][co], t_lv], [t_Aa[p][co]], bias=lv[:, c:c + 1],
                        scale=lv[:, c:c + 1])
                for co in range(2):
                    act(Mq[p][:, co, :], Aa[p][:, co, :], AF.Square, [t_Aa[p][co]], [t_Mq[p][co]])
                for co in range(2):
                    act(Mq[p][:, co, :], Mq[p][:, co, :], AF.Ln, [t_Mq[p][co], t_const], [t_Mq[p][co]], bias=epsc[:, 1:2], scale=-1.0)
                for co in range(2):
                    act(Mq[p][:, co, :], Mq[p][:, co, :], AF.Exp, [t_Mq[p][co], t_const], [t_Mq[p][co]], bias=epsc[:, 2:3], scale=0.5)
                if g == NG - 1:
                    for cc in range(2):
                        c = 2 * n + cc
                        cp(CONVST[:, c, 0:3], XRb[:, cc, TS:TS + 3], [t_XR[cc][NG - 1]], [t_convst[c]])

            def tail(i):
                n, g = its[i]
                p = i % 2
                for co in range(2):
                    c = 2 * n + co
                    stt(It[p][:, co, :], It[p][:, co, :], 1.0, XC32[p][:, co, :], ALU.add, ALU.mult,
                        [t_It[p][co], t_XC32[p][co]], [t_It[p][co]])
                    tt(It[p][:, co, :], It[p][:, co, :], Mq[p][:, co, :], ALU.mult, [t_It[p][co], t_Mq[p][co]], [t_It[p][co]])
                    mk.op("dve", lambda e, co=co, c=c: e.tensor_tensor_scan(
                        out=XC32[p][:, co, :], data0=Aa[p][:, co, :], data1=It[p][:, co, :], initial=SCANST[:, c:c + 1],
                        op0=ALU.mult, op1=ALU.add), reads=[t_Aa[p][co], t_It[p][co], t_scanst[c]], writes=[t_XC32[p][co]])
                    cp(SCANST[:, c:c + 1], XC32[p][:, co, 511:512], [t_XC32[p][co]], [t_scanst[c]])
                    tt(YR[:, c, g * 512:(g + 1) * 512], XC32[p][:, co, :], GL[p][:, co, :], ALU.mult,
                       [t_XC32[p][co], t_GL[p][co]], [t_YR[c][g]])

            front(0)
            for i in range(1, len(its)):
                front(i)
                tail(i - 1)
            tail(len(its) - 1)

        def att_phase(l, seg, YA, t_YA):
            mk.mark('att')
            o = Bump(32768)
            ROPE = av(o(8192), [2, TS], F32)
            KA = av(o(2 * (128 + TS)), [128 + TS], BF16)
            KB = av(o(2 * (128 + TS)), [128 + TS], BF16)
            VA = av(o(2 * (1 + NT) * 256), [1 + NT, 2, 128], BF16)
            VB = av(o(2 * (1 + NT) * 256), [1 + NT, 2, 128], BF16)
            QT = av(o(8 * TS), [NT, 4, 128], BF16)
            QRAW = av(o(1024), [512], BF16)
            T1 = av(o(2048), [512], F32)
            T2 = av(o(2048), [512], F32)
            EE2 = [av(o(4096), [2, 2, 512], BF16), av(o(4096), [2, 2, 512], BF16)]
            LG = av(o(2048), [512], F32)
            t_rope, t_KA, t_KB, t_VA, t_VB, t_QRAW, t_T1, t_T2, t_LG = (mk.tile() for _ in range(9))
            t_QT = mk.tiles(NT)
            t_E2 = mk.tiles(2, 2)
            mk.dma("sp", ROPE, rope_d[:, :, seg * TS:(seg + 1) * TS].rearrange("a p s -> p a s"), writes=[t_rope])

            rp = [0]
            T12 = [T1, T2]
            t_T12 = [t_T1, t_T2]

            def roped(bk, dsts, g, rd_extra):
                p = rp[0] % 2
                rp[0] += 1
                Tp, t_Tp = T12[p], t_T12[p]
                act(QRAW, PS[:, bk, :], AF.Copy, [t_ps[bk]], [t_QRAW])
                bs = psum()[0]
                mm(bs, pmT[:, :], QRAW, True, True, [t_const, t_QRAW])
                tt(Tp, PS[:, bk, :], ROPE[:, 0, g * 512:(g + 1) * 512], ALU.mult, [t_ps[bk], t_rope], [t_Tp])
                tt(PS[:, bs, :], PS[:, bs, :], ROPE[:, 1, g * 512:(g + 1) * 512], ALU.mult, [t_ps[bs], t_rope], [t_ps[bs]])
                for (out_ap, p0, p1, shp, wr) in dsts:
                    i0, i1 = PS[p0:p1, bs, :], Tp[p0:p1, :]
                    if shp is not None:
                        i0 = i0.rearrange("p (a b) -> p a b", a=shp)
                        i1 = i1.rearrange("p (a b) -> p a b", a=shp)
                    tt(out_ap, i0, i1, ALU.add, [t_ps[bs], t_Tp], wr)

            memset(VA[:, :, :, :], 0.0, [t_VA])
            memset(VB[:, :, :, :], 0.0, [t_VB])
            for gk in range(2):
                memset(KA[64:128, :], 0.0, [t_KA])
                memset(KB[0:64, :], 0.0, [t_KB])
                cp(KA[0:64, 0:128], KPREV[0:64, gk, 0, :], [t_kprev[gk]], [t_KA])
                cp(KB[64:128, 0:128], KPREV[64:128, gk, 1, :], [t_kprev[gk]], [t_KB])
                sK = wslot()
                wK = wview(sK, 8, 128)
                wV = wview(sK, 8, 128, off=1024)
                pairs = [(wK, wsrc(wk2_d[l], gk * 128, 128))]
                if gk == 0:
                    pairs.append((wV, wsrc(w_in_d[l], 3200, 128)))
                wload(sK, pairs)
                for g in range(NG):
                    bk = psum()[0]
                    for k in range(KC):
                        mm(bk, wK[:, k, :], HT[:, k, g * 512:(g + 1) * 512], k == 0, k == KC - 1, [t_w[sK], t_HT[k][g]])
                    c0 = 128 + g * 512
                    roped(bk, [(KA[0:64, c0:c0 + 512], 0, 64, None, [t_KA]), (KB[64:128, c0:c0 + 512], 64, 128, None, [t_KB])], g, None)
                if _KCUT < 12:
                    continue
                if gk == 0:
                    for i in range(2):
                        cp(VA[:, 0, i, :], VPREV[:, i, 0, :], [t_vprev[i]], [t_VA])
                        cp(VB[:, 0, i, :], VPREV[:, i, 1, :], [t_vprev[i]], [t_VB])
                    for tk in range(NT):
                        bv = psum()[0]
                        for k in range(KC):
                            mm(bv, HT[:, k, tk * 128:(tk + 1) * 128], wV[:, k, :], k == 0, k == KC - 1,
                               [t_w[sK], t_HT[k][tk // 4]], cols=(0, 128))
                        src = PS[:, bv, 0:128].rearrange("p (g d) -> p g d", g=2)
                        act(VA[:, 1 + tk, :, 0:64], src, AF.Copy, [t_ps[bv]], [t_VA])
                        cp(VB[:, 1 + tk, :, 64:128], src, [t_ps[bv]], [t_VB])
                if _KCUT < 13:
                    continue
                sQ = wslot()
                wQ = wview(sQ, 8, 512)
                wload(sQ, [(wQ, wsrc(w_in_d[l], 2048 + gk * 512, 512))])
                for cc in range(4):
                    for g in range(NG):
                        bq = psum()[0]
                        for k in range(KC):
                            mm(bq, wQ[:, k, cc * 128:(cc + 1) * 128], HT[:, k, g * 512:(g + 1) * 512], k == 0, k == KC - 1,
                               [t_w[sQ], t_HT[k][g]])
                        roped(bq, [(QT[:, g * 4:(g + 1) * 4, cc, :], 0, 128, 4, t_QT[g * 4:(g + 1) * 4])], g, None)
                def kblocks(qb):
                    gb = seg * NT + qb
                    kb = []
                    if gb > 0:
                        kb.append((0, qb * 128, qb, maskp))
                    kb.append((1, (qb + 1) * 128, qb + 1, maskc))
                    return kb

                def emit_scores(qb):
                    qrhs = QT[:, qb, :, :].rearrange("p a b -> p (a b)")
                    EE, t_E = EE2[qb % 2], t_E2[qb % 2]
                    for (slot, kcol, vt, mask) in kblocks(qb):
                        sc = [2 * slot, 2 * slot + 1]
                        for hf in range(2):
                            mm(sc[hf], ident[:, :], mask[:, :], True, False, [t_const])
                            kk_ = KA if hf == 0 else KB
                            mm(sc[hf], kk_[:, kcol:kcol + 128], qrhs, False, True, [t_KA if hf == 0 else t_KB, t_QT[qb]])
                        act(EE[:, slot, :, :], PS[:, sc[0]:sc[0] + 2, :], AF.Exp, [t_ps[sc[0]], t_ps[sc[1]]], [t_E[slot]], scale=0.125)

                def emit_nd(qb):
                    EE, t_E = EE2[qb % 2], t_E2[qb % 2]
                    kb = kblocks(qb)
                    bnum, bden = (4, 5) if qb % 2 == 0 else (6, 7)
                    nmm = 2 * len(kb)
                    i = 0
                    for (slot, kcol, vt, mask) in kb:
                        for hf in range(2):
                            vv = VA if hf == 0 else VB
                            mm(bnum, vv[:, vt, gk, :], EE[:, slot, hf, :], i == 0, i == nmm - 1,
                               [t_VA if hf == 0 else t_VB, t_E[slot]])
                            i += 1
                    i = 0
                    for (slot, kcol, vt, mask) in kb:
                        for hf in range(2):
                            mm(bden, (onesA if hf == 0 else onesB)[:, :], EE[:, slot, hf, :], i == 0, i == nmm - 1,
                               [t_const, t_E[slot]])
                            i += 1
                    for cc in range(4):
                        j = 24 + gk * 4 + cc
                        act(LG[:, cc * 128:(cc + 1) * 128], PS[:, bden, cc * 128:(cc + 1) * 128], AF.Ln, [t_ps[bden], t_lv], [t_LG],
                            bias=lv[:, j:j + 1], scale=1.0)
                    act(LG, LG, AF.Exp, [t_LG], [t_LG], scale=-1.0)
                    tt(YA[:, gk * 4:(gk + 1) * 4, qb * 128:(qb + 1) * 128], PS[:, bnum, :].rearrange("p (a b) -> p a b", a=4),
                       LG.rearrange("p (a b) -> p a b", a=4), ALU.mult, [t_ps[bnum], t_LG],
                       [t_YA[gk * 4 + cc][qb // 4] for cc in range(4)])

                emit_scores(0)
                for qb in range(1, NT):
                    emit_scores(qb)
                    emit_nd(qb - 1)
                emit_nd(NT - 1)
                cp(KPREV[0:64, gk, 0, :], KA[0:64, TS:TS + 128], [t_KA], [t_kprev[gk]])
                cp(KPREV[64:128, gk, 1, :], KB[64:128, TS:TS + 128], [t_KB], [t_kprev[gk]])
            for i in range(2):
                cp(VPREV[:, i, 0, :], VA[:, NT, i, :], [t_VA], [t_vprev[i]])
                cp(VPREV[:, i, 1, :], VB[:, NT, i, :], [t_VB], [t_vprev[i]])

        def merge_phase(l, seg, YR, t_YR, YA, t_YA, M, t_M):
            mk.mark('merge')
            TM = av(49152, [4, TS], F32)
            SG = av(65536, [2, 512], F32)
            V2 = av(69632, [512], F32)
            t_TM = mk.tiles(4, NG)
            t_SG = mk.tiles(2)
            t_V2 = mk.tile()
            r = 0
            for ob in range(2):
                for br in range(2):
                    Ysrc, t_Ysrc = (YR, t_YR) if br == 0 else (YA, t_YA)
                    w1 = w_brr_d[l] if br == 0 else w_bra_d[l]
                    gcol = (3328 if br == 0 else 4352) + ob * 512
                    s1 = wslot()
                    wload(s1, [(wview(s1, 8, 512), wsrc(w1, ob * 512, 512))])
                    s2 = wslot()
                    wload(s2, [(wview(s2, 8, 512), wsrc(w_in_d[l], gcol, 512))])
                    for cc in range(4):
                        c = ob * 4 + cc
                        for g in range(NG):
                            bz = psum()[0]
                            for k in range(KC):
                                mm(bz, wview(s1, 8, 512)[:, k, cc * 128:(cc + 1) * 128], Ysrc[:, k, g * 512:(g + 1) * 512],
                                   k == 0, k == KC - 1, [t_w[s1], t_Ysrc[k][g]])
                            bg = psum()[0]
                            for k in range(KC):
                                mm(bg, wview(s2, 8, 512)[:, k, cc * 128:(cc + 1) * 128], HT[:, k, g * 512:(g + 1) * 512],
                                   k == 0, k == KC - 1, [t_w[s2], t_HT[k][g]])
                            b = r % 2
                            r += 1
                            act(SG[:, b, :], PS[:, bg, :], AF.Sigmoid, [t_ps[bg]], [t_SG[b]])
                            if br == 0:
                                tt(TM[:, cc, g * 512:(g + 1) * 512], PS[:, bz, :], SG[:, b, :], ALU.mult,
                                   [t_ps[bz], t_SG[b]], [t_TM[cc][g]])
                            else:
                                tt(V2, PS[:, bz, :], SG[:, b, :], ALU.mult, [t_ps[bz], t_SG[b]], [t_V2])
                                tt(M[:, c, g * 512:(g + 1) * 512], V2, TM[:, cc, g * 512:(g + 1) * 512], ALU.add,
                                   [t_V2, t_TM[cc][g]], [t_M[c][g]])

        def mixer(l, seg):
            mk.set_fence()
            YR = av(0, [KC, TS], BF16)
            YA = av(16384, [KC, TS], BF16)
            t_YR = mk.tiles(KC, NG)
            t_YA = mk.tiles(KC, NG)
            if mix_stage & 1:
                rnn_phase(l, seg, YR, t_YR)
            mk.set_fence()
            if mix_stage & 2:
                att_phase(l, seg, YA, t_YA)
            mk.set_fence()
            if not (mix_stage & 4):
                return
            M = av(32768, [KC, TS], BF16)
            t_M = mk.tiles(KC, NG)
            merge_phase(l, seg, YR, t_YR, YA, t_YA, M, t_M)
            mk.set_fence()
            out_proj_ln(l, seg, w_out_d[l], M, t_M, KC, 0, 0, rebuild=True)
            mk.set_fence()

        if "cross" in subs:
            prep_mem()
        for l in range(L):
            mk.dma("sp", vecs[:, :], vecs_d[l], writes=[t_vecs])
            if "cross" in subs:
                cross_kv(l)
            if "mixer" in subs:
                layer_prep(l)
            for seg in range(NSEG):
                mk.set_fence()
                if (l == 0 and seg == 0) or "ffn" not in subs:
                    make_HT(seg, 0)
                    mk.set_fence()
                if "mixer" in subs:
                    mixer(l, seg)
                if "cross" in subs:
                    cross(l, seg)
                if "ffn" in subs:
                    ffn(l, seg)

        yv = y_d.rearrange("(t p) d -> p t d", p=128)
        t_out = mk.tile()
        for t0 in range(0, NTT, 4):
            mk.dma("sp", yv[:, t0:t0 + 4, :], H[:, t0:t0 + 4, :], reads=t_H[t0:t0 + 4], writes=[t_out])
        sp = mk.E["sp"]
        for i in range(NDMA):
            if mk.dtot[i] > 0 and sp.waited.get(("d", i), 0) < mk.dtot[i]:
                sp.h.wait_ge(mk.dsem[i], mk.dtot[i])
    return nc, mk


def prep_inputs(inputs, S, L):
    f32 = np.float32
    g = {k: np.asarray(v, dtype=f32) for k, v in inputs.items() if k not in ("x", "mem")}
    w_in = g["w_in"][:L]
    kcols = w_in[:, :, 3072:3200]
    wk2 = np.concatenate([kcols[:, :, 0:64], kcols[:, :, 0:64], kcols[:, :, 64:128], kcols[:, :, 64:128]], axis=2)
    vecs = np.zeros((L, 128, NV), f32)
    for l in range(L):
        for tap in range(4):
            vecs[l, :, V_CONVW + tap * 8:V_CONVW + tap * 8 + 8] = _fm(g["conv_w"][l, tap])
        vecs[l, :, V_CONVB:V_CONVB + 8] = _fm(g["conv_b"][l])
        vecs[l, :, V_BRG:V_BRG + 8] = _fm(g["b_rg"][l])
        vecs[l, :, V_BIG:V_BIG + 8] = _fm(g["b_ig"][l])
        vecs[l, :, V_LAM:V_LAM + 8] = _fm(g["lru_lambda"][l])
    lnrow = np.stack([g["ln1_g"][:L], g["ln1_b"][:L], g["ln2_g"][:L], g["ln2_b"][:L], g["ln3_g"][:L], g["ln3_b"][:L]], axis=1)
    sinkcol = np.zeros((L, 128, 8), f32)
    for l in range(L):
        for j in range(8):
            sinkcol[l, 0:64, j] = g["sinks"][l, 2 * j]
            sinkcol[l, 64:128, j] = g["sinks"][l, 2 * j + 1]
    consts, rope = _consts(S)
    shared = {
        "w_in": np.ascontiguousarray(w_in), "wk2": np.ascontiguousarray(wk2),
        "w_rg": g["w_rg"][:L], "w_ig": g["w_ig"][:L], "w_br_rnn": g["w_br_rnn"][:L], "w_br_attn": g["w_br_attn"][:L],
        "w_out": g["w_out"][:L], "cq_w": g["cq_w"][:L], "ckv_w": g["ckv_w"][:L], "co_w": g["co_w"][:L],
        "ffn_wi": g["ffn_wi"][:L], "ffn_wo": g["ffn_wo"][:L], "vecs": vecs, "lnrow": np.ascontiguousarray(lnrow),
        "sinkcol": sinkcol, "consts": consts, "rope": rope,
    }
    return shared


_CACHE = {}


def kernel(**inputs):
    x = np.asarray(inputs["x"], dtype=np.float32)
    mem = np.asarray(inputs["mem"], dtype=np.float32)
    B, S, _ = x.shape
    L = inputs["w_in"].shape[0]
    key = (S, L)
    if key not in _CACHE:
        _CACHE[key] = build_program(S=S, DEPTH=L)[0]
    nc = _CACHE[key]
    shared = prep_inputs(inputs, S, L)
    in_maps = []
    for b in range(B):
        m = dict(shared)
        m["x"] = np.ascontiguousarray(x[b])
        m["mem"] = np.ascontiguousarray(mem[b])
        in_maps.append(m)
    res = run_bass_kernel_spmd(nc, in_maps, core_ids=list(range(B)))
    return np.stack([r["y"] for r in res.results], axis=0).astype(np.float32)
```

```python
import math
_KCUT = 99
LN_G_ENG = "dve"
from contextlib import ExitStack

import numpy as np
import concourse.bass as bass
import concourse.mybir as mybir
from concourse.bass_utils import run_bass_kernel_spmd

F32 = mybir.dt.float32
BF16 = mybir.dt.bfloat16
AF = mybir.ActivationFunctionType
ALU = mybir.AluOpType

D = 1024
KC = 8
MEM = 256
DFF = 2816
NFC = 22
IN_COLS = 5376
ALPHA = 8.0 ** 0.25
LN_EPS = 1e-5
ROPE_THETA = 500000.0
EPOCH = 16000
NEPOCH = 8
NDMA = 24
WSLOT = 4096
NWSLOT = 4
NV = 112

V_CONVW = 0
V_CONVB = 32
V_BRG = 40
V_BIG = 48
V_LAM = 56


class Tile:
    __slots__ = ("w", "rs", "excl")

    def __init__(self, fence=None, excl=False):
        self.w = None
        self.rs = dict(fence) if fence else {}
        self.excl = excl


class Eng:
    def __init__(self, name, h, key, sems, self_sync):
        self.name = name
        self.h = h
        self.key = key
        self.sems = sems
        self.seq = 0
        self.waited = {}
        self.self_sync = self_sync


class MK:
    def __init__(self, nc, st, self_sync=True):
        self.nc = nc
        self.E = {}
        for name, h, ss in (("pe", nc.tensor, False), ("act", nc.scalar, self_sync),
                            ("dve", nc.vector, self_sync), ("pool", nc.gpsimd, self_sync),
                            ("sp", nc.sync, False)):
            sems = [st.enter_context(nc.semaphore(f"s_{name}{i}")) for i in range(NEPOCH if name != "sp" else 1)]
            self.E[name] = Eng(name, h, ("e", name), sems, ss)
        self.dsem = [st.enter_context(nc.semaphore(f"s_dma{i}")) for i in range(NDMA)]
        self.dtot = [0] * NDMA
        self.drr = {"pool": 0, "sp": 0}
        self.dbase = {"pool": (0, NDMA // 2), "sp": (NDMA // 2, NDMA - NDMA // 2)}
        self.fence = {}
        self.nops = 0
        self.marks = []

    def mark(self, name):
        self.marks.append((name, self.E['pe'].seq))

    def tile(self):
        return Tile(self.fence)

    def tiles(self, *shape):
        if len(shape) == 1:
            return [self.tile() for _ in range(shape[0])]
        return [self.tiles(*shape[1:]) for _ in range(shape[0])]

    def set_fence(self):
        f = {}
        for e in self.E.values():
            if e.seq > 0:
                f[e.key] = e.seq
        self.fence = f

    def _sem(self, key, val):
        if key[0] == "e":
            e = self.E[key[1]]
            ep = (val - 1) // EPOCH
            return e.sems[ep], val - ep * EPOCH
        return self.dsem[key[1]], val

    def _waits(self, E, reads, writes, extra=None):
        need = {}
        for t in reads:
            if t.w is not None:
                k, v = t.w
                if need.get(k, 0) < v:
                    need[k] = v
            if t.excl:
                for k, v in t.rs.items():
                    if k != E.key and need.get(k, 0) < v:
                        need[k] = v
        for t in writes:
            if t.w is not None:
                k, v = t.w
                if need.get(k, 0) < v:
                    need[k] = v
            for k, v in t.rs.items():
                if need.get(k, 0) < v:
                    need[k] = v
        if extra:
            for k, v in extra:
                if need.get(k, 0) < v:
                    need[k] = v
        for k, v in need.items():
            if k == E.key and not E.self_sync:
                continue
            if E.waited.get(k, 0) >= v:
                continue
            E.waited[k] = v
            sem, sv = self._sem(k, v)
            E.h.wait_ge(sem, sv)

    def _mark(self, dep, reads, writes):
        k, v = dep
        for t in reads:
            if t.rs.get(k, 0) < v:
                t.rs[k] = v
        for t in writes:
            t.w = dep
            t.rs = {}

    def op(self, eng, fn, reads=(), writes=()):
        E = self.E[eng]
        self._waits(E, reads, writes)
        ins = fn(E.h)
        E.seq += 1
        ep = (E.seq - 1) // EPOCH
        assert ep < len(E.sems), f"too many instructions on {eng}"
        ins.then_inc(E.sems[ep], 1)
        self._mark((E.key, E.seq), reads, writes)
        self.nops += 1
        return ins

    def dma(self, queue, out, in_, reads=(), writes=()):
        Q = self.E[queue]
        base, cnt = self.dbase[queue]
        i = base + self.drr[queue] % cnt
        self.drr[queue] += 1
        extra = [(("d", i), self.dtot[i])] if self.dtot[i] > 0 else None
        self._waits(Q, reads, writes, extra)
        ins = Q.h.dma_start(out=out, in_=in_)
        ins.then_inc(self.dsem[i], 16)
        self.dtot[i] += 16
        self._mark((("d", i), self.dtot[i]), reads, writes)
        return ins

    def dma_multi(self, queue, pairs, reads=(), writes=()):
        Q = self.E[queue]
        base, cnt = self.dbase[queue]
        i = base + self.drr[queue] % cnt
        self.drr[queue] += 1
        extra = [(("d", i), self.dtot[i])] if self.dtot[i] > 0 else None
        self._waits(Q, reads, writes, extra)
        for out, in_ in pairs:
            ins = Q.h.dma_start(out=out, in_=in_)
            ins.then_inc(self.dsem[i], 16)
            self.dtot[i] += 16
        self._mark((("d", i), self.dtot[i]), reads, writes)

    def wait_all(self, eng, tiles):
        E = self.E[eng]
        self._waits(E, tiles, ())


def _fm(v):
    return np.ascontiguousarray(v.reshape(KC, 128).T)


def _consts(S):
    c = np.zeros((128, 128 + 512 + 512 + 128), np.float32)
    c[:, 0:128] = np.eye(128, dtype=np.float32)
    p = np.arange(128)[:, None]
    f = np.arange(128)[None, :]
    cur = np.where(p <= f, 0.0, -30000.0).astype(np.float32)
    prev = np.where(p > f, 0.0, -30000.0).astype(np.float32)
    c[:, 128:640] = np.tile(cur, (1, 4))
    c[:, 640:1152] = np.tile(prev, (1, 4))
    pm = np.zeros((128, 128), np.float32)
    for m in range(128):
        j = m % 64
        if j < 8:
            pm[m + 8, m] = 1.0
        elif j < 16:
            pm[m - 8, m] = 1.0
    c[:, 1152:1280] = pm
    pos = np.arange(S, dtype=np.float32)
    inv = (np.float32(ROPE_THETA) ** (-(np.arange(0, 16, 2, dtype=np.float32)) / np.float32(16))).astype(np.float32)
    ang = (pos[None, :] * inv[:, None]).astype(np.float32)
    cs = np.cos(ang.astype(np.float64)).astype(np.float32)
    sn = np.sin(ang.astype(np.float64)).astype(np.float32)
    C = np.ones((128, S), np.float32)
    Sg = np.zeros((128, S), np.float32)
    for m in range(128):
        j = m % 64
        if j < 8:
            C[m] = cs[j]
            Sg[m] = -sn[j]
        elif j < 16:
            C[m] = cs[j - 8]
            Sg[m] = sn[j - 8]
    return c, np.stack([C, Sg], 0)


ARENA_BYTES = 77 * 1024


def build_program(S=2048, DEPTH=4, TS=1024, subs=("mixer", "cross", "ffn"), self_sync=True, mix_stage=7):
    NSEG = S // TS
    NT = TS // 128
    NG = TS // 512
    NTT = S // 128
    nc = bass.Bass("TRN2", target_bir_lowering=False)
    L = DEPTH

    def din(name, shape):
        return nc.dram_tensor(name, list(shape), F32, kind="ExternalInput").ap()

    x_d = din("x", [S, D])
    mem_d = din("mem", [MEM, D])
    w_in_d = din("w_in", [L, D, IN_COLS])
    wk2_d = din("wk2", [L, D, 256])
    w_rg_d = din("w_rg", [L, 4, 256, 256])
    w_ig_d = din("w_ig", [L, 4, 256, 256])
    w_brr_d = din("w_br_rnn", [L, D, D])
    w_bra_d = din("w_br_attn", [L, D, D])
    w_out_d = din("w_out", [L, D, D])
    cq_d = din("cq_w", [L, D, D])
    ckv_d = din("ckv_w", [L, D, 2 * D])
    co_d = din("co_w", [L, D, D])
    wi_d = din("ffn_wi", [L, D, 2 * DFF])
    wo_d = din("ffn_wo", [L, DFF, D])
    vecs_d = din("vecs", [L, 128, NV])
    lnrow_d = din("lnrow", [L, 6, D])
    sinkcol_d = din("sinkcol", [L, 128, 8])
    consts_d = din("consts", [128, 1280])
    rope_d = din("rope", [2, 128, S])
    y_d = nc.dram_tensor("y", [S, D], F32, kind="ExternalOutput").ap()

    with ExitStack() as st:
        mk = MK(nc, st, self_sync=self_sync)

        def sb(name, shape, dt):
            return st.enter_context(nc.sbuf_tensor("sb_" + name, list(shape), dt))

        H = sb("H", [128, NTT, D], F32)
        HT = sb("HT", [128, KC, TS], BF16)
        ident = sb("ident", [128, 128], BF16)
        maskc = sb("maskc", [128, 512], BF16)
        maskp = sb("maskp", [128, 512], BF16)
        pmT = sb("pmT", [128, 128], BF16)
        ones = sb("ones", [128, 128], BF16)
        onesA = sb("onesA", [128, 128], BF16)
        onesB = sb("onesB", [128, 128], BF16)
        vecs = sb("vecs", [128, NV], F32)
        lv = sb("lv", [128, 64], F32)
        memT = sb("memT", [128, KC, MEM], BF16)
        KCT = sb("KCT", [128, KC, MEM], BF16)
        VC = sb("VC", [128, 2, D], BF16)
        wring = sb("wring", [128, NWSLOT, WSLOT], BF16)
        small = sb("small", [128, 64], F32)
        epsc = sb("epsc", [128, 4], F32)
        sinkt = sb("sinkt", [128, 8], F32)
        CONVST = sb("convst", [128, KC, 4], BF16)
        SCANST = sb("scanst", [128, KC], F32)
        KPREV = sb("kprev", [128, 2, 2, 128], BF16)
        VPREV = sb("vprev", [128, 2, 2, 128], BF16)
        ARENA = sb("arena", [128, ARENA_BYTES // 2], BF16)
        PS = st.enter_context(nc.psum_tensor("PS", [128, 8, 512], F32))

        def av(off, shape, dt):
            n = 1
            for d_ in shape:
                n *= d_
            nb = n * (4 if dt == F32 else 2)
            assert off % 4 == 0 and off + nb <= ARENA_BYTES, (off, nb)
            a = ARENA[:, off // 2:(off + nb) // 2]
            if dt == F32:
                a = a.bitcast(F32)
            if len(shape) == 2:
                a = a.rearrange("p (a b) -> p a b", a=shape[0])
            elif len(shape) == 3:
                a = a.rearrange("p (a b c) -> p a b c", a=shape[0], b=shape[1])
            return a

        t_H = mk.tiles(NTT)
        t_HT = mk.tiles(KC, NG)
        t_const = mk.tile()
        t_vecs = mk.tile()
        t_lv = mk.tile()
        t_memT = mk.tile()
        t_KCT = mk.tiles(KC)
        t_VC = mk.tiles(2)
        t_w = mk.tiles(NWSLOT)
        t_ps = [Tile(excl=True) for _ in range(8)]
        t_small = mk.tile()
        t_small4 = mk.tiles(4)
        t_convst = mk.tiles(KC)
        t_scanst = mk.tiles(KC)
        t_kprev = mk.tiles(2)
        t_vprev = mk.tiles(2)
        ps_rr = [0]
        w_rr = [0]

        def psum(n=1):
            i = ps_rr[0] % 8
            if n == 2 and i % 2 == 1:
                i = (i + 1) % 8
            ps_rr[0] = i + n
            return list(range(i, i + n))

        def wslot():
            i = w_rr[0] % NWSLOT
            w_rr[0] += 1
            return i

        def wload(slot, pairs):
            mk.dma_multi("pool", pairs, writes=[t_w[slot]])

        def wview(slot, k, n, off=0):
            return wring[:, slot, off:off + k * n].rearrange("p (k n) -> p k n", k=k)

        def wsrc(dram2d, c0, n, r0=0, k=KC):
            return dram2d[r0:r0 + k * 128, c0:c0 + n].rearrange("(k p) n -> p k n", p=128)

        def mm(bk, lhsT, rhs, start, stop, reads, cols=None):
            out = PS[:, bk, :] if cols is None else PS[:, bk, cols[0]:cols[1]]
            mk.op("pe", lambda e: e.matmul(out, lhsT=lhsT, rhs=rhs, start=start, stop=stop),
                  reads=reads, writes=[t_ps[bk]])

        def act(out, in_, func, reads, writes, bias=None, scale=None):
            kw = {}
            if bias is not None:
                kw["bias"] = bias
            if scale is not None:
                kw["scale"] = scale
            mk.op("act", lambda e: e.activation(out=out, in_=in_, func=func, **kw), reads=reads, writes=writes)

        def tt(out, in0, in1, op, reads, writes, eng="dve"):
            mk.op(eng, lambda e: e.tensor_tensor(out=out, in0=in0, in1=in1, op=op), reads=reads, writes=writes)

        def stt(out, in0, scalar, in1, op0, op1, reads, writes):
            mk.op("dve", lambda e: e.scalar_tensor_tensor(out=out, in0=in0, scalar=scalar, in1=in1, op0=op0, op1=op1),
                  reads=reads, writes=writes)

        def ts(out, in0, s1, s2, op0, op1, reads, writes):
            if s2 is None:
                mk.op("dve", lambda e: e.tensor_scalar(out=out, in0=in0, scalar1=s1, scalar2=None, op0=op0),
                      reads=reads, writes=writes)
            else:
                mk.op("dve", lambda e: e.tensor_scalar(out=out, in0=in0, scalar1=s1, scalar2=s2, op0=op0, op1=op1),
                      reads=reads, writes=writes)

        def cp(out, in_, reads, writes, eng="dve"):
            mk.op(eng, lambda e: e.tensor_copy(out=out, in_=in_), reads=reads, writes=writes)

        def memset(ap, val, writes, eng="dve"):
            mk.op(eng, lambda e: e.memset(ap, val), writes=writes)

        mk.dma_multi("pool", [(ident[:, :], consts_d[:, 0:128]), (maskc[:, :], consts_d[:, 128:640]),
                              (maskp[:, :], consts_d[:, 640:1152]), (pmT[:, :], consts_d[:, 1152:1280])],
                     writes=[t_const])
        memset(ones[:, :], 1.0, [t_const])
        memset(onesA[:, :], 0.0, [t_const])
        memset(onesB[:, :], 0.0, [t_const])
        memset(onesA[:, 0:64], 1.0, [t_const])
        memset(onesB[:, 64:128], 1.0, [t_const])
        memset(epsc[:, 0:1], LN_EPS, [t_const])
        memset(epsc[:, 1:2], 1.0, [t_const])
        memset(epsc[:, 2:3], math.log(0.5), [t_const])

        xv = x_d.rearrange("(t p) d -> p t d", p=128)
        for t0 in range(0, NTT, 4):
            mk.dma("sp", H[:, t0:t0 + 4, :], xv[:, t0:t0 + 4, :], writes=t_H[t0:t0 + 4])

        def transpose_into(src_of, n_tiles, dst_of, t_src_of, t_dst_of, hb_off, groups=None, t_hb=None, all_act=False):
            hb = av(hb_off, [2, D], BF16)
            if t_hb is None:
                t_hb = mk.tiles(2)
            for g0 in range(0, n_tiles, 4):
                if groups is not None and g0 // 4 not in groups:
                    continue
                nq = min(4, n_tiles - g0)
                banks = []
                for q in range(nq):
                    ti = g0 + q
                    b = q % 2
                    act(hb[:, b, :], src_of(ti), AF.Copy, [t_src_of(ti)], [t_hb[b]])
                    for c in range(KC):
                        if q == 0:
                            banks.append(psum()[0])
                        mm(banks[c], hb[:, b, c * 128:(c + 1) * 128], ident[:, :], True, True, [t_hb[b], t_const],
                           cols=(q * 128, (q + 1) * 128))
                for c in range(KC):
                    bk = banks[c]
                    dst, tdst = dst_of(c, g0, nq), t_dst_of(c, g0)
                    if c % 2 == 0 or all_act:
                        act(dst, PS[:, bk, 0:nq * 128], AF.Copy, [t_ps[bk]], [tdst])
                    else:
                        cp(dst, PS[:, bk, 0:nq * 128], [t_ps[bk]], [tdst])

        def make_HT(seg, hb_off, groups=None, t_hb=None, all_act=False):
            transpose_into(lambda ti: H[:, seg * NT + ti, :], NT,
                           lambda c, g0, nq: HT[:, c, g0 * 128:(g0 + nq) * 128],
                           lambda ti: t_H[seg * NT + ti], lambda c, g0: t_HT[c][g0 // 4], hb_off, groups, t_hb, all_act)

        ln_junk = [0]
        t_junk = [None]

        def ln_parts(gt, lng, lnb, t_lng, t_lnb):
            so = (gt % 4) * 16
            tsm = t_small4[gt % 4]

            def A():
                junk = av(ln_junk[0], [D], BF16)
                mk.op("act", lambda e: e.activation(out=junk, in_=H[:, gt, :], func=AF.Identity,
                                                    accum_out=small[:, so:so + 1]),
                      reads=[t_H[gt]], writes=[tsm, t_junk[0]])
                mk.op("act", lambda e: e.activation(out=junk, in_=H[:, gt, :], func=AF.Square,
                                                    accum_out=small[:, so + 1:so + 2]),
                      reads=[t_H[gt]], writes=[tsm, t_junk[0]])
                ts(small[:, so + 12:so + 13], small[:, so:so + 1], 1.0 / D, None, ALU.mult, None, [tsm], [tsm])
                tt(small[:, so + 2:so + 3], small[:, so + 12:so + 13], small[:, so + 12:so + 13], ALU.mult, [tsm], [tsm])
                stt(small[:, so + 13:so + 14], small[:, so + 1:so + 2], 1.0 / D, small[:, so + 2:so + 3], ALU.mult, ALU.subtract,
                    [tsm], [tsm])
                act(small[:, so + 14:so + 15], small[:, so + 13:so + 14], AF.Ln, [tsm, t_const], [tsm],
                    bias=epsc[:, 0:1], scale=1.0)
                act(small[:, so + 14:so + 15], small[:, so + 14:so + 15], AF.Exp, [tsm], [tsm], scale=-0.5)

            def B1():
                stt(small[:, so + 15:so + 16], small[:, so + 12:so + 13], -1.0, small[:, so + 14:so + 15], ALU.mult, ALU.mult,
                    [tsm], [tsm])
                act(H[:, gt, :], H[:, gt, :], AF.Identity, [tsm, t_H[gt]], [t_H[gt]],
                    bias=small[:, so + 15:so + 16], scale=small[:, so + 14:so + 15])

            def B2():
                tt(H[:, gt, :], H[:, gt, :], lng, ALU.mult, [t_lng, t_H[gt]], [t_H[gt]], eng=LN_G_ENG)
                tt(H[:, gt, :], H[:, gt, :], lnb, ALU.add, [t_lnb, t_H[gt]], [t_H[gt]])
            return A, B1, B2

        def out_proj_ln(l, seg, w_d, srcT, t_srcT, nk, ln_idx, ln_off, rebuild=True):
            mk.mark('outproj')
            lng = av(ln_off, [D], F32)
            lnb = av(ln_off + 4096, [D], F32)
            t_lng, t_lnb = mk.tile(), mk.tile()
            mk.dma("sp", lng, lnrow_d[l, ln_idx, :].partition_broadcast(128), writes=[t_lng])
            mk.dma("sp", lnb, lnrow_d[l, ln_idx + 1, :].partition_broadcast(128), writes=[t_lnb])
            hbt = mk.tiles(2)
            ln_junk[0] = ln_off + (12288 if rebuild else 8192)
            t_junk[0] = mk.tile()
            pend = None
            stages = []

            def ln_issue(gt):
                A, B1, B2 = ln_parts(gt, lng, lnb, t_lng, t_lnb)
                A()
                stages.append((B1, B2))
                n = len(stages)
                if n >= 2:
                    stages[n - 2][0]()
                if n >= 3:
                    stages[n - 3][1]()

            def ln_flush():
                n = len(stages)
                if n >= 1:
                    stages[n - 1][0]()
                if n >= 2:
                    stages[n - 2][1]()
                if n >= 1:
                    stages[n - 1][1]()

            for g in range(NG):
                for hf in range(2):
                    sl = []
                    for k0 in range(0, nk, 8):
                        kk = min(8, nk - k0)
                        s_ = wslot()
                        wload(s_, [(wview(s_, kk, 512), wsrc(w_d, hf * 512, 512, r0=k0 * 128, k=kk))])
                        sl.append((s_, k0, kk))
                    for q in range(4):
                        tk = g * 4 + q
                        gt = seg * NT + tk
                        bk = psum()[0]
                        for (s_, k0, kk) in sl:
                            for k in range(kk):
                                kg = k0 + k
                                mm(bk, srcT[:, kg, tk * 128:(tk + 1) * 128], wview(s_, kk, 512)[:, k, :],
                                   kg == 0, kg == nk - 1, [t_srcT[kg][g], t_w[s_]])
                        stt(H[:, gt, hf * 512:(hf + 1) * 512], H[:, gt, hf * 512:(hf + 1) * 512], ALPHA, PS[:, bk, :],
                            ALU.mult, ALU.add, [t_ps[bk], t_H[gt]], [t_H[gt]])
                        if hf == 1:
                            ln_issue(gt)
                if rebuild:
                    if pend is not None:
                        pend()
                    pend = (lambda g=g: make_HT(seg, ln_off + 8192, groups=[g], t_hb=hbt, all_act=True))
            ln_flush()
            if pend is not None:
                pend()

        def ffn(l, seg):
            mk.mark('ffn')
            mk.set_fence()
            actT = av(0, [NFC, TS], BF16)
            sg = av(45056, [2, 512], F32)
            t_act = mk.tiles(NFC, NG)
            t_sg = mk.tiles(2)
            r = 0
            for fb in range(6):
                ncol = 512 if fb < 5 else 256
                ncc = ncol // 128
                s_g = wslot()
                wload(s_g, [(wview(s_g, 8, ncol), wsrc(wi_d[l], fb * 512, ncol))])
                s_u = wslot()
                wload(s_u, [(wview(s_u, 8, ncol), wsrc(wi_d[l], DFF + fb * 512, ncol))])
                for cc in range(ncc):
                    c = fb * 4 + cc
                    for g in range(NG):
                        bg = psum()[0]
                        bu = psum()[0]
                        for k in range(KC):
                            mm(bg, wview(s_g, 8, ncol)[:, k, cc * 128:(cc + 1) * 128], HT[:, k, g * 512:(g + 1) * 512],
                               k == 0, k == KC - 1, [t_w[s_g], t_HT[k][g]])
                        for k in range(KC):
                            mm(bu, wview(s_u, 8, ncol)[:, k, cc * 128:(cc + 1) * 128], HT[:, k, g * 512:(g + 1) * 512],
                               k == 0, k == KC - 1, [t_w[s_u], t_HT[k][g]])
                        b = r % 2
                        r += 1
                        act(sg[:, b, :], PS[:, bg, :], AF.Silu, [t_ps[bg]], [t_sg[b]])
                        tt(actT[:, c, g * 512:(g + 1) * 512], PS[:, bu, :], sg[:, b, :], ALU.mult,
                           [t_ps[bu], t_sg[b]], [t_act[c][g]])
            nxt = l * NSEG + seg + 1
            if nxt < L * NSEG:
                make_HT(nxt % NSEG, 61440)
            out_proj_ln(l, seg, wo_d[l], actT, t_act, NFC, 4, 49152, rebuild=False)
            mk.set_fence()

        def prep_mem():
            mk.set_fence()
            mf = av(0, [2, D], F32)
            t_mf = mk.tiles(2)
            mk.dma("sp", mf, mem_d.rearrange("(t p) d -> p t d", p=128), writes=t_mf)
            transpose_into(lambda ti: mf[:, ti, :], 2, lambda c, g0, nq: memT[:, c, g0 * 128:(g0 + nq) * 128],
                           lambda ti: t_mf[ti], lambda c, g0: t_memT, 8192)
            mk.set_fence()

        def cross_kv(l):
            for ob in range(2):
                s_ = wslot()
                wload(s_, [(wview(s_, 8, 512), wsrc(ckv_d[l], ob * 512, 512))])
                for cc in range(4):
                    c = ob * 4 + cc
                    bk = psum()[0]
                    for k in range(KC):
                        mm(bk, wview(s_, 8, 512)[:, k, cc * 128:(cc + 1) * 128], memT[:, k, :], k == 0, k == KC - 1,
                           [t_w[s_], t_memT], cols=(0, MEM))
                    act(KCT[:, c, :], PS[:, bk, 0:MEM], AF.Copy, [t_ps[bk]], [t_KCT[c]])
            for hf in range(2):
                s_ = wslot()
                wload(s_, [(wview(s_, 8, 512), wsrc(ckv_d[l], D + hf * 512, 512))])
                for mt in range(2):
                    bk = psum()[0]
                    for k in range(KC):
                        mm(bk, memT[:, k, mt * 128:(mt + 1) * 128], wview(s_, 8, 512)[:, k, :], k == 0, k == KC - 1,
                           [t_w[s_], t_memT])
                    cp(VC[:, mt, hf * 512:(hf + 1) * 512], PS[:, bk, :], [t_ps[bk]], [t_VC[mt]])

        def cross(l, seg):
            mk.mark('cross')
            mk.set_fence()
            QCT = av(0, [KC, TS], BF16)
            OT = av(16384, [KC, TS], BF16)
            EE = av(32768, [4, 512], BF16)
            LG = av(36864, [2, 512], F32)
            t_Q = mk.tiles(KC, NG)
            t_O = mk.tiles(KC, NG)
            t_E = mk.tiles(4)
            t_LG = mk.tiles(2)
            for ob in range(2):
                s_ = wslot()
                wload(s_, [(wview(s_, 8, 512), wsrc(cq_d[l], ob * 512, 512))])
                for cc in range(4):
                    c = ob * 4 + cc
                    for g in range(NG):
                        bk = psum()[0]
                        for k in range(KC):
                            mm(bk, wview(s_, 8, 512)[:, k, cc * 128:(cc + 1) * 128], HT[:, k, g * 512:(g + 1) * 512],
                               k == 0, k == KC - 1, [t_w[s_], t_HT[k][g]])
                        if (cc + g) % 2 == 0:
                            act(QCT[:, c, g * 512:(g + 1) * 512], PS[:, bk, :], AF.Copy, [t_ps[bk]], [t_Q[c][g]])
                        else:
                            cp(QCT[:, c, g * 512:(g + 1) * 512], PS[:, bk, :], [t_ps[bk]], [t_Q[c][g]])
            r = 0
            for hh in range(4):
                for g in range(NG):
                    eb = (r % 2) * 2
                    r += 1
                    for mt in range(2):
                        bk = psum()[0]
                        for kc in range(2):
                            c = 2 * hh + kc
                            mm(bk, KCT[:, c, mt * 128:(mt + 1) * 128], QCT[:, c, g * 512:(g + 1) * 512], kc == 0, kc == 1,
                               [t_KCT[c], t_Q[c][g]])
                        act(EE[:, eb + mt, :], PS[:, bk, :], AF.Exp, [t_ps[bk]], [t_E[eb + mt]], scale=1.0 / 16.0)
                    bd = psum()[0]
                    for mt in range(2):
                        mm(bd, ones[:, :], EE[:, eb + mt, :], mt == 0, mt == 1, [t_const, t_E[eb + mt]])
                    bn = []
                    for kc in range(2):
                        c = 2 * hh + kc
                        bk = psum()[0]
                        bn.append(bk)
                        for mt in range(2):
                            mm(bk, VC[:, mt, c * 128:(c + 1) * 128], EE[:, eb + mt, :], mt == 0, mt == 1,
                               [t_VC[mt], t_E[eb + mt]])
                    lb = (r % 2)
                    act(LG[:, lb, :], PS[:, bd, :], AF.Ln, [t_ps[bd]], [t_LG[lb]])
                    act(LG[:, lb, :], LG[:, lb, :], AF.Exp, [t_LG[lb]], [t_LG[lb]], scale=-1.0)
                    for kc in range(2):
                        c = 2 * hh + kc
                        tt(OT[:, c, g * 512:(g + 1) * 512], PS[:, bn[kc], :], LG[:, lb, :], ALU.mult,
                           [t_ps[bn[kc]], t_LG[lb]], [t_O[c][g]])
            mk.set_fence()
            out_proj_ln(l, seg, co_d[l], OT, t_O, KC, 2, 0, rebuild=True)
            mk.set_fence()


        class Bump:
            def __init__(self, base):
                self.o = base

            def __call__(self, n):
                n = (n + 31) // 32 * 32
                r = self.o
                self.o += n
                assert self.o <= ARENA_BYTES, self.o
                return r

        def layer_prep(l):
            mk.dma("sp", sinkt[:, :], sinkcol_d[l], writes=[t_small])
            act(lv[:, 32:40], vecs[:, V_LAM:V_LAM + 8], AF.Exp, [t_vecs], [t_lv], scale=-1.0)
            act(lv[:, 32:40], lv[:, 32:40], AF.Ln, [t_lv, t_const], [t_lv], bias=epsc[:, 1:2], scale=1.0)
            ts(lv[:, 0:8], lv[:, 32:40], -4.0, None, ALU.mult, None, [t_lv], [t_lv])
            ts(lv[:, 8:16], vecs[:, V_BRG:V_BRG + 8], 0.5, None, ALU.mult, None, [t_vecs, t_lv], [t_lv])
            ts(lv[:, 16:24], vecs[:, V_BIG:V_BIG + 8], 0.5, None, ALU.mult, None, [t_vecs, t_lv], [t_lv])
            act(lv[:, 24:32], sinkt[:, :], AF.Exp, [t_small, t_lv], [t_lv])
            for c in range(KC):
                memset(CONVST[:, c, :], 0.0, [t_convst[c]])
                memset(SCANST[:, c:c + 1], 0.0, [t_scanst[c]])
            for gk in range(2):
                memset(KPREV[:, gk, :, :], 0.0, [t_kprev[gk]])
                memset(VPREV[:, gk, :, :], 0.0, [t_vprev[gk]])

        def rnn_phase(l, seg, YR, t_YR):
            mk.mark('rnn')
            o = Bump(32768)
            o2 = Bump(16384)
            DW = [av(o(2048), [2, 4, 128], BF16) for _ in range(2)]
            XRb = av(o(2 * (TS + 4) * 2), [2, TS + 4], BF16)
            XCb = av(o(2048), [2, 512], BF16)

            def two(a, b):
                return [av(a(4096), [2, 512], F32), av(b(4096), [2, 512], F32)]
            XC32 = two(o, o2)
            GL = two(o, o2)
            Rt = two(o, o2)
            It = two(o, o2)
            Aa = two(o, o)
            Mq = two(o, o)
            assert o2.o <= 32768
            t_DW = mk.tiles(2, 2)
            t_XR = mk.tiles(2, NG)
            t_XRh = mk.tiles(2)
            t_XCb = mk.tiles(2)
            t_XC32, t_GL, t_Rt, t_It, t_Aa, t_Mq = (mk.tiles(2, 2) for _ in range(6))
            its = [(n, g) for n in range(4) for g in range(NG)]
            blk = {}

            def front(i):
                n, g = its[i]
                p = i % 2
                dw, tdw = DW[n % 2], t_DW[n % 2]
                if g == 0:
                    sA = wslot()
                    wA = wview(sA, 8, 512)
                    wload(sA, [(wA[:, :, 0:256], wsrc(w_in_d[l], n * 256, 256)),
                               (wA[:, :, 256:512], wsrc(w_in_d[l], 1024 + n * 256, 256))])
                    sB = wslot()
                    rgv = wview(sB, 2, 256)
                    igv = wview(sB, 2, 256, off=512)
                    wload(sB, [(rgv, w_rg_d[l, n].rearrange("(k p) n -> p k n", p=128)),
                               (igv, w_ig_d[l, n].rearrange("(k p) n -> p k n", p=128))])
                    blk[n] = (sA, wA, sB, rgv, igv)
                    for cc in range(2):
                        c = 2 * n + cc
                        for tap in range(4):
                            ts(dw[:, cc, tap, :], ident[:, :], vecs[:, V_CONVW + tap * 8 + c:V_CONVW + tap * 8 + c + 1], None,
                               ALU.mult, None, [t_const, t_vecs], [tdw[cc]])
                        cp(XRb[:, cc, 0:3], CONVST[:, c, 0:3], [t_convst[c]], [t_XRh[cc]])
                sA, wA, sB, rgv, igv = blk[n]
                for cc in range(2):
                    bx = psum()[0]
                    for k in range(KC):
                        mm(bx, wA[:, k, cc * 128:(cc + 1) * 128], HT[:, k, g * 512:(g + 1) * 512], k == 0, k == KC - 1,
                           [t_w[sA], t_HT[k][g]])
                    cp(XRb[:, cc, 3 + g * 512:3 + (g + 1) * 512], PS[:, bx, :], [t_ps[bx]], [t_XR[cc][g]])
                    bg = psum()[0]
                    for k in range(KC):
                        mm(bg, wA[:, k, 256 + cc * 128:256 + (cc + 1) * 128], HT[:, k, g * 512:(g + 1) * 512], k == 0,
                           k == KC - 1, [t_w[sA], t_HT[k][g]])
                    act(GL[p][:, cc, :], PS[:, bg, :], AF.Gelu_apprx_tanh, [t_ps[bg]], [t_GL[p][cc]])
                for cc in range(2):
                    c = 2 * n + cc
                    bc = psum()[0]
                    rd = [tdw[cc], t_XR[cc][g], t_XR[cc][g - 1] if g > 0 else t_XRh[cc]]
                    for tap in range(4):
                        mm(bc, dw[:, cc, tap, :], XRb[:, cc, g * 512 + tap:g * 512 + tap + 512], tap == 0, tap == 3, rd)
                    act(XCb[:, cc, :], PS[:, bc, :], AF.Identity, [t_ps[bc], t_vecs], [t_XCb[cc]],
                        bias=vecs[:, V_CONVB + c:V_CONVB + c + 1], scale=1.0)
                    ts(XC32[p][:, cc, :], PS[:, bc, :], vecs[:, V_CONVB + c:V_CONVB + c + 1], None, ALU.add, None,
                       [t_ps[bc], t_vecs], [t_XC32[p][cc]])
                brs, bis = [], []
                for co in range(2):
                    br = psum()[0]
                    for kc in range(2):
                        mm(br, rgv[:, kc, co * 128:(co + 1) * 128], XCb[:, kc, :], kc == 0, kc == 1, [t_w[sB], t_XCb[kc]])
                    bi = psum()[0]
                    for kc in range(2):
                        mm(bi, igv[:, kc, co * 128:(co + 1) * 128], XCb[:, kc, :], kc == 0, kc == 1, [t_w[sB], t_XCb[kc]])
                    brs.append(br)
                    bis.append(bi)
                for co in range(2):
                    c = 2 * n + co
                    act(Rt[p][:, co, :], PS[:, brs[co], :], AF.Tanh, [t_ps[brs[co]], t_lv], [t_Rt[p][co]],
                        bias=lv[:, 8 + c:9 + c], scale=0.5)
                    act(It[p][:, co, :], PS[:, bis[co], :], AF.Tanh, [t_ps[bis[co]], t_lv], [t_It[p][co]],
                        bias=lv[:, 16 + c:17 + c], scale=0.5)
                for co in range(2):
                    c = 2 * n + co
                    act(Aa[p][:, co, :], Rt[p][:, co, :], AF.Exp, [t_Rt[p][co], t_lv], [t_Aa[p][co]], bias=lv[:, c:c + 1],
                        scale=lv[:, c:c + 1])
                for co in range(2):
                    act(Mq[p][:, co, :], Aa[p][:, co, :], AF.Square, [t_Aa[p][co]], [t_Mq[p][co]])
                for co in range(2):
                    act(Mq[p][:, co, :], Mq[p][:, co, :], AF.Ln, [t_Mq[p][co], t_const], [t_Mq[p][co]], bias=epsc[:, 1:2], scale=-1.0)
                for co in range(2):
                    act(Mq[p][:, co, :], Mq[p][:, co, :], AF.Exp, [t_Mq[p][co], t_const], [t_Mq[p][co]], bias=epsc[:, 2:3], scale=0.5)
                if g == NG - 1:
                    for cc in range(2):
                        c = 2 * n + cc
                        cp(CONVST[:, c, 0:3], XRb[:, cc, TS:TS + 3], [t_XR[cc][NG - 1]], [t_convst[c]])

            def tail(i):
                n, g = its[i]
                p = i % 2
                for co in range(2):
                    c = 2 * n + co
                    stt(It[p][:, co, :], It[p][:, co, :], 1.0, XC32[p][:, co, :], ALU.add, ALU.mult,
                        [t_It[p][co], t_XC32[p][co]], [t_It[p][co]])
                    tt(It[p][:, co, :], It[p][:, co, :], Mq[p][:, co, :], ALU.mult, [t_It[p][co], t_Mq[p][co]], [t_It[p][co]])
                    mk.op("dve", lambda e, co=co, c=c: e.tensor_tensor_scan(
                        out=XC32[p][:, co, :], data0=Aa[p][:, co, :], data1=It[p][:, co, :], initial=SCANST[:, c:c + 1],
                        op0=ALU.mult, op1=ALU.add), reads=[t_Aa[p][co], t_It[p][co], t_scanst[c]], writes=[t_XC32[p][co]])
                    cp(SCANST[:, c:c + 1], XC32[p][:, co, 511:512], [t_XC32[p][co]], [t_scanst[c]])
                    tt(YR[:, c, g * 512:(g + 1) * 512], XC32[p][:, co, :], GL[p][:, co, :], ALU.mult,
                       [t_XC32[p][co], t_GL[p][co]], [t_YR[c][g]])

            front(0)
            for i in range(1, len(its)):
                front(i)
                tail(i - 1)
            tail(len(its) - 1)

        def att_phase(l, seg, YA, t_YA):
            mk.mark('att')
            o = Bump(32768)
            ROPE = av(o(8192), [2, TS], F32)
            KA = av(o(2 * (128 + TS)), [128 + TS], BF16)
            KB = av(o(2 * (128 + TS)), [128 + TS], BF16)
            VA = av(o(2 * (1 + NT) * 256), [1 + NT, 2, 128], BF16)
            VB = av(o(2 * (1 + NT) * 256), [1 + NT, 2, 128], BF16)
            QT = av(o(8 * TS), [NT, 4, 128], BF16)
            QRAW = av(o(1024), [512], BF16)
            T1 = av(o(2048), [512], F32)
            T2 = av(o(2048), [512], F32)
            EE2 = [av(o(4096), [2, 2, 512], BF16), av(o(4096), [2, 2, 512], BF16)]
            LG = av(o(2048), [512], F32)
            t_rope, t_KA, t_KB, t_VA, t_VB, t_QRAW, t_T1, t_T2, t_LG = (mk.tile() for _ in range(9))
            t_QT = mk.tiles(NT)
            t_E2 = mk.tiles(2, 2)
            mk.dma("sp", ROPE, rope_d[:, :, seg * TS:(seg + 1) * TS].rearrange("a p s -> p a s"), writes=[t_rope])

            rp = [0]
            T12 = [T1, T2]
            t_T12 = [t_T1, t_T2]

            def roped(bk, dsts, g, rd_extra):
                p = rp[0] % 2
                rp[0] += 1
                Tp, t_Tp = T12[p], t_T12[p]
                act(QRAW, PS[:, bk, :], AF.Copy, [t_ps[bk]], [t_QRAW])
                bs = psum()[0]
                mm(bs, pmT[:, :], QRAW, True, True, [t_const, t_QRAW])
                tt(Tp, PS[:, bk, :], ROPE[:, 0, g * 512:(g + 1) * 512], ALU.mult, [t_ps[bk], t_rope], [t_Tp])
                tt(PS[:, bs, :], PS[:, bs, :], ROPE[:, 1, g * 512:(g + 1) * 512], ALU.mult, [t_ps[bs], t_rope], [t_ps[bs]])
                for (out_ap, p0, p1, shp, wr) in dsts:
                    i0, i1 = PS[p0:p1, bs, :], Tp[p0:p1, :]
                    if shp is not None:
                        i0 = i0.rearrange("p (a b) -> p a b", a=shp)
                        i1 = i1.rearrange("p (a b) -> p a b", a=shp)
                    tt(out_ap, i0, i1, ALU.add, [t_ps[bs], t_Tp], wr)

            memset(VA[:, :, :, :], 0.0, [t_VA])
            memset(VB[:, :, :, :], 0.0, [t_VB])
            for gk in range(2):
                memset(KA[64:128, :], 0.0, [t_KA])
                memset(KB[0:64, :], 0.0, [t_KB])
                cp(KA[0:64, 0:128], KPREV[0:64, gk, 0, :], [t_kprev[gk]], [t_KA])
                cp(KB[64:128, 0:128], KPREV[64:128, gk, 1, :], [t_kprev[gk]], [t_KB])
                sK = wslot()
                wK = wview(sK, 8, 128)
                wV = wview(sK, 8, 128, off=1024)
                pairs = [(wK, wsrc(wk2_d[l], gk * 128, 128))]
                if gk == 0:
                    pairs.append((wV, wsrc(w_in_d[l], 3200, 128)))
                wload(sK, pairs)
                for g in range(NG):
                    bk = psum()[0]
                    for k in range(KC):
                        mm(bk, wK[:, k, :], HT[:, k, g * 512:(g + 1) * 512], k == 0, k == KC - 1, [t_w[sK], t_HT[k][g]])
                    c0 = 128 + g * 512
                    roped(bk, [(KA[0:64, c0:c0 + 512], 0, 64, None, [t_KA]), (KB[64:128, c0:c0 + 512], 64, 128, None, [t_KB])], g, None)
                if _KCUT < 12:
                    continue
                if gk == 0:
                    for i in range(2):
                        cp(VA[:, 0, i, :], VPREV[:, i, 0, :], [t_vprev[i]], [t_VA])
                        cp(VB[:, 0, i, :], VPREV[:, i, 1, :], [t_vprev[i]], [t_VB])
                    for tk in range(NT):
                        bv = psum()[0]
                        for k in range(KC):
                            mm(bv, HT[:, k, tk * 128:(tk + 1) * 128], wV[:, k, :], k == 0, k == KC - 1,
                               [t_w[sK], t_HT[k][tk // 4]], cols=(0, 128))
                        src = PS[:, bv, 0:128].rearrange("p (g d) -> p g d", g=2)
                        act(VA[:, 1 + tk, :, 0:64], src, AF.Copy, [t_ps[bv]], [t_VA])
                        cp(VB[:, 1 + tk, :, 64:128], src, [t_ps[bv]], [t_VB])
                if _KCUT < 13:
                    continue
                sQ = wslot()
                wQ = wview(sQ, 8, 512)
                wload(sQ, [(wQ, wsrc(w_in_d[l], 2048 + gk * 512, 512))])
                for cc in range(4):
                    for g in range(NG):
                        bq = psum()[0]
                        for k in range(KC):
                            mm(bq, wQ[:, k, cc * 128:(cc + 1) * 128], HT[:, k, g * 512:(g + 1) * 512], k == 0, k == KC - 1,
                               [t_w[sQ], t_HT[k][g]])
                        roped(bq, [(QT[:, g * 4:(g + 1) * 4, cc, :], 0, 128, 4, t_QT[g * 4:(g + 1) * 4])], g, None)
                def kblocks(qb):
                    gb = seg * NT + qb
                    kb = []
                    if gb > 0:
                        kb.append((0, qb * 128, qb, maskp))
                    kb.append((1, (qb + 1) * 128, qb + 1, maskc))
                    return kb

                def emit_scores(qb):
                    qrhs = QT[:, qb, :, :].rearrange("p a b -> p (a b)")
                    EE, t_E = EE2[qb % 2], t_E2[qb % 2]
                    for (slot, kcol, vt, mask) in kblocks(qb):
                        sc = [2 * slot, 2 * slot + 1]
                        for hf in range(2):
                            mm(sc[hf], ident[:, :], mask[:, :], True, False, [t_const])
                            kk_ = KA if hf == 0 else KB
                            mm(sc[hf], kk_[:, kcol:kcol + 128], qrhs, False, True, [t_KA if hf == 0 else t_KB, t_QT[qb]])
                        act(EE[:, slot, :, :], PS[:, sc[0]:sc[0] + 2, :], AF.Exp, [t_ps[sc[0]], t_ps[sc[1]]], [t_E[slot]], scale=0.125)

                def emit_nd(qb):
                    EE, t_E = EE2[qb % 2], t_E2[qb % 2]
                    kb = kblocks(qb)
                    bnum, bden = (4, 5) if qb % 2 == 0 else (6, 7)
                    nmm = 2 * len(kb)
                    i = 0
                    for (slot, kcol, vt, mask) in kb:
                        for hf in range(2):
                            vv = VA if hf == 0 else VB
                            mm(bnum, vv[:, vt, gk, :], EE[:, slot, hf, :], i == 0, i == nmm - 1,
                               [t_VA if hf == 0 else t_VB, t_E[slot]])
                            i += 1
                    i = 0
                    for (slot, kcol, vt, mask) in kb:
                        for hf in range(2):
                            mm(bden, (onesA if hf == 0 else onesB)[:, :], EE[:, slot, hf, :], i == 0, i == nmm - 1,
                               [t_const, t_E[slot]])
                            i += 1
                    for cc in range(4):
                        j = 24 + gk * 4 + cc
                        act(LG[:, cc * 128:(cc + 1) * 128], PS[:, bden, cc * 128:(cc + 1) * 128], AF.Ln, [t_ps[bden], t_lv], [t_LG],
                            bias=lv[:, j:j + 1], scale=1.0)
                    act(LG, LG, AF.Exp, [t_LG], [t_LG], scale=-1.0)
                    tt(YA[:, gk * 4:(gk + 1) * 4, qb * 128:(qb + 1) * 128], PS[:, bnum, :].rearrange("p (a b) -> p a b", a=4),
                       LG.rearrange("p (a b) -> p a b", a=4), ALU.mult, [t_ps[bnum], t_LG],
                       [t_YA[gk * 4 + cc][qb // 4] for cc in range(4)])

                emit_scores(0)
                for qb in range(1, NT):
                    emit_scores(qb)
                    emit_nd(qb - 1)
                emit_nd(NT - 1)
                cp(KPREV[0:64, gk, 0, :], KA[0:64, TS:TS + 128], [t_KA], [t_kprev[gk]])
                cp(KPREV[64:128, gk, 1, :], KB[64:128, TS:TS + 128], [t_KB], [t_kprev[gk]])
            for i in range(2):
                cp(VPREV[:, i, 0, :], VA[:, NT, i, :], [t_VA], [t_vprev[i]])
                cp(VPREV[:, i, 1, :], VB[:, NT, i, :], [t_VB], [t_vprev[i]])

        def merge_phase(l, seg, YR, t_YR, YA, t_YA, M, t_M):
            mk.mark('merge')
            TM = av(49152, [4, TS], F32)
            SG = av(65536, [2, 512], F32)
            V2 = av(69632, [512], F32)
            t_TM = mk.tiles(4, NG)
            t_SG = mk.tiles(2)
            t_V2 = mk.tile()
            r = 0
            for ob in range(2):
                for br in range(2):
                    Ysrc, t_Ysrc = (YR, t_YR) if br == 0 else (YA, t_YA)
                    w1 = w_brr_d[l] if br == 0 else w_bra_d[l]
                    gcol = (3328 if br == 0 else 4352) + ob * 512
                    s1 = wslot()
                    wload(s1, [(wview(s1, 8, 512), wsrc(w1, ob * 512, 512))])
                    s2 = wslot()
                    wload(s2, [(wview(s2, 8, 512), wsrc(w_in_d[l], gcol, 512))])
                    for cc in range(4):
                        c = ob * 4 + cc
                        for g in range(NG):
                            bz = psum()[0]
                            for k in range(KC):
                                mm(bz, wview(s1, 8, 512)[:, k, cc * 128:(cc + 1) * 128], Ysrc[:, k, g * 512:(g + 1) * 512],
                                   k == 0, k == KC - 1, [t_w[s1], t_Ysrc[k][g]])
                            bg = psum()[0]
                            for k in range(KC):
                                mm(bg, wview(s2, 8, 512)[:, k, cc * 128:(cc + 1) * 128], HT[:, k, g * 512:(g + 1) * 512],
                                   k == 0, k == KC - 1, [t_w[s2], t_HT[k][g]])
                            b = r % 2
                            r += 1
                            act(SG[:, b, :], PS[:, bg, :], AF.Sigmoid, [t_ps[bg]], [t_SG[b]])
                            if br == 0:
                                tt(TM[:, cc, g * 512:(g + 1) * 512], PS[:, bz, :], SG[:, b, :], ALU.mult,
                                   [t_ps[bz], t_SG[b]], [t_TM[cc][g]])
                            else:
                                tt(V2, PS[:, bz, :], SG[:, b, :], ALU.mult, [t_ps[bz], t_SG[b]], [t_V2])
                                tt(M[:, c, g * 512:(g + 1) * 512], V2, TM[:, cc, g * 512:(g + 1) * 512], ALU.add,
                                   [t_V2, t_TM[cc][g]], [t_M[c][g]])

        def mixer(l, seg):
            mk.set_fence()
            YR = av(0, [KC, TS], BF16)
            YA = av(16384, [KC, TS], BF16)
            t_YR = mk.tiles(KC, NG)
            t_YA = mk.tiles(KC, NG)
            if mix_stage & 1:
                rnn_phase(l, seg, YR, t_YR)
            mk.set_fence()
            if mix_stage & 2:
                att_phase(l, seg, YA, t_YA)
            mk.set_fence()
            if not (mix_stage & 4):
                return
            M = av(32768, [KC, TS], BF16)
            t_M = mk.tiles(KC, NG)
            merge_phase(l, seg, YR, t_YR, YA, t_YA, M, t_M)
            mk.set_fence()
            out_proj_ln(l, seg, w_out_d[l], M, t_M, KC, 0, 0, rebuild=True)
            mk.set_fence()

        if "cross" in subs:
            prep_mem()
        for l in range(L):
            mk.dma("sp", vecs[:, :], vecs_d[l], writes=[t_vecs])
            if "cross" in subs:
                cross_kv(l)
            if "mixer" in subs:
                layer_prep(l)
            for seg in range(NSEG):
                mk.set_fence()
                if (l == 0 and seg == 0) or "ffn" not in subs:
                    make_HT(seg, 0)
                    mk.set_fence()
                if "mixer" in subs:
                    mixer(l, seg)
                if "cross" in subs:
                    cross(l, seg)
                if "ffn" in subs:
                    ffn(l, seg)

        yv = y_d.rearrange("(t p) d -> p t d", p=128)
        t_out = mk.tile()
        for t0 in range(0, NTT, 4):
            mk.dma("sp", yv[:, t0:t0 + 4, :], H[:, t0:t0 + 4, :], reads=t_H[t0:t0 + 4], writes=[t_out])
        sp = mk.E["sp"]
        for i in range(NDMA):
            if mk.dtot[i] > 0 and sp.waited.get(("d", i), 0) < mk.dtot[i]:
                sp.h.wait_ge(mk.dsem[i], mk.dtot[i])
    return nc, mk


def prep_inputs(inputs, S, L):
    f32 = np.float32
    g = {k: np.asarray(v, dtype=f32) for k, v in inputs.items() if k not in ("x", "mem")}
    w_in = g["w_in"][:L]
    kcols = w_in[:, :, 3072:3200]
    wk2 = np.concatenate([kcols[:, :, 0:64], kcols[:, :, 0:64], kcols[:, :, 64:128], kcols[:, :, 64:128]], axis=2)
    vecs = np.zeros((L, 128, NV), f32)
    for l in range(L):
        for tap in range(4):
            vecs[l, :, V_CONVW + tap * 8:V_CONVW + tap * 8 + 8] = _fm(g["conv_w"][l, tap])
        vecs[l, :, V_CONVB:V_CONVB + 8] = _fm(g["conv_b"][l])
        vecs[l, :, V_BRG:V_BRG + 8] = _fm(g["b_rg"][l])
        vecs[l, :, V_BIG:V_BIG + 8] = _fm(g["b_ig"][l])
        vecs[l, :, V_LAM:V_LAM + 8] = _fm(g["lru_lambda"][l])
    lnrow = np.stack([g["ln1_g"][:L], g["ln1_b"][:L], g["ln2_g"][:L], g["ln2_b"][:L], g["ln3_g"][:L], g["ln3_b"][:L]], axis=1)
    sinkcol = np.zeros((L, 128, 8), f32)
    for l in range(L):
        for j in range(8):
            sinkcol[l, 0:64, j] = g["sinks"][l, 2 * j]
            sinkcol[l, 64:128, j] = g["sinks"][l, 2 * j + 1]
    consts, rope = _consts(S)
    shared = {
        "w_in": np.ascontiguousarray(w_in), "wk2": np.ascontiguousarray(wk2),
        "w_rg": g["w_rg"][:L], "w_ig": g["w_ig"][:L], "w_br_rnn": g["w_br_rnn"][:L], "w_br_attn": g["w_br_attn"][:L],
        "w_out": g["w_out"][:L], "cq_w": g["cq_w"][:L], "ckv_w": g["ckv_w"][:L], "co_w": g["co_w"][:L],
        "ffn_wi": g["ffn_wi"][:L], "ffn_wo": g["ffn_wo"][:L], "vecs": vecs, "lnrow": np.ascontiguousarray(lnrow),
        "sinkcol": sinkcol, "consts": consts, "rope": rope,
    }
    return shared


_CACHE = {}


def kernel(**inputs):
    x = np.asarray(inputs["x"], dtype=np.float32)
    mem = np.asarray(inputs["mem"], dtype=np.float32)
    B, S, _ = x.shape
    L = inputs["w_in"].shape[0]
    key = (S, L)
    if key not in _CACHE:
        _CACHE[key] = build_program(S=S, DEPTH=L)[0]
    nc = _CACHE[key]
    shared = prep_inputs(inputs, S, L)
    in_maps = []
    for b in range(B):
        m = dict(shared)
        m["x"] = np.ascontiguousarray(x[b])
        m["mem"] = np.ascontiguousarray(mem[b])
        in_maps.append(m)
    res = run_bass_kernel_spmd(nc, in_maps, core_ids=list(range(B)))
    return np.stack([r["y"] for r in res.results], axis=0).astype(np.float32)
```

```python
import math
_KCUT = 99
LN_G_ENG = "dve"
from contextlib import ExitStack

import numpy as np
import concourse.bass as bass
import concourse.mybir as mybir
from concourse.bass_utils import run_bass_kernel_spmd

F32 = mybir.dt.float32
BF16 = mybir.dt.bfloat16
AF = mybir.ActivationFunctionType
ALU = mybir.AluOpType

D = 1024
KC = 8
MEM = 256
DFF = 2816
NFC = 22
IN_COLS = 5376
ALPHA = 8.0 ** 0.25
LN_EPS = 1e-5
ROPE_THETA = 500000.0
EPOCH = 16000
NEPOCH = 8
NDMA = 24
WSLOT = 4096
NWSLOT = 4
NV = 112

V_CONVW = 0
V_CONVB = 32
V_BRG = 40
V_BIG = 48
V_LAM = 56


class Tile:
    __slots__ = ("w", "rs", "excl")

    def __init__(self, fence=None, excl=False):
        self.w = None
        self.rs = dict(fence) if fence else {}
        self.excl = excl


class Eng:
    def __init__(self, name, h, key, sems, self_sync):
        self.name = name
        self.h = h
        self.key = key
        self.sems = sems
        self.seq = 0
        self.waited = {}
        self.self_sync = self_sync


class MK:
    def __init__(self, nc, st, self_sync=True):
        self.nc = nc
        self.E = {}
        for name, h, ss in (("pe", nc.tensor, False), ("act", nc.scalar, self_sync),
                            ("dve", nc.vector, self_sync), ("pool", nc.gpsimd, self_sync),
                            ("sp", nc.sync, False)):
            sems = [st.enter_context(nc.semaphore(f"s_{name}{i}")) for i in range(NEPOCH if name != "sp" else 1)]
            self.E[name] = Eng(name, h, ("e", name), sems, ss)
        self.dsem = [st.enter_context(nc.semaphore(f"s_dma{i}")) for i in range(NDMA)]
        self.dtot = [0] * NDMA
        self.drr = {"pool": 0, "sp": 0}
        self.dbase = {"pool": (0, NDMA // 2), "sp": (NDMA // 2, NDMA - NDMA // 2)}
        self.fence = {}
        self.nops = 0
        self.marks = []

    def mark(self, name):
        self.marks.append((name, self.E['pe'].seq))

    def tile(self):
        return Tile(self.fence)

    def tiles(self, *shape):
        if len(shape) == 1:
            return [self.tile() for _ in range(shape[0])]
        return [self.tiles(*shape[1:]) for _ in range(shape[0])]

    def set_fence(self):
        f = {}
        for e in self.E.values():
            if e.seq > 0:
                f[e.key] = e.seq
        self.fence = f

    def _sem(self, key, val):
        if key[0] == "e":
            e = self.E[key[1]]
            ep = (val - 1) // EPOCH
            return e.sems[ep], val - ep * EPOCH
        return self.dsem[key[1]], val

    def _waits(self, E, reads, writes, extra=None):
        need = {}
        for t in reads:
            if t.w is not None:
                k, v = t.w
                if need.get(k, 0) < v:
                    need[k] = v
            if t.excl:
                for k, v in t.rs.items():
                    if k != E.key and need.get(k, 0) < v:
                        need[k] = v
        for t in writes:
            if t.w is not None:
                k, v = t.w
                if need.get(k, 0) < v:
                    need[k] = v
            for k, v in t.rs.items():
                if need.get(k, 0) < v:
                    need[k] = v
        if extra:
            for k, v in extra:
                if need.get(k, 0) < v:
                    need[k] = v
        for k, v in need.items():
            if k == E.key and not E.self_sync:
                continue
            if E.waited.get(k, 0) >= v:
                continue
            E.waited[k] = v
            sem, sv = self._sem(k, v)
            E.h.wait_ge(sem, sv)

    def _mark(self, dep, reads, writes):
        k, v = dep
        for t in reads:
            if t.rs.get(k, 0) < v:
                t.rs[k] = v
        for t in writes:
            t.w = dep
            t.rs = {}

    def op(self, eng, fn, reads=(), writes=()):
        E = self.E[eng]
        self._waits(E, reads, writes)
        ins = fn(E.h)
        E.seq += 1
        ep = (E.seq - 1) // EPOCH
        assert ep < len(E.sems), f"too many instructions on {eng}"
        ins.then_inc(E.sems[ep], 1)
        self._mark((E.key, E.seq), reads, writes)
        self.nops += 1
        return ins

    def dma(self, queue, out, in_, reads=(), writes=()):
        Q = self.E[queue]
        base, cnt = self.dbase[queue]
        i = base + self.drr[queue] % cnt
        self.drr[queue] += 1
        extra = [(("d", i), self.dtot[i])] if self.dtot[i] > 0 else None
        self._waits(Q, reads, writes, extra)
        ins = Q.h.dma_start(out=out, in_=in_)
        ins.then_inc(self.dsem[i], 16)
        self.dtot[i] += 16
        self._mark((("d", i), self.dtot[i]), reads, writes)
        return ins

    def dma_multi(self, queue, pairs, reads=(), writes=()):
        Q = self.E[queue]
        base, cnt = self.dbase[queue]
        i = base + self.drr[queue] % cnt
        self.drr[queue] += 1
        extra = [(("d", i), self.dtot[i])] if self.dtot[i] > 0 else None
        self._waits(Q, reads, writes, extra)
        for out, in_ in pairs:
            ins = Q.h.dma_start(out=out, in_=in_)
            ins.then_inc(self.dsem[i], 16)
            self.dtot[i] += 16
        self._mark((("d", i), self.dtot[i]), reads, writes)

    def wait_all(self, eng, tiles):
        E = self.E[eng]
        self._waits(E, tiles, ())


def _fm(v):
    return np.ascontiguousarray(v.reshape(KC, 128).T)


def _consts(S):
    c = np.zeros((128, 128 + 512 + 512 + 128), np.float32)
    c[:, 0:128] = np.eye(128, dtype=np.float32)
    p = np.arange(128)[:, None]
    f = np.arange(128)[None, :]
    cur = np.where(p <= f, 0.0, -30000.0).astype(np.float32)
    prev = np.where(p > f, 0.0, -30000.0).astype(np.float32)
    c[:, 128:640] = np.tile(cur, (1, 4))
    c[:, 640:1152] = np.tile(prev, (1, 4))
    pm = np.zeros((128, 128), np.float32)
    for m in range(128):
        j = m % 64
        if j < 8:
            pm[m + 8, m] = 1.0
        elif j < 16:
            pm[m - 8, m] = 1.0
    c[:, 1152:1280] = pm
    pos = np.arange(S, dtype=np.float32)
    inv = (np.float32(ROPE_THETA) ** (-(np.arange(0, 16, 2, dtype=np.float32)) / np.float32(16))).astype(np.float32)
    ang = (pos[None, :] * inv[:, None]).astype(np.float32)
    cs = np.cos(ang.astype(np.float64)).astype(np.float32)
    sn = np.sin(ang.astype(np.float64)).astype(np.float32)
    C = np.ones((128, S), np.float32)
    Sg = np.zeros((128, S), np.float32)
    for m in range(128):
        j = m % 64
        if j < 8:
            C[m] = cs[j]
            Sg[m] = -sn[j]
        elif j < 16:
            C[m] = cs[j - 8]
            Sg[m] = sn[j - 8]
    return c, np.stack([C, Sg], 0)


ARENA_BYTES = 77 * 1024


def build_program(S=2048, DEPTH=4, TS=1024, subs=("mixer", "cross", "ffn"), self_sync=True, mix_stage=7):
    NSEG = S // TS
    NT = TS // 128
    NG = TS // 512
    NTT = S // 128
    nc = bass.Bass("TRN2", target_bir_lowering=False)
    L = DEPTH

    def din(name, shape):
        return nc.dram_tensor(name, list(shape), F32, kind="ExternalInput").ap()

    x_d = din("x", [S, D])
    mem_d = din("mem", [MEM, D])
    w_in_d = din("w_in", [L, D, IN_COLS])
    wk2_d = din("wk2", [L, D, 256])
    w_rg_d = din("w_rg", [L, 4, 256, 256])
    w_ig_d = din("w_ig", [L, 4, 256, 256])
    w_brr_d = din("w_br_rnn", [L, D, D])
    w_bra_d = din("w_br_attn", [L, D, D])
    w_out_d = din("w_out", [L, D, D])
    cq_d = din("cq_w", [L, D, D])
    ckv_d = din("ckv_w", [L, D, 2 * D])
    co_d = din("co_w", [L, D, D])
    wi_d = din("ffn_wi", [L, D, 2 * DFF])
    wo_d = din("ffn_wo", [L, DFF, D])
    vecs_d = din("vecs", [L, 128, NV])
    lnrow_d = din("lnrow", [L, 6, D])
    sinkcol_d = din("sinkcol", [L, 128, 8])
    consts_d = din("consts", [128, 1280])
    rope_d = din("rope", [2, 128, S])
    y_d = nc.dram_tensor("y", [S, D], F32, kind="ExternalOutput").ap()

    with ExitStack() as st:
        mk = MK(nc, st, self_sync=self_sync)

        def sb(name, shape, dt):
            return st.enter_context(nc.sbuf_tensor("sb_" + name, list(shape), dt))

        H = sb("H", [128, NTT, D], F32)
        HT = sb("HT", [128, KC, TS], BF16)
        ident = sb("ident", [128, 128], BF16)
        maskc = sb("maskc", [128, 512], BF16)
        maskp = sb("maskp", [128, 512], BF16)
        pmT = sb("pmT", [128, 128], BF16)
        ones = sb("ones", [128, 128], BF16)
        onesA = sb("onesA", [128, 128], BF16)
        onesB = sb("onesB", [128, 128], BF16)
        vecs = sb("vecs", [128, NV], F32)
        lv = sb("lv", [128, 64], F32)
        memT = sb("memT", [128, KC, MEM], BF16)
        KCT = sb("KCT", [128, KC, MEM], BF16)
        VC = sb("VC", [128, 2, D], BF16)
        wring = sb("wring", [128, NWSLOT, WSLOT], BF16)
        small = sb("small", [128, 64], F32)
        epsc = sb("epsc", [128, 4], F32)
        sinkt = sb("sinkt", [128, 8], F32)
        CONVST = sb("convst", [128, KC, 4], BF16)
        SCANST = sb("scanst", [128, KC], F32)
        KPREV = sb("kprev", [128, 2, 2, 128], BF16)
        VPREV = sb("vprev", [128, 2, 2, 128], BF16)
        ARENA = sb("arena", [128, ARENA_BYTES // 2], BF16)
        PS = st.enter_context(nc.psum_tensor("PS", [128, 8, 512], F32))

        def av(off, shape, dt):
            n = 1
            for d_ in shape:
                n *= d_
            nb = n * (4 if dt == F32 else 2)
            assert off % 4 == 0 and off + nb <= ARENA_BYTES, (off, nb)
            a = ARENA[:, off // 2:(off + nb) // 2]
            if dt == F32:
                a = a.bitcast(F32)
            if len(shape) == 2:
                a = a.rearrange("p (a b) -> p a b", a=shape[0])
            elif len(shape) == 3:
                a = a.rearrange("p (a b c) -> p a b c", a=shape[0], b=shape[1])
            return a

        t_H = mk.tiles(NTT)
        t_HT = mk.tiles(KC, NG)
        t_const = mk.tile()
        t_vecs = mk.tile()
        t_lv = mk.tile()
        t_memT = mk.tile()
        t_KCT = mk.tiles(KC)
        t_VC = mk.tiles(2)
        t_w = mk.tiles(NWSLOT)
        t_ps = [Tile(excl=True) for _ in range(8)]
        t_small = mk.tile()
        t_small4 = mk.tiles(4)
        t_convst = mk.tiles(KC)
        t_scanst = mk.tiles(KC)
        t_kprev = mk.tiles(2)
        t_vprev = mk.tiles(2)
        ps_rr = [0]
        w_rr = [0]

        def psum(n=1):
            i = ps_rr[0] % 8
            if n == 2 and i % 2 == 1:
                i = (i + 1) % 8
            ps_rr[0] = i + n
            return list(range(i, i + n))

        def wslot():
            i = w_rr[0] % NWSLOT
            w_rr[0] += 1
            return i

        def wload(slot, pairs):
            mk.dma_multi("pool", pairs, writes=[t_w[slot]])

        def wview(slot, k, n, off=0):
            return wring[:, slot, off:off + k * n].rearrange("p (k n) -> p k n", k=k)

        def wsrc(dram2d, c0, n, r0=0, k=KC):
            return dram2d[r0:r0 + k * 128, c0:c0 + n].rearrange("(k p) n -> p k n", p=128)

        def mm(bk, lhsT, rhs, start, stop, reads, cols=None):
            out = PS[:, bk, :] if cols is None else PS[:, bk, cols[0]:cols[1]]
            mk.op("pe", lambda e: e.matmul(out, lhsT=lhsT, rhs=rhs, start=start, stop=stop),
                  reads=reads, writes=[t_ps[bk]])

        def act(out, in_, func, reads, writes, bias=None, scale=None):
            kw = {}
            if bias is not None:
                kw["bias"] = bias
            if scale is not None:
                kw["scale"] = scale
            mk.op("act", lambda e: e.activation(out=out, in_=in_, func=func, **kw), reads=reads, writes=writes)

        def tt(out, in0, in1, op, reads, writes, eng="dve"):
            mk.op(eng, lambda e: e.tensor_tensor(out=out, in0=in0, in1=in1, op=op), reads=reads, writes=writes)

        def stt(out, in0, scalar, in1, op0, op1, reads, writes):
            mk.op("dve", lambda e: e.scalar_tensor_tensor(out=out, in0=in0, scalar=scalar, in1=in1, op0=op0, op1=op1),
                  reads=reads, writes=writes)

        def ts(out, in0, s1, s2, op0, op1, reads, writes):
            if s2 is None:
                mk.op("dve", lambda e: e.tensor_scalar(out=out, in0=in0, scalar1=s1, scalar2=None, op0=op0),
                      reads=reads, writes=writes)
            else:
                mk.op("dve", lambda e: e.tensor_scalar(out=out, in0=in0, scalar1=s1, scalar2=s2, op0=op0, op1=op1),
                      reads=reads, writes=writes)

        def cp(out, in_, reads, writes, eng="dve"):
            mk.op(eng, lambda e: e.tensor_copy(out=out, in_=in_), reads=reads, writes=writes)

        def memset(ap, val, writes, eng="dve"):
            mk.op(eng, lambda e: e.memset(ap, val), writes=writes)

        mk.dma_multi("pool", [(ident[:, :], consts_d[:, 0:128]), (maskc[:, :], consts_d[:, 128:640]),
                              (maskp[:, :], consts_d[:, 640:1152]), (pmT[:, :], consts_d[:, 1152:1280])],
                     writes=[t_const])
        memset(ones[:, :], 1.0, [t_const])
        memset(onesA[:, :], 0.0, [t_const])
        memset(onesB[:, :], 0.0, [t_const])
        memset(onesA[:, 0:64], 1.0, [t_const])
        memset(onesB[:, 64:128], 1.0, [t_const])
        memset(epsc[:, 0:1], LN_EPS, [t_const])
        memset(epsc[:, 1:2], 1.0, [t_const])
        memset(epsc[:, 2:3], math.log(0.5), [t_const])

        xv = x_d.rearrange("(t p) d -> p t d", p=128)
        for t0 in range(0, NTT, 4):
            mk.dma("sp", H[:, t0:t0 + 4, :], xv[:, t0:t0 + 4, :], writes=t_H[t0:t0 + 4])

        def transpose_into(src_of, n_tiles, dst_of, t_src_of, t_dst_of, hb_off, groups=None, t_hb=None, all_act=False):
            hb = av(hb_off, [2, D], BF16)
            if t_hb is None:
                t_hb = mk.tiles(2)
            for g0 in range(0, n_tiles, 4):
                if groups is not None and g0 // 4 not in groups:
                    continue
                nq = min(4, n_tiles - g0)
                banks = []
                for q in range(nq):
                    ti = g0 + q
                    b = q % 2
                    act(hb[:, b, :], src_of(ti), AF.Copy, [t_src_of(ti)], [t_hb[b]])
                    for c in range(KC):
                        if q == 0:
                            banks.append(psum()[0])
                        mm(banks[c], hb[:, b, c * 128:(c + 1) * 128], ident[:, :], True, True, [t_hb[b], t_const],
                           cols=(q * 128, (q + 1) * 128))
                for c in range(KC):
                    bk = banks[c]
                    dst, tdst = dst_of(c, g0, nq), t_dst_of(c, g0)
                    if c % 2 == 0 or all_act:
                        act(dst, PS[:, bk, 0:nq * 128], AF.Copy, [t_ps[bk]], [tdst])
                    else:
                        cp(dst, PS[:, bk, 0:nq * 128], [t_ps[bk]], [tdst])

        def make_HT(seg, hb_off, groups=None, t_hb=None, all_act=False):
            transpose_into(lambda ti: H[:, seg * NT + ti, :], NT,
                           lambda c, g0, nq: HT[:, c, g0 * 128:(g0 + nq) * 128],
                           lambda ti: t_H[seg * NT + ti], lambda c, g0: t_HT[c][g0 // 4], hb_off, groups, t_hb, all_act)

        def ln_parts(gt, lng, lnb, t_lng, t_lnb):
            so = (gt % 4) * 16
            tsm = t_small4[gt % 4]

            def A():
                for hf in range(2):
                    mk.op("dve", lambda e, hf=hf: e.bn_stats(out=small[:, so + hf * 6:so + hf * 6 + 6],
                                                             in_=H[:, gt, hf * 512:(hf + 1) * 512]),
                          reads=[t_H[gt]], writes=[tsm])
                mk.op("dve", lambda e: e.bn_aggr(out=small[:, so + 12:so + 14], in_=small[:, so:so + 12]),
                      reads=[tsm], writes=[tsm])
                act(small[:, so + 14:so + 15], small[:, so + 13:so + 14], AF.Ln, [tsm, t_const], [tsm],
                    bias=epsc[:, 0:1], scale=1.0)
                act(small[:, so + 14:so + 15], small[:, so + 14:so + 15], AF.Exp, [tsm], [tsm], scale=-0.5)

            def B1():
                stt(small[:, so + 15:so + 16], small[:, so + 12:so + 13], -1.0, small[:, so + 14:so + 15], ALU.mult, ALU.mult,
                    [tsm], [tsm])
                act(H[:, gt, :], H[:, gt, :], AF.Identity, [tsm, t_H[gt]], [t_H[gt]],
                    bias=small[:, so + 15:so + 16], scale=small[:, so + 14:so + 15])

            def B2():
                tt(H[:, gt, :], H[:, gt, :], lng, ALU.mult, [t_lng, t_H[gt]], [t_H[gt]], eng=LN_G_ENG)
                tt(H[:, gt, :], H[:, gt, :], lnb, ALU.add, [t_lnb, t_H[gt]], [t_H[gt]])
            return A, B1, B2

        def out_proj_ln(l, seg, w_d, srcT, t_srcT, nk, ln_idx, ln_off, rebuild=True):
            mk.mark('outproj')
            lng = av(ln_off, [D], F32)
            lnb = av(ln_off + 4096, [D], F32)
            t_lng, t_lnb = mk.tile(), mk.tile()
            mk.dma("sp", lng, lnrow_d[l, ln_idx, :].partition_broadcast(128), writes=[t_lng])
            mk.dma("sp", lnb, lnrow_d[l, ln_idx + 1, :].partition_broadcast(128), writes=[t_lnb])
            hbt = mk.tiles(2)
            pend = None
            stages = []

            def ln_issue(gt):
                A, B1, B2 = ln_parts(gt, lng, lnb, t_lng, t_lnb)
                A()
                stages.append((B1, B2))
                n = len(stages)
                if n >= 2:
                    stages[n - 2][0]()
                if n >= 3:
                    stages[n - 3][1]()

            def ln_flush():
                n = len(stages)
                if n >= 1:
                    stages[n - 1][0]()
                if n >= 2:
                    stages[n - 2][1]()
                if n >= 1:
                    stages[n - 1][1]()

            for g in range(NG):
                for hf in range(2):
                    sl = []
                    for k0 in range(0, nk, 8):
                        kk = min(8, nk - k0)
                        s_ = wslot()
                        wload(s_, [(wview(s_, kk, 512), wsrc(w_d, hf * 512, 512, r0=k0 * 128, k=kk))])
                        sl.append((s_, k0, kk))
                    for q in range(4):
                        tk = g * 4 + q
                        gt = seg * NT + tk
                        bk = psum()[0]
                        for (s_, k0, kk) in sl:
                            for k in range(kk):
                                kg = k0 + k
                                mm(bk, srcT[:, kg, tk * 128:(tk + 1) * 128], wview(s_, kk, 512)[:, k, :],
                                   kg == 0, kg == nk - 1, [t_srcT[kg][g], t_w[s_]])
                        stt(H[:, gt, hf * 512:(hf + 1) * 512], H[:, gt, hf * 512:(hf + 1) * 512], ALPHA, PS[:, bk, :],
                            ALU.mult, ALU.add, [t_ps[bk], t_H[gt]], [t_H[gt]])
                        if hf == 1:
                            ln_issue(gt)
                if rebuild:
                    if pend is not None:
                        pend()
                    pend = (lambda g=g: make_HT(seg, ln_off + 8192, groups=[g], t_hb=hbt, all_act=True))
            ln_flush()
            if pend is not None:
                pend()

        def ffn(l, seg):
            mk.mark('ffn')
            actT = av(0, [NFC, TS], BF16)
            sg = av(45056, [2, 512], F32)
            t_act = mk.tiles(NFC, NG)
            t_sg = mk.tiles(2)
            r = 0
            for fb in range(6):
                ncol = 512 if fb < 5 else 256
                ncc = ncol // 128
                s_g = wslot()
                wload(s_g, [(wview(s_g, 8, ncol), wsrc(wi_d[l], fb * 512, ncol))])
                s_u = wslot()
                wload(s_u, [(wview(s_u, 8, ncol), wsrc(wi_d[l], DFF + fb * 512, ncol))])
                for cc in range(ncc):
                    c = fb * 4 + cc
                    for g in range(NG):
                        bg = psum()[0]
                        bu = psum()[0]
                        for k in range(KC):
                            mm(bg, wview(s_g, 8, ncol)[:, k, cc * 128:(cc + 1) * 128], HT[:, k, g * 512:(g + 1) * 512],
                               k == 0, k == KC - 1, [t_w[s_g], t_HT[k][g]])
                        for k in range(KC):
                            mm(bu, wview(s_u, 8, ncol)[:, k, cc * 128:(cc + 1) * 128], HT[:, k, g * 512:(g + 1) * 512],
                               k == 0, k == KC - 1, [t_w[s_u], t_HT[k][g]])
                        b = r % 2
                        r += 1
                        act(sg[:, b, :], PS[:, bg, :], AF.Silu, [t_ps[bg]], [t_sg[b]])
                        tt(actT[:, c, g * 512:(g + 1) * 512], PS[:, bu, :], sg[:, b, :], ALU.mult,
                           [t_ps[bu], t_sg[b]], [t_act[c][g]])
            nxt = l * NSEG + seg + 1
            if nxt < L * NSEG:
                make_HT(nxt % NSEG, 61440)
            out_proj_ln(l, seg, wo_d[l], actT, t_act, NFC, 4, 49152, rebuild=False)
            mk.set_fence()

        def prep_mem():
            mk.set_fence()
            mf = av(0, [2, D], F32)
            t_mf = mk.tiles(2)
            mk.dma("sp", mf, mem_d.rearrange("(t p) d -> p t d", p=128), writes=t_mf)
            transpose_into(lambda ti: mf[:, ti, :], 2, lambda c, g0, nq: memT[:, c, g0 * 128:(g0 + nq) * 128],
                           lambda ti: t_mf[ti], lambda c, g0: t_memT, 8192)
            mk.set_fence()

        def cross_kv(l):
            for ob in range(2):
                s_ = wslot()
                wload(s_, [(wview(s_, 8, 512), wsrc(ckv_d[l], ob * 512, 512))])
                for cc in range(4):
                    c = ob * 4 + cc
                    bk = psum()[0]
                    for k in range(KC):
                        mm(bk, wview(s_, 8, 512)[:, k, cc * 128:(cc + 1) * 128], memT[:, k, :], k == 0, k == KC - 1,
                           [t_w[s_], t_memT], cols=(0, MEM))
                    act(KCT[:, c, :], PS[:, bk, 0:MEM], AF.Copy, [t_ps[bk]], [t_KCT[c]])
            for hf in range(2):
                s_ = wslot()
                wload(s_, [(wview(s_, 8, 512), wsrc(ckv_d[l], D + hf * 512, 512))])
                for mt in range(2):
                    bk = psum()[0]
                    for k in range(KC):
                        mm(bk, memT[:, k, mt * 128:(mt + 1) * 128], wview(s_, 8, 512)[:, k, :], k == 0, k == KC - 1,
                           [t_w[s_], t_memT])
                    cp(VC[:, mt, hf * 512:(hf + 1) * 512], PS[:, bk, :], [t_ps[bk]], [t_VC[mt]])

        def cross(l, seg):
            mk.mark('cross')
            QCT = av(0, [KC, TS], BF16)
            OT = av(16384, [KC, TS], BF16)
            EE = av(32768, [4, 512], BF16)
            LG = av(36864, [2, 512], F32)
            t_Q = mk.tiles(KC, NG)
            t_O = mk.tiles(KC, NG)
            t_E = mk.tiles(4)
            t_LG = mk.tiles(2)
            for ob in range(2):
                s_ = wslot()
                wload(s_, [(wview(s_, 8, 512), wsrc(cq_d[l], ob * 512, 512))])
                for cc in range(4):
                    c = ob * 4 + cc
                    for g in range(NG):
                        bk = psum()[0]
                        for k in range(KC):
                            mm(bk, wview(s_, 8, 512)[:, k, cc * 128:(cc + 1) * 128], HT[:, k, g * 512:(g + 1) * 512],
                               k == 0, k == KC - 1, [t_w[s_], t_HT[k][g]])
                        if (cc + g) % 2 == 0:
                            act(QCT[:, c, g * 512:(g + 1) * 512], PS[:, bk, :], AF.Copy, [t_ps[bk]], [t_Q[c][g]])
                        else:
                            cp(QCT[:, c, g * 512:(g + 1) * 512], PS[:, bk, :], [t_ps[bk]], [t_Q[c][g]])
            r = 0
            for hh in range(4):
                for g in range(NG):
                    eb = (r % 2) * 2
                    r += 1
                    for mt in range(2):
                        bk = psum()[0]
                        for kc in range(2):
                            c = 2 * hh + kc
                            mm(bk, KCT[:, c, mt * 128:(mt + 1) * 128], QCT[:, c, g * 512:(g + 1) * 512], kc == 0, kc == 1,
                               [t_KCT[c], t_Q[c][g]])
                        act(EE[:, eb + mt, :], PS[:, bk, :], AF.Exp, [t_ps[bk]], [t_E[eb + mt]], scale=1.0 / 16.0)
                    bd = psum()[0]
                    for mt in range(2):
                        mm(bd, ones[:, :], EE[:, eb + mt, :], mt == 0, mt == 1, [t_const, t_E[eb + mt]])
                    bn = []
                    for kc in range(2):
                        c = 2 * hh + kc
                        bk = psum()[0]
                        bn.append(bk)
                        for mt in range(2):
                            mm(bk, VC[:, mt, c * 128:(c + 1) * 128], EE[:, eb + mt, :], mt == 0, mt == 1,
                               [t_VC[mt], t_E[eb + mt]])
                    lb = (r % 2)
                    act(LG[:, lb, :], PS[:, bd, :], AF.Ln, [t_ps[bd]], [t_LG[lb]])
                    act(LG[:, lb, :], LG[:, lb, :], AF.Exp, [t_LG[lb]], [t_LG[lb]], scale=-1.0)
                    for kc in range(2):
                        c = 2 * hh + kc
                        tt(OT[:, c, g * 512:(g + 1) * 512], PS[:, bn[kc], :], LG[:, lb, :], ALU.mult,
                           [t_ps[bn[kc]], t_LG[lb]], [t_O[c][g]])
            mk.set_fence()
            out_proj_ln(l, seg, co_d[l], OT, t_O, KC, 2, 65536, rebuild=True)


        class Bump:
            def __init__(self, base):
                self.o = base

            def __call__(self, n):
                n = (n + 31) // 32 * 32
                r = self.o
                self.o += n
                assert self.o <= ARENA_BYTES, self.o
                return r

        def layer_prep(l):
            mk.dma("sp", sinkt[:, :], sinkcol_d[l], writes=[t_small])
            act(lv[:, 32:40], vecs[:, V_LAM:V_LAM + 8], AF.Exp, [t_vecs], [t_lv], scale=-1.0)
            act(lv[:, 32:40], lv[:, 32:40], AF.Ln, [t_lv, t_const], [t_lv], bias=epsc[:, 1:2], scale=1.0)
            ts(lv[:, 0:8], lv[:, 32:40], -4.0, None, ALU.mult, None, [t_lv], [t_lv])
            ts(lv[:, 8:16], vecs[:, V_BRG:V_BRG + 8], 0.5, None, ALU.mult, None, [t_vecs, t_lv], [t_lv])
            ts(lv[:, 16:24], vecs[:, V_BIG:V_BIG + 8], 0.5, None, ALU.mult, None, [t_vecs, t_lv], [t_lv])
            act(lv[:, 24:32], sinkt[:, :], AF.Exp, [t_small, t_lv], [t_lv])
            for c in range(KC):
                memset(CONVST[:, c, :], 0.0, [t_convst[c]])
                memset(SCANST[:, c:c + 1], 0.0, [t_scanst[c]])
            for gk in range(2):
                memset(KPREV[:, gk, :, :], 0.0, [t_kprev[gk]])
                memset(VPREV[:, gk, :, :], 0.0, [t_vprev[gk]])

        def rnn_phase(l, seg, YR, t_YR):
            mk.mark('rnn')
            o = Bump(32768)
            o2 = Bump(16384)
            DW = [av(o(2048), [2, 4, 128], BF16) for _ in range(2)]
            XRb = av(o(2 * (TS + 4) * 2), [2, TS + 4], BF16)
            XCb = av(o(2048), [2, 512], BF16)

            def two(a, b):
                return [av(a(4096), [2, 512], F32), av(b(4096), [2, 512], F32)]
            XC32 = two(o, o2)
            GL = two(o, o2)
            Rt = two(o, o2)
            It = two(o, o2)
            Aa = two(o, o)
            Mq = two(o, o)
            assert o2.o <= 32768
            t_DW = mk.tiles(2, 2)
            t_XR = mk.tiles(2, NG)
            t_XRh = mk.tiles(2)
            t_XCb = mk.tiles(2)
            t_XC32, t_GL, t_Rt, t_It, t_Aa, t_Mq = (mk.tiles(2, 2) for _ in range(6))
            its = [(n, g) for n in range(4) for g in range(NG)]
            blk = {}

            def front(i):
                n, g = its[i]
                p = i % 2
                dw, tdw = DW[n % 2], t_DW[n % 2]
                if g == 0:
                    sA = wslot()
                    wA = wview(sA, 8, 512)
                    wload(sA, [(wA[:, :, 0:256], wsrc(w_in_d[l], n * 256, 256)),
                               (wA[:, :, 256:512], wsrc(w_in_d[l], 1024 + n * 256, 256))])
                    sB = wslot()
                    rgv = wview(sB, 2, 256)
                    igv = wview(sB, 2, 256, off=512)
                    wload(sB, [(rgv, w_rg_d[l, n].rearrange("(k p) n -> p k n", p=128)),
                               (igv, w_ig_d[l, n].rearrange("(k p) n -> p k n", p=128))])
                    blk[n] = (sA, wA, sB, rgv, igv)
                    for cc in range(2):
                        c = 2 * n + cc
                        for tap in range(4):
                            ts(dw[:, cc, tap, :], ident[:, :], vecs[:, V_CONVW + tap * 8 + c:V_CONVW + tap * 8 + c + 1], None,
                               ALU.mult, None, [t_const, t_vecs], [tdw[cc]])
                        cp(XRb[:, cc, 0:3], CONVST[:, c, 0:3], [t_convst[c]], [t_XRh[cc]])
                sA, wA, sB, rgv, igv = blk[n]
                for cc in range(2):
                    bx = psum()[0]
                    for k in range(KC):
                        mm(bx, wA[:, k, cc * 128:(cc + 1) * 128], HT[:, k, g * 512:(g + 1) * 512], k == 0, k == KC - 1,
                           [t_w[sA], t_HT[k][g]])
                    cp(XRb[:, cc, 3 + g * 512:3 + (g + 1) * 512], PS[:, bx, :], [t_ps[bx]], [t_XR[cc][g]])
                    bg = psum()[0]
                    for k in range(KC):
                        mm(bg, wA[:, k, 256 + cc * 128:256 + (cc + 1) * 128], HT[:, k, g * 512:(g + 1) * 512], k == 0,
                           k == KC - 1, [t_w[sA], t_HT[k][g]])
                    act(GL[p][:, cc, :], PS[:, bg, :], AF.Gelu_apprx_tanh, [t_ps[bg]], [t_GL[p][cc]])
                for cc in range(2):
                    c = 2 * n + cc
                    bc = psum()[0]
                    rd = [tdw[cc], t_XR[cc][g], t_XR[cc][g - 1] if g > 0 else t_XRh[cc]]
                    for tap in range(4):
                        mm(bc, dw[:, cc, tap, :], XRb[:, cc, g * 512 + tap:g * 512 + tap + 512], tap == 0, tap == 3, rd)
                    act(XCb[:, cc, :], PS[:, bc, :], AF.Identity, [t_ps[bc], t_vecs], [t_XCb[cc]],
                        bias=vecs[:, V_CONVB + c:V_CONVB + c + 1], scale=1.0)
                    ts(XC32[p][:, cc, :], PS[:, bc, :], vecs[:, V_CONVB + c:V_CONVB + c + 1], None, ALU.add, None,
                       [t_ps[bc], t_vecs], [t_XC32[p][cc]])
                brs, bis = [], []
                for co in range(2):
                    br = psum()[0]
                    for kc in range(2):
                        mm(br, rgv[:, kc, co * 128:(co + 1) * 128], XCb[:, kc, :], kc == 0, kc == 1, [t_w[sB], t_XCb[kc]])
                    bi = psum()[0]
                    for kc in range(2):
                        mm(bi, igv[:, kc, co * 128:(co + 1) * 128], XCb[:, kc, :], kc == 0, kc == 1, [t_w[sB], t_XCb[kc]])
                    brs.append(br)
                    bis.append(bi)
                for co in range(2):
                    c = 2 * n + co
                    act(Rt[p][:, co, :], PS[:, brs[co], :], AF.Tanh, [t_ps[brs[co]], t_lv], [t_Rt[p][co]],
                        bias=lv[:, 8 + c:9 + c], scale=0.5)
                    act(It[p][:, co, :], PS[:, bis[co], :], AF.Tanh, [t_ps[bis[co]], t_lv], [t_It[p][co]],
                        bias=lv[:, 16 + c:17 + c], scale=0.5)
                for co in range(2):
                    c = 2 * n + co
                    act(Aa[p][:, co, :], Rt[p][:, co, :], AF.Exp, [t_Rt[p][co], t_lv], [t_Aa[p][co]], bias=lv[:, c:c + 1],
                        scale=lv[:, c:c + 1])
                for co in range(2):
                    act(Mq[p][:, co, :], Aa[p][:, co, :], AF.Square, [t_Aa[p][co]], [t_Mq[p][co]])
                for co in range(2):
                    act(Mq[p][:, co, :], Mq[p][:, co, :], AF.Ln, [t_Mq[p][co], t_const], [t_Mq[p][co]], bias=epsc[:, 1:2], scale=-1.0)
                for co in range(2):
                    act(Mq[p][:, co, :], Mq[p][:, co, :], AF.Exp, [t_Mq[p][co], t_const], [t_Mq[p][co]], bias=epsc[:, 2:3], scale=0.5)
                if g == NG - 1:
                    for cc in range(2):
                        c = 2 * n + cc
                        cp(CONVST[:, c, 0:3], XRb[:, cc, TS:TS + 3], [t_XR[cc][NG - 1]], [t_convst[c]])

            def tail(i):
                n, g = its[i]
                p = i % 2
                for co in range(2):
                    c = 2 * n + co
                    stt(It[p][:, co, :], It[p][:, co, :], 1.0, XC32[p][:, co, :], ALU.add, ALU.mult,
                        [t_It[p][co], t_XC32[p][co]], [t_It[p][co]])
                    tt(It[p][:, co, :], It[p][:, co, :], Mq[p][:, co, :], ALU.mult, [t_It[p][co], t_Mq[p][co]], [t_It[p][co]])
                    mk.op("dve", lambda e, co=co, c=c: e.tensor_tensor_scan(
                        out=XC32[p][:, co, :], data0=Aa[p][:, co, :], data1=It[p][:, co, :], initial=SCANST[:, c:c + 1],
                        op0=ALU.mult, op1=ALU.add), reads=[t_Aa[p][co], t_It[p][co], t_scanst[c]], writes=[t_XC32[p][co]])
                    cp(SCANST[:, c:c + 1], XC32[p][:, co, 511:512], [t_XC32[p][co]], [t_scanst[c]])
                    tt(YR[:, c, g * 512:(g + 1) * 512], XC32[p][:, co, :], GL[p][:, co, :], ALU.mult,
                       [t_XC32[p][co], t_GL[p][co]], [t_YR[c][g]])

            front(0)
            for i in range(1, len(its)):
                front(i)
                tail(i - 1)
            tail(len(its) - 1)

        def att_phase(l, seg, YA, t_YA):
            mk.mark('att')
            o = Bump(32768)
            ROPE = av(o(8192), [2, TS], F32)
            KA = av(o(2 * (128 + TS)), [128 + TS], BF16)
            KB = av(o(2 * (128 + TS)), [128 + TS], BF16)
            VA = av(o(2 * (1 + NT) * 256), [1 + NT, 2, 128], BF16)
            VB = av(o(2 * (1 + NT) * 256), [1 + NT, 2, 128], BF16)
            QT = av(o(8 * TS), [NT, 4, 128], BF16)
            QRAW = av(o(1024), [512], BF16)
            T1 = av(o(2048), [512], F32)
            T2 = av(o(2048), [512], F32)
            EE2 = [av(o(4096), [2, 2, 512], BF16), av(o(4096), [2, 2, 512], BF16)]
            LG = av(o(2048), [512], F32)
            t_rope, t_KA, t_KB, t_VA, t_VB, t_QRAW, t_T1, t_T2, t_LG = (mk.tile() for _ in range(9))
            t_QT = mk.tiles(NT)
            t_E2 = mk.tiles(2, 2)
            mk.dma("sp", ROPE, rope_d[:, :, seg * TS:(seg + 1) * TS].rearrange("a p s -> p a s"), writes=[t_rope])

            rp = [0]
            T12 = [T1, T2]
            t_T12 = [t_T1, t_T2]

            def roped(bk, dsts, g, rd_extra):
                p = rp[0] % 2
                rp[0] += 1
                Tp, t_Tp = T12[p], t_T12[p]
                act(QRAW, PS[:, bk, :], AF.Copy, [t_ps[bk]], [t_QRAW])
                bs = psum()[0]
                mm(bs, pmT[:, :], QRAW, True, True, [t_const, t_QRAW])
                tt(Tp, PS[:, bk, :], ROPE[:, 0, g * 512:(g + 1) * 512], ALU.mult, [t_ps[bk], t_rope], [t_Tp])
                tt(PS[:, bs, :], PS[:, bs, :], ROPE[:, 1, g * 512:(g + 1) * 512], ALU.mult, [t_ps[bs], t_rope], [t_ps[bs]])
                for (out_ap, p0, p1, shp, wr) in dsts:
                    i0, i1 = PS[p0:p1, bs, :], Tp[p0:p1, :]
                    if shp is not None:
                        i0 = i0.rearrange("p (a b) -> p a b", a=shp)
                        i1 = i1.rearrange("p (a b) -> p a b", a=shp)
                    tt(out_ap, i0, i1, ALU.add, [t_ps[bs], t_Tp], wr)

            memset(VA[:, :, :, :], 0.0, [t_VA])
            memset(VB[:, :, :, :], 0.0, [t_VB])
            for gk in range(2):
                memset(KA[64:128, :], 0.0, [t_KA])
                memset(KB[0:64, :], 0.0, [t_KB])
                cp(KA[0:64, 0:128], KPREV[0:64, gk, 0, :], [t_kprev[gk]], [t_KA])
                cp(KB[64:128, 0:128], KPREV[64:128, gk, 1, :], [t_kprev[gk]], [t_KB])
                sK = wslot()
                wK = wview(sK, 8, 128)
                wV = wview(sK, 8, 128, off=1024)
                pairs = [(wK, wsrc(wk2_d[l], gk * 128, 128))]
                if gk == 0:
                    pairs.append((wV, wsrc(w_in_d[l], 3200, 128)))
                wload(sK, pairs)
                for g in range(NG):
                    bk = psum()[0]
                    for k in range(KC):
                        mm(bk, wK[:, k, :], HT[:, k, g * 512:(g + 1) * 512], k == 0, k == KC - 1, [t_w[sK], t_HT[k][g]])
                    c0 = 128 + g * 512
                    roped(bk, [(KA[0:64, c0:c0 + 512], 0, 64, None, [t_KA]), (KB[64:128, c0:c0 + 512], 64, 128, None, [t_KB])], g, None)
                if _KCUT < 12:
                    continue
                if gk == 0:
                    for i in range(2):
                        cp(VA[:, 0, i, :], VPREV[:, i, 0, :], [t_vprev[i]], [t_VA])
                        cp(VB[:, 0, i, :], VPREV[:, i, 1, :], [t_vprev[i]], [t_VB])
                    for tk in range(NT):
                        bv = psum()[0]
                        for k in range(KC):
                            mm(bv, HT[:, k, tk * 128:(tk + 1) * 128], wV[:, k, :], k == 0, k == KC - 1,
                               [t_w[sK], t_HT[k][tk // 4]], cols=(0, 128))
                        src = PS[:, bv, 0:128].rearrange("p (g d) -> p g d", g=2)
                        act(VA[:, 1 + tk, :, 0:64], src, AF.Copy, [t_ps[bv]], [t_VA])
                        cp(VB[:, 1 + tk, :, 64:128], src, [t_ps[bv]], [t_VB])
                if _KCUT < 13:
                    continue
                sQ = wslot()
                wQ = wview(sQ, 8, 512)
                wload(sQ, [(wQ, wsrc(w_in_d[l], 2048 + gk * 512, 512))])
                for cc in range(4):
                    for g in range(NG):
                        bq = psum()[0]
                        for k in range(KC):
                            mm(bq, wQ[:, k, cc * 128:(cc + 1) * 128], HT[:, k, g * 512:(g + 1) * 512], k == 0, k == KC - 1,
                               [t_w[sQ], t_HT[k][g]])
                        roped(bq, [(QT[:, g * 4:(g + 1) * 4, cc, :], 0, 128, 4, t_QT[g * 4:(g + 1) * 4])], g, None)
                def kblocks(qb):
                    gb = seg * NT + qb
                    kb = []
                    if gb > 0:
                        kb.append((0, qb * 128, qb, maskp))
                    kb.append((1, (qb + 1) * 128, qb + 1, maskc))
                    return kb

                def emit_scores(qb):
                    qrhs = QT[:, qb, :, :].rearrange("p a b -> p (a b)")
                    EE, t_E = EE2[qb % 2], t_E2[qb % 2]
                    for (slot, kcol, vt, mask) in kblocks(qb):
                        sc = [2 * slot, 2 * slot + 1]
                        for hf in range(2):
                            mm(sc[hf], ident[:, :], mask[:, :], True, False, [t_const])
                            kk_ = KA if hf == 0 else KB
                            mm(sc[hf], kk_[:, kcol:kcol + 128], qrhs, False, True, [t_KA if hf == 0 else t_KB, t_QT[qb]])
                        act(EE[:, slot, :, :], PS[:, sc[0]:sc[0] + 2, :], AF.Exp, [t_ps[sc[0]], t_ps[sc[1]]], [t_E[slot]], scale=0.125)

                def emit_nd(qb):
                    EE, t_E = EE2[qb % 2], t_E2[qb % 2]
                    kb = kblocks(qb)
                    bnum, bden = (4, 5) if qb % 2 == 0 else (6, 7)
                    nmm = 2 * len(kb)
                    i = 0
                    for (slot, kcol, vt, mask) in kb:
                        for hf in range(2):
                            vv = VA if hf == 0 else VB
                            mm(bnum, vv[:, vt, gk, :], EE[:, slot, hf, :], i == 0, i == nmm - 1,
                               [t_VA if hf == 0 else t_VB, t_E[slot]])
                            i += 1
                    i = 0
                    for (slot, kcol, vt, mask) in kb:
                        for hf in range(2):
                            mm(bden, (onesA if hf == 0 else onesB)[:, :], EE[:, slot, hf, :], i == 0, i == nmm - 1,
                               [t_const, t_E[slot]])
                            i += 1
                    for cc in range(4):
                        j = 24 + gk * 4 + cc
                        act(LG[:, cc * 128:(cc + 1) * 128], PS[:, bden, cc * 128:(cc + 1) * 128], AF.Ln, [t_ps[bden], t_lv], [t_LG],
                            bias=lv[:, j:j + 1], scale=1.0)
                    act(LG, LG, AF.Exp, [t_LG], [t_LG], scale=-1.0)
                    tt(YA[:, gk * 4:(gk + 1) * 4, qb * 128:(qb + 1) * 128], PS[:, bnum, :].rearrange("p (a b) -> p a b", a=4),
                       LG.rearrange("p (a b) -> p a b", a=4), ALU.mult, [t_ps[bnum], t_LG],
                       [t_YA[gk * 4 + cc][qb // 4] for cc in range(4)])

                emit_scores(0)
                for qb in range(1, NT):
                    emit_scores(qb)
                    emit_nd(qb - 1)
                emit_nd(NT - 1)
                cp(KPREV[0:64, gk, 0, :], KA[0:64, TS:TS + 128], [t_KA], [t_kprev[gk]])
                cp(KPREV[64:128, gk, 1, :], KB[64:128, TS:TS + 128], [t_KB], [t_kprev[gk]])
            for i in range(2):
                cp(VPREV[:, i, 0, :], VA[:, NT, i, :], [t_VA], [t_vprev[i]])
                cp(VPREV[:, i, 1, :], VB[:, NT, i, :], [t_VB], [t_vprev[i]])

        def merge_phase(l, seg, YR, t_YR, YA, t_YA, M, t_M):
            mk.mark('merge')
            TM = av(49152, [4, TS], F32)
            SG = av(65536, [2, 512], F32)
            V2 = av(69632, [512], F32)
            t_TM = mk.tiles(4, NG)
            t_SG = mk.tiles(2)
            t_V2 = mk.tile()
            r = 0
            for ob in range(2):
                for br in range(2):
                    Ysrc, t_Ysrc = (YR, t_YR) if br == 0 else (YA, t_YA)
                    w1 = w_brr_d[l] if br == 0 else w_bra_d[l]
                    gcol = (3328 if br == 0 else 4352) + ob * 512
                    s1 = wslot()
                    wload(s1, [(wview(s1, 8, 512), wsrc(w1, ob * 512, 512))])
                    s2 = wslot()
                    wload(s2, [(wview(s2, 8, 512), wsrc(w_in_d[l], gcol, 512))])
                    for cc in range(4):
                        c = ob * 4 + cc
                        for g in range(NG):
                            bz = psum()[0]
                            for k in range(KC):
                                mm(bz, wview(s1, 8, 512)[:, k, cc * 128:(cc + 1) * 128], Ysrc[:, k, g * 512:(g + 1) * 512],
                                   k == 0, k == KC - 1, [t_w[s1], t_Ysrc[k][g]])
                            bg = psum()[0]
                            for k in range(KC):
                                mm(bg, wview(s2, 8, 512)[:, k, cc * 128:(cc + 1) * 128], HT[:, k, g * 512:(g + 1) * 512],
                                   k == 0, k == KC - 1, [t_w[s2], t_HT[k][g]])
                            b = r % 2
                            r += 1
                            act(SG[:, b, :], PS[:, bg, :], AF.Sigmoid, [t_ps[bg]], [t_SG[b]])
                            if br == 0:
                                tt(TM[:, cc, g * 512:(g + 1) * 512], PS[:, bz, :], SG[:, b, :], ALU.mult,
                                   [t_ps[bz], t_SG[b]], [t_TM[cc][g]])
                            else:
                                tt(V2, PS[:, bz, :], SG[:, b, :], ALU.mult, [t_ps[bz], t_SG[b]], [t_V2])
                                tt(M[:, c, g * 512:(g + 1) * 512], V2, TM[:, cc, g * 512:(g + 1) * 512], ALU.add,
                                   [t_V2, t_TM[cc][g]], [t_M[c][g]])

        def mixer(l, seg):
            mk.set_fence()
            YR = av(0, [KC, TS], BF16)
            YA = av(16384, [KC, TS], BF16)
            t_YR = mk.tiles(KC, NG)
            t_YA = mk.tiles(KC, NG)
            if mix_stage & 1:
                rnn_phase(l, seg, YR, t_YR)
            mk.set_fence()
            if mix_stage & 2:
                att_phase(l, seg, YA, t_YA)
            mk.set_fence()
            if not (mix_stage & 4):
                return
            M = av(32768, [KC, TS], BF16)
            t_M = mk.tiles(KC, NG)
            merge_phase(l, seg, YR, t_YR, YA, t_YA, M, t_M)
            mk.set_fence()
            out_proj_ln(l, seg, w_out_d[l], M, t_M, KC, 0, 49152, rebuild=True)

        if "cross" in subs:
            prep_mem()
        for l in range(L):
            mk.dma("sp", vecs[:, :], vecs_d[l], writes=[t_vecs])
            if "cross" in subs:
                cross_kv(l)
            if "mixer" in subs:
                layer_prep(l)
            for seg in range(NSEG):
                mk.set_fence()
                if (l == 0 and seg == 0) or "ffn" not in subs:
                    make_HT(seg, 0)
                    mk.set_fence()
                if "mixer" in subs:
                    mixer(l, seg)
                if "cross" in subs:
                    cross(l, seg)
                if "ffn" in subs:
                    ffn(l, seg)

        yv = y_d.rearrange("(t p) d -> p t d", p=128)
        t_out = mk.tile()
        for t0 in range(0, NTT, 4):
            mk.dma("sp", yv[:, t0:t0 + 4, :], H[:, t0:t0 + 4, :], reads=t_H[t0:t0 + 4], writes=[t_out])
        sp = mk.E["sp"]
        for i in range(NDMA):
            if mk.dtot[i] > 0 and sp.waited.get(("d", i), 0) < mk.dtot[i]:
                sp.h.wait_ge(mk.dsem[i], mk.dtot[i])
    return nc, mk


def prep_inputs(inputs, S, L):
    f32 = np.float32
    g = {k: np.asarray(v, dtype=f32) for k, v in inputs.items() if k not in ("x", "mem")}
    w_in = g["w_in"][:L]
    kcols = w_in[:, :, 3072:3200]
    wk2 = np.concatenate([kcols[:, :, 0:64], kcols[:, :, 0:64], kcols[:, :, 64:128], kcols[:, :, 64:128]], axis=2)
    vecs = np.zeros((L, 128, NV), f32)
    for l in range(L):
        for tap in range(4):
            vecs[l, :, V_CONVW + tap * 8:V_CONVW + tap * 8 + 8] = _fm(g["conv_w"][l, tap])
        vecs[l, :, V_CONVB:V_CONVB + 8] = _fm(g["conv_b"][l])
        vecs[l, :, V_BRG:V_BRG + 8] = _fm(g["b_rg"][l])
        vecs[l, :, V_BIG:V_BIG + 8] = _fm(g["b_ig"][l])
        vecs[l, :, V_LAM:V_LAM + 8] = _fm(g["lru_lambda"][l])
    lnrow = np.stack([g["ln1_g"][:L], g["ln1_b"][:L], g["ln2_g"][:L], g["ln2_b"][:L], g["ln3_g"][:L], g["ln3_b"][:L]], axis=1)
    sinkcol = np.zeros((L, 128, 8), f32)
    for l in range(L):
        for j in range(8):
            sinkcol[l, 0:64, j] = g["sinks"][l, 2 * j]
            sinkcol[l, 64:128, j] = g["sinks"][l, 2 * j + 1]
    consts, rope = _consts(S)
    shared = {
        "w_in": np.ascontiguousarray(w_in), "wk2": np.ascontiguousarray(wk2),
        "w_rg": g["w_rg"][:L], "w_ig": g["w_ig"][:L], "w_br_rnn": g["w_br_rnn"][:L], "w_br_attn": g["w_br_attn"][:L],
        "w_out": g["w_out"][:L], "cq_w": g["cq_w"][:L], "ckv_w": g["ckv_w"][:L], "co_w": g["co_w"][:L],
        "ffn_wi": g["ffn_wi"][:L], "ffn_wo": g["ffn_wo"][:L], "vecs": vecs, "lnrow": np.ascontiguousarray(lnrow),
        "sinkcol": sinkcol, "consts": consts, "rope": rope,
    }
    return shared


_CACHE = {}


def kernel(**inputs):
    x = np.asarray(inputs["x"], dtype=np.float32)
    mem = np.asarray(inputs["mem"], dtype=np.float32)
    B, S, _ = x.shape
    L = inputs["w_in"].shape[0]
    key = (S, L)
    if key not in _CACHE:
        _CACHE[key] = build_program(S=S, DEPTH=L)[0]
    nc = _CACHE[key]
    shared = prep_inputs(inputs, S, L)
    in_maps = []
    for b in range(B):
        m = dict(shared)
        m["x"] = np.ascontiguousarray(x[b])
        m["mem"] = np.ascontiguousarray(mem[b])
        in_maps.append(m)
    res = run_bass_kernel_spmd(nc, in_maps, core_ids=list(range(B)))
    return np.stack([r["y"] for r in res.results], axis=0).astype(np.float32)
```

```python
import math
_KCUT = 99
LN_G_ENG = "dve"
from contextlib import ExitStack

import numpy as np
import concourse.bass as bass
import concourse.mybir as mybir
from concourse.bass_utils import run_bass_kernel_spmd

F32 = mybir.dt.float32
BF16 = mybir.dt.bfloat16
AF = mybir.ActivationFunctionType
ALU = mybir.AluOpType

D = 1024
KC = 8
MEM = 256
DFF = 2816
NFC = 22
IN_COLS = 5376
ALPHA = 8.0 ** 0.25
LN_EPS = 1e-5
ROPE_THETA = 500000.0
EPOCH = 16000
NEPOCH = 8
NDMA = 24
WSLOT = 4096
NWSLOT = 4
NV = 112

V_CONVW = 0
V_CONVB = 32
V_BRG = 40
V_BIG = 48
V_LAM = 56


class Tile:
    __slots__ = ("w", "rs", "excl")

    def __init__(self, fence=None, excl=False):
        self.w = None
        self.rs = dict(fence) if fence else {}
        self.excl = excl


class Eng:
    def __init__(self, name, h, key, sems, self_sync):
        self.name = name
        self.h = h
        self.key = key
        self.sems = sems
        self.seq = 0
        self.waited = {}
        self.self_sync = self_sync


class MK:
    def __init__(self, nc, st, self_sync=True):
        self.nc = nc
        self.E = {}
        for name, h, ss in (("pe", nc.tensor, False), ("act", nc.scalar, self_sync),
                            ("dve", nc.vector, self_sync), ("pool", nc.gpsimd, self_sync),
                            ("sp", nc.sync, False)):
            sems = [st.enter_context(nc.semaphore(f"s_{name}{i}")) for i in range(NEPOCH if name != "sp" else 1)]
            self.E[name] = Eng(name, h, ("e", name), sems, ss)
        self.dsem = [st.enter_context(nc.semaphore(f"s_dma{i}")) for i in range(NDMA)]
        self.dtot = [0] * NDMA
        self.drr = {"pool": 0, "sp": 0}
        self.dbase = {"pool": (0, NDMA // 2), "sp": (NDMA // 2, NDMA - NDMA // 2)}
        self.fence = {}
        self.nops = 0
        self.marks = []

    def mark(self, name):
        self.marks.append((name, self.E['pe'].seq))

    def tile(self):
        return Tile(self.fence)

    def tiles(self, *shape):
        if len(shape) == 1:
            return [self.tile() for _ in range(shape[0])]
        return [self.tiles(*shape[1:]) for _ in range(shape[0])]

    def set_fence(self):
        f = {}
        for e in self.E.values():
            if e.seq > 0:
                f[e.key] = e.seq
        self.fence = f

    def _sem(self, key, val):
        if key[0] == "e":
            e = self.E[key[1]]
            ep = (val - 1) // EPOCH
            return e.sems[ep], val - ep * EPOCH
        return self.dsem[key[1]], val

    def _waits(self, E, reads, writes, extra=None):
        need = {}
        for t in reads:
            if t.w is not None:
                k, v = t.w
                if need.get(k, 0) < v:
                    need[k] = v
            if t.excl:
                for k, v in t.rs.items():
                    if k != E.key and need.get(k, 0) < v:
                        need[k] = v
        for t in writes:
            if t.w is not None:
                k, v = t.w
                if need.get(k, 0) < v:
                    need[k] = v
            for k, v in t.rs.items():
                if need.get(k, 0) < v:
                    need[k] = v
        if extra:
            for k, v in extra:
                if need.get(k, 0) < v:
                    need[k] = v
        for k, v in need.items():
            if k == E.key and not E.self_sync:
                continue
            if E.waited.get(k, 0) >= v:
                continue
            E.waited[k] = v
            sem, sv = self._sem(k, v)
            E.h.wait_ge(sem, sv)

    def _mark(self, dep, reads, writes):
        k, v = dep
        for t in reads:
            if t.rs.get(k, 0) < v:
                t.rs[k] = v
        for t in writes:
            t.w = dep
            t.rs = {}

    def op(self, eng, fn, reads=(), writes=()):
        E = self.E[eng]
        self._waits(E, reads, writes)
        ins = fn(E.h)
        E.seq += 1
        ep = (E.seq - 1) // EPOCH
        assert ep < len(E.sems), f"too many instructions on {eng}"
        ins.then_inc(E.sems[ep], 1)
        self._mark((E.key, E.seq), reads, writes)
        self.nops += 1
        return ins

    def dma(self, queue, out, in_, reads=(), writes=()):
        Q = self.E[queue]
        base, cnt = self.dbase[queue]
        i = base + self.drr[queue] % cnt
        self.drr[queue] += 1
        extra = [(("d", i), self.dtot[i])] if self.dtot[i] > 0 else None
        self._waits(Q, reads, writes, extra)
        ins = Q.h.dma_start(out=out, in_=in_)
        ins.then_inc(self.dsem[i], 16)
        self.dtot[i] += 16
        self._mark((("d", i), self.dtot[i]), reads, writes)
        return ins

    def dma_multi(self, queue, pairs, reads=(), writes=()):
        Q = self.E[queue]
        base, cnt = self.dbase[queue]
        i = base + self.drr[queue] % cnt
        self.drr[queue] += 1
        extra = [(("d", i), self.dtot[i])] if self.dtot[i] > 0 else None
        self._waits(Q, reads, writes, extra)
        for out, in_ in pairs:
            ins = Q.h.dma_start(out=out, in_=in_)
            ins.then_inc(self.dsem[i], 16)
            self.dtot[i] += 16
        self._mark((("d", i), self.dtot[i]), reads, writes)

    def wait_all(self, eng, tiles):
        E = self.E[eng]
        self._waits(E, tiles, ())


def _fm(v):
    return np.ascontiguousarray(v.reshape(KC, 128).T)


def _consts(S):
    c = np.zeros((128, 128 + 512 + 512 + 128), np.float32)
    c[:, 0:128] = np.eye(128, dtype=np.float32)
    p = np.arange(128)[:, None]
    f = np.arange(128)[None, :]
    cur = np.where(p <= f, 0.0, -30000.0).astype(np.float32)
    prev = np.where(p > f, 0.0, -30000.0).astype(np.float32)
    c[:, 128:640] = np.tile(cur, (1, 4))
    c[:, 640:1152] = np.tile(prev, (1, 4))
    pm = np.zeros((128, 128), np.float32)
    for m in range(128):
        j = m % 64
        if j < 8:
            pm[m + 8, m] = 1.0
        elif j < 16:
            pm[m - 8, m] = 1.0
    c[:, 1152:1280] = pm
    pos = np.arange(S, dtype=np.float32)
    inv = (np.float32(ROPE_THETA) ** (-(np.arange(0, 16, 2, dtype=np.float32)) / np.float32(16))).astype(np.float32)
    ang = (pos[None, :] * inv[:, None]).astype(np.float32)
    cs = np.cos(ang.astype(np.float64)).astype(np.float32)
    sn = np.sin(ang.astype(np.float64)).astype(np.float32)
    C = np.ones((128, S), np.float32)
    Sg = np.zeros((128, S), np.float32)
    for m in range(128):
        j = m % 64
        if j < 8:
            C[m] = cs[j]
            Sg[m] = -sn[j]
        elif j < 16:
            C[m] = cs[j - 8]
            Sg[m] = sn[j - 8]
    return c, np.stack([C, Sg], 0)


ARENA_BYTES = 77 * 1024


def build_program(S=2048, DEPTH=4, TS=1024, subs=("mixer", "cross", "ffn"), self_sync=True, mix_stage=7):
    NSEG = S // TS
    NT = TS // 128
    NG = TS // 512
    NTT = S // 128
    nc = bass.Bass("TRN2", target_bir_lowering=False)
    L = DEPTH

    def din(name, shape):
        return nc.dram_tensor(name, list(shape), F32, kind="ExternalInput").ap()

    x_d = din("x", [S, D])
    mem_d = din("mem", [MEM, D])
    w_in_d = din("w_in", [L, D, IN_COLS])
    wk2_d = din("wk2", [L, D, 256])
    w_rg_d = din("w_rg", [L, 4, 256, 256])
    w_ig_d = din("w_ig", [L, 4, 256, 256])
    w_brr_d = din("w_br_rnn", [L, D, D])
    w_bra_d = din("w_br_attn", [L, D, D])
    w_out_d = din("w_out", [L, D, D])
    cq_d = din("cq_w", [L, D, D])
    ckv_d = din("ckv_w", [L, D, 2 * D])
    co_d = din("co_w", [L, D, D])
    wi_d = din("ffn_wi", [L, D, 2 * DFF])
    wo_d = din("ffn_wo", [L, DFF, D])
    vecs_d = din("vecs", [L, 128, NV])
    lnrow_d = din("lnrow", [L, 6, D])
    sinkcol_d = din("sinkcol", [L, 128, 8])
    consts_d = din("consts", [128, 1280])
    rope_d = din("rope", [2, 128, S])
    y_d = nc.dram_tensor("y", [S, D], F32, kind="ExternalOutput").ap()

    with ExitStack() as st:
        mk = MK(nc, st, self_sync=self_sync)

        def sb(name, shape, dt):
            return st.enter_context(nc.sbuf_tensor("sb_" + name, list(shape), dt))

        H = sb("H", [128, NTT, D], F32)
        HT = sb("HT", [128, KC, TS], BF16)
        ident = sb("ident", [128, 128], BF16)
        maskc = sb("maskc", [128, 512], BF16)
        maskp = sb("maskp", [128, 512], BF16)
        pmT = sb("pmT", [128, 128], BF16)
        ones = sb("ones", [128, 128], BF16)
        onesA = sb("onesA", [128, 128], BF16)
        onesB = sb("onesB", [128, 128], BF16)
        vecs = sb("vecs", [128, NV], F32)
        lv = sb("lv", [128, 64], F32)
        memT = sb("memT", [128, KC, MEM], BF16)
        KCT = sb("KCT", [128, KC, MEM], BF16)
        VC = sb("VC", [128, 2, D], BF16)
        wring = sb("wring", [128, NWSLOT, WSLOT], BF16)
        small = sb("small", [128, 64], F32)
        epsc = sb("epsc", [128, 4], F32)
        sinkt = sb("sinkt", [128, 8], F32)
        CONVST = sb("convst", [128, KC, 4], BF16)
        SCANST = sb("scanst", [128, KC], F32)
        KPREV = sb("kprev", [128, 2, 2, 128], BF16)
        VPREV = sb("vprev", [128, 2, 2, 128], BF16)
        ARENA = sb("arena", [128, ARENA_BYTES // 2], BF16)
        PS = st.enter_context(nc.psum_tensor("PS", [128, 8, 512], F32))

        def av(off, shape, dt):
            n = 1
            for d_ in shape:
                n *= d_
            nb = n * (4 if dt == F32 else 2)
            assert off % 4 == 0 and off + nb <= ARENA_BYTES, (off, nb)
            a = ARENA[:, off // 2:(off + nb) // 2]
            if dt == F32:
                a = a.bitcast(F32)
            if len(shape) == 2:
                a = a.rearrange("p (a b) -> p a b", a=shape[0])
            elif len(shape) == 3:
                a = a.rearrange("p (a b c) -> p a b c", a=shape[0], b=shape[1])
            return a

        t_H = mk.tiles(NTT)
        t_HT = mk.tiles(KC, NG)
        t_const = mk.tile()
        t_vecs = mk.tile()
        t_lv = mk.tile()
        t_memT = mk.tile()
        t_KCT = mk.tiles(KC)
        t_VC = mk.tiles(2)
        t_w = mk.tiles(NWSLOT)
        t_ps = [Tile(excl=True) for _ in range(8)]
        t_small = mk.tile()
        t_small4 = mk.tiles(4)
        t_convst = mk.tiles(KC)
        t_scanst = mk.tiles(KC)
        t_kprev = mk.tiles(2)
        t_vprev = mk.tiles(2)
        ps_rr = [0]
        w_rr = [0]

        def psum(n=1):
            i = ps_rr[0] % 8
            if n == 2 and i % 2 == 1:
                i = (i + 1) % 8
            ps_rr[0] = i + n
            return list(range(i, i + n))

        def wslot():
            i = w_rr[0] % NWSLOT
            w_rr[0] += 1
            return i

        def wload(slot, pairs):
            mk.dma_multi("pool", pairs, writes=[t_w[slot]])

        def wview(slot, k, n, off=0):
            return wring[:, slot, off:off + k * n].rearrange("p (k n) -> p k n", k=k)

        def wsrc(dram2d, c0, n, r0=0, k=KC):
            return dram2d[r0:r0 + k * 128, c0:c0 + n].rearrange("(k p) n -> p k n", p=128)

        def mm(bk, lhsT, rhs, start, stop, reads, cols=None):
            out = PS[:, bk, :] if cols is None else PS[:, bk, cols[0]:cols[1]]
            mk.op("pe", lambda e: e.matmul(out, lhsT=lhsT, rhs=rhs, start=start, stop=stop),
                  reads=reads, writes=[t_ps[bk]])

        def act(out, in_, func, reads, writes, bias=None, scale=None):
            kw = {}
            if bias is not None:
                kw["bias"] = bias
            if scale is not None:
                kw["scale"] = scale
            mk.op("act", lambda e: e.activation(out=out, in_=in_, func=func, **kw), reads=reads, writes=writes)

        def tt(out, in0, in1, op, reads, writes, eng="dve"):
            mk.op(eng, lambda e: e.tensor_tensor(out=out, in0=in0, in1=in1, op=op), reads=reads, writes=writes)

        def stt(out, in0, scalar, in1, op0, op1, reads, writes):
            mk.op("dve", lambda e: e.scalar_tensor_tensor(out=out, in0=in0, scalar=scalar, in1=in1, op0=op0, op1=op1),
                  reads=reads, writes=writes)

        def ts(out, in0, s1, s2, op0, op1, reads, writes):
            if s2 is None:
                mk.op("dve", lambda e: e.tensor_scalar(out=out, in0=in0, scalar1=s1, scalar2=None, op0=op0),
                      reads=reads, writes=writes)
            else:
                mk.op("dve", lambda e: e.tensor_scalar(out=out, in0=in0, scalar1=s1, scalar2=s2, op0=op0, op1=op1),
                      reads=reads, writes=writes)

        def cp(out, in_, reads, writes, eng="dve"):
            mk.op(eng, lambda e: e.tensor_copy(out=out, in_=in_), reads=reads, writes=writes)

        def memset(ap, val, writes, eng="dve"):
            mk.op(eng, lambda e: e.memset(ap, val), writes=writes)

        mk.dma_multi("pool", [(ident[:, :], consts_d[:, 0:128]), (maskc[:, :], consts_d[:, 128:640]),
                              (maskp[:, :], consts_d[:, 640:1152]), (pmT[:, :], consts_d[:, 1152:1280])],
                     writes=[t_const])
        memset(ones[:, :], 1.0, [t_const])
        memset(onesA[:, :], 0.0, [t_const])
        memset(onesB[:, :], 0.0, [t_const])
        memset(onesA[:, 0:64], 1.0, [t_const])
        memset(onesB[:, 64:128], 1.0, [t_const])
        memset(epsc[:, 0:1], LN_EPS, [t_const])
        memset(epsc[:, 1:2], 1.0, [t_const])
        memset(epsc[:, 2:3], math.log(0.5), [t_const])

        xv = x_d.rearrange("(t p) d -> p t d", p=128)
        for t0 in range(0, NTT, 4):
            mk.dma("sp", H[:, t0:t0 + 4, :], xv[:, t0:t0 + 4, :], writes=t_H[t0:t0 + 4])

        def transpose_into(src_of, n_tiles, dst_of, t_src_of, t_dst_of, hb_off, groups=None, t_hb=None, all_act=False):
            hb = av(hb_off, [2, D], BF16)
            if t_hb is None:
                t_hb = mk.tiles(2)
            for g0 in range(0, n_tiles, 4):
                if groups is not None and g0 // 4 not in groups:
                    continue
                nq = min(4, n_tiles - g0)
                banks = []
                for q in range(nq):
                    ti = g0 + q
                    b = q % 2
                    act(hb[:, b, :], src_of(ti), AF.Copy, [t_src_of(ti)], [t_hb[b]])
                    for c in range(KC):
                        if q == 0:
                            banks.append(psum()[0])
                        mm(banks[c], hb[:, b, c * 128:(c + 1) * 128], ident[:, :], True, True, [t_hb[b], t_const],
                           cols=(q * 128, (q + 1) * 128))
                for c in range(KC):
                    bk = banks[c]
                    dst, tdst = dst_of(c, g0, nq), t_dst_of(c, g0)
                    if c % 2 == 0 or all_act:
                        act(dst, PS[:, bk, 0:nq * 128], AF.Copy, [t_ps[bk]], [tdst])
                    else:
                        cp(dst, PS[:, bk, 0:nq * 128], [t_ps[bk]], [tdst])

        def make_HT(seg, hb_off, groups=None, t_hb=None, all_act=False):
            transpose_into(lambda ti: H[:, seg * NT + ti, :], NT,
                           lambda c, g0, nq: HT[:, c, g0 * 128:(g0 + nq) * 128],
                           lambda ti: t_H[seg * NT + ti], lambda c, g0: t_HT[c][g0 // 4], hb_off, groups, t_hb, all_act)

        def ln_parts(gt, lng, lnb, t_lng, t_lnb):
            so = (gt % 4) * 16
            tsm = t_small4[gt % 4]

            def A():
                for hf in range(2):
                    mk.op("dve", lambda e, hf=hf: e.bn_stats(out=small[:, so + hf * 6:so + hf * 6 + 6],
                                                             in_=H[:, gt, hf * 512:(hf + 1) * 512]),
                          reads=[t_H[gt]], writes=[tsm])
                mk.op("dve", lambda e: e.bn_aggr(out=small[:, so + 12:so + 14], in_=small[:, so:so + 12]),
                      reads=[tsm], writes=[tsm])
                act(small[:, so + 14:so + 15], small[:, so + 13:so + 14], AF.Ln, [tsm, t_const], [tsm],
                    bias=epsc[:, 0:1], scale=1.0)
                act(small[:, so + 14:so + 15], small[:, so + 14:so + 15], AF.Exp, [tsm], [tsm], scale=-0.5)

            def B1():
                stt(small[:, so + 15:so + 16], small[:, so + 12:so + 13], -1.0, small[:, so + 14:so + 15], ALU.mult, ALU.mult,
                    [tsm], [tsm])
                act(H[:, gt, :], H[:, gt, :], AF.Identity, [tsm, t_H[gt]], [t_H[gt]],
                    bias=small[:, so + 15:so + 16], scale=small[:, so + 14:so + 15])

            def B2():
                tt(H[:, gt, :], H[:, gt, :], lng, ALU.mult, [t_lng, t_H[gt]], [t_H[gt]], eng=LN_G_ENG)
                tt(H[:, gt, :], H[:, gt, :], lnb, ALU.add, [t_lnb, t_H[gt]], [t_H[gt]])
            return A, B1, B2

        def out_proj_ln(l, seg, w_d, srcT, t_srcT, nk, ln_idx, ln_off, rebuild=True):
            mk.mark('outproj')
            lng = av(ln_off, [D], F32)
            lnb = av(ln_off + 4096, [D], F32)
            t_lng, t_lnb = mk.tile(), mk.tile()
            mk.dma("sp", lng, lnrow_d[l, ln_idx, :].partition_broadcast(128), writes=[t_lng])
            mk.dma("sp", lnb, lnrow_d[l, ln_idx + 1, :].partition_broadcast(128), writes=[t_lnb])
            hbt = mk.tiles(2)
            pend = None
            stages = []

            def ln_issue(gt):
                A, B1, B2 = ln_parts(gt, lng, lnb, t_lng, t_lnb)
                A()
                stages.append((B1, B2))
                n = len(stages)
                if n >= 2:
                    stages[n - 2][0]()
                if n >= 3:
                    stages[n - 3][1]()

            def ln_flush():
                n = len(stages)
                if n >= 1:
                    stages[n - 1][0]()
                if n >= 2:
                    stages[n - 2][1]()
                if n >= 1:
                    stages[n - 1][1]()

            for g in range(NG):
                for hf in range(2):
                    sl = []
                    for k0 in range(0, nk, 8):
                        kk = min(8, nk - k0)
                        s_ = wslot()
                        wload(s_, [(wview(s_, kk, 512), wsrc(w_d, hf * 512, 512, r0=k0 * 128, k=kk))])
                        sl.append((s_, k0, kk))
                    for q in range(4):
                        tk = g * 4 + q
                        gt = seg * NT + tk
                        bk = psum()[0]
                        for (s_, k0, kk) in sl:
                            for k in range(kk):
                                kg = k0 + k
                                mm(bk, srcT[:, kg, tk * 128:(tk + 1) * 128], wview(s_, kk, 512)[:, k, :],
                                   kg == 0, kg == nk - 1, [t_srcT[kg][g], t_w[s_]])
                        stt(H[:, gt, hf * 512:(hf + 1) * 512], H[:, gt, hf * 512:(hf + 1) * 512], ALPHA, PS[:, bk, :],
                            ALU.mult, ALU.add, [t_ps[bk], t_H[gt]], [t_H[gt]])
                        if hf == 1:
                            ln_issue(gt)
                if rebuild:
                    if pend is not None:
                        pend()
                    pend = (lambda g=g: make_HT(seg, ln_off + 8192, groups=[g], t_hb=hbt, all_act=True))
            ln_flush()
            if pend is not None:
                pend()

        def ffn(l, seg):
            mk.mark('ffn')
            actT = av(0, [NFC, TS], BF16)
            sg = av(45056, [2, 512], F32)
            t_act = mk.tiles(NFC, NG)
            t_sg = mk.tiles(2)
            r = 0
            for fb in range(6):
                ncol = 512 if fb < 5 else 256
                ncc = ncol // 128
                s_g = wslot()
                wload(s_g, [(wview(s_g, 8, ncol), wsrc(wi_d[l], fb * 512, ncol))])
                s_u = wslot()
                wload(s_u, [(wview(s_u, 8, ncol), wsrc(wi_d[l], DFF + fb * 512, ncol))])
                for cc in range(ncc):
                    c = fb * 4 + cc
                    for g in range(NG):
                        bg = psum()[0]
                        bu = psum()[0]
                        for k in range(KC):
                            mm(bg, wview(s_g, 8, ncol)[:, k, cc * 128:(cc + 1) * 128], HT[:, k, g * 512:(g + 1) * 512],
                               k == 0, k == KC - 1, [t_w[s_g], t_HT[k][g]])
                        for k in range(KC):
                            mm(bu, wview(s_u, 8, ncol)[:, k, cc * 128:(cc + 1) * 128], HT[:, k, g * 512:(g + 1) * 512],
                               k == 0, k == KC - 1, [t_w[s_u], t_HT[k][g]])
                        b = r % 2
                        r += 1
                        act(sg[:, b, :], PS[:, bg, :], AF.Silu, [t_ps[bg]], [t_sg[b]])
                        tt(actT[:, c, g * 512:(g + 1) * 512], PS[:, bu, :], sg[:, b, :], ALU.mult,
                           [t_ps[bu], t_sg[b]], [t_act[c][g]])
            nxt = l * NSEG + seg + 1
            if nxt < L * NSEG:
                make_HT(nxt % NSEG, 61440)
            out_proj_ln(l, seg, wo_d[l], actT, t_act, NFC, 4, 49152, rebuild=False)
            mk.set_fence()

        def prep_mem():
            mk.set_fence()
            mf = av(0, [2, D], F32)
            t_mf = mk.tiles(2)
            mk.dma("sp", mf, mem_d.rearrange("(t p) d -> p t d", p=128), writes=t_mf)
            transpose_into(lambda ti: mf[:, ti, :], 2, lambda c, g0, nq: memT[:, c, g0 * 128:(g0 + nq) * 128],
                           lambda ti: t_mf[ti], lambda c, g0: t_memT, 8192)
            mk.set_fence()

        def cross_kv(l):
            for ob in range(2):
                s_ = wslot()
                wload(s_, [(wview(s_, 8, 512), wsrc(ckv_d[l], ob * 512, 512))])
                for cc in range(4):
                    c = ob * 4 + cc
                    bk = psum()[0]
                    for k in range(KC):
                        mm(bk, wview(s_, 8, 512)[:, k, cc * 128:(cc + 1) * 128], memT[:, k, :], k == 0, k == KC - 1,
                           [t_w[s_], t_memT], cols=(0, MEM))
                    act(KCT[:, c, :], PS[:, bk, 0:MEM], AF.Copy, [t_ps[bk]], [t_KCT[c]])
            for hf in range(2):
                s_ = wslot()
                wload(s_, [(wview(s_, 8, 512), wsrc(ckv_d[l], D + hf * 512, 512))])
                for mt in range(2):
                    bk = psum()[0]
                    for k in range(KC):
                        mm(bk, memT[:, k, mt * 128:(mt + 1) * 128], wview(s_, 8, 512)[:, k, :], k == 0, k == KC - 1,
                           [t_w[s_], t_memT])
                    cp(VC[:, mt, hf * 512:(hf + 1) * 512], PS[:, bk, :], [t_ps[bk]], [t_VC[mt]])

        def cross(l, seg):
            mk.mark('cross')
            QCT = av(0, [KC, TS], BF16)
            OT = av(16384, [KC, TS], BF16)
            EE = av(32768, [4, 512], BF16)
            LG = av(36864, [2, 512], F32)
            t_Q = mk.tiles(KC, NG)
            t_O = mk.tiles(KC, NG)
            t_E = mk.tiles(4)
            t_LG = mk.tiles(2)
            for ob in range(2):
                s_ = wslot()
                wload(s_, [(wview(s_, 8, 512), wsrc(cq_d[l], ob * 512, 512))])
                for cc in range(4):
                    c = ob * 4 + cc
                    for g in range(NG):
                        bk = psum()[0]
                        for k in range(KC):
                            mm(bk, wview(s_, 8, 512)[:, k, cc * 128:(cc + 1) * 128], HT[:, k, g * 512:(g + 1) * 512],
                               k == 0, k == KC - 1, [t_w[s_], t_HT[k][g]])
                        if (cc + g) % 2 == 0:
                            act(QCT[:, c, g * 512:(g + 1) * 512], PS[:, bk, :], AF.Copy, [t_ps[bk]], [t_Q[c][g]])
                        else:
                            cp(QCT[:, c, g * 512:(g + 1) * 512], PS[:, bk, :], [t_ps[bk]], [t_Q[c][g]])
            r = 0
            for hh in range(4):
                for g in range(NG):
                    eb = (r % 2) * 2
                    r += 1
                    for mt in range(2):
                        bk = psum()[0]
                        for kc in range(2):
                            c = 2 * hh + kc
                            mm(bk, KCT[:, c, mt * 128:(mt + 1) * 128], QCT[:, c, g * 512:(g + 1) * 512], kc == 0, kc == 1,
                               [t_KCT[c], t_Q[c][g]])
                        act(EE[:, eb + mt, :], PS[:, bk, :], AF.Exp, [t_ps[bk]], [t_E[eb + mt]], scale=1.0 / 16.0)
                    bd = psum()[0]
                    for mt in range(2):
                        mm(bd, ones[:, :], EE[:, eb + mt, :], mt == 0, mt == 1, [t_const, t_E[eb + mt]])
                    bn = []
                    for kc in range(2):
                        c = 2 * hh + kc
                        bk = psum()[0]
                        bn.append(bk)
                        for mt in range(2):
                            mm(bk, VC[:, mt, c * 128:(c + 1) * 128], EE[:, eb + mt, :], mt == 0, mt == 1,
                               [t_VC[mt], t_E[eb + mt]])
                    lb = (r % 2)
                    act(LG[:, lb, :], PS[:, bd, :], AF.Ln, [t_ps[bd]], [t_LG[lb]])
                    act(LG[:, lb, :], LG[:, lb, :], AF.Exp, [t_LG[lb]], [t_LG[lb]], scale=-1.0)
                    for kc in range(2):
                        c = 2 * hh + kc
                        tt(OT[:, c, g * 512:(g + 1) * 512], PS[:, bn[kc], :], LG[:, lb, :], ALU.mult,
                           [t_ps[bn[kc]], t_LG[lb]], [t_O[c][g]])
            mk.set_fence()
            out_proj_ln(l, seg, co_d[l], OT, t_O, KC, 2, 65536, rebuild=True)


        class Bump:
            def __init__(self, base):
                self.o = base

            def __call__(self, n):
                n = (n + 31) // 32 * 32
                r = self.o
                self.o += n
                assert self.o <= ARENA_BYTES, self.o
                return r

        def layer_prep(l):
            mk.dma("sp", sinkt[:, :], sinkcol_d[l], writes=[t_small])
            act(lv[:, 32:40], vecs[:, V_LAM:V_LAM + 8], AF.Exp, [t_vecs], [t_lv], scale=-1.0)
            act(lv[:, 32:40], lv[:, 32:40], AF.Ln, [t_lv, t_const], [t_lv], bias=epsc[:, 1:2], scale=1.0)
            ts(lv[:, 0:8], lv[:, 32:40], -4.0, None, ALU.mult, None, [t_lv], [t_lv])
            ts(lv[:, 8:16], vecs[:, V_BRG:V_BRG + 8], 0.5, None, ALU.mult, None, [t_vecs, t_lv], [t_lv])
            ts(lv[:, 16:24], vecs[:, V_BIG:V_BIG + 8], 0.5, None, ALU.mult, None, [t_vecs, t_lv], [t_lv])
            act(lv[:, 24:32], sinkt[:, :], AF.Exp, [t_small, t_lv], [t_lv])
            for c in range(KC):
                memset(CONVST[:, c, :], 0.0, [t_convst[c]])
                memset(SCANST[:, c:c + 1], 0.0, [t_scanst[c]])
            for gk in range(2):
                memset(KPREV[:, gk, :, :], 0.0, [t_kprev[gk]])
                memset(VPREV[:, gk, :, :], 0.0, [t_vprev[gk]])

        def rnn_phase(l, seg, YR, t_YR):
            mk.mark('rnn')
            o = Bump(32768)
            o2 = Bump(16384)
            DW = [av(o(2048), [2, 4, 128], BF16) for _ in range(2)]
            XRb = av(o(2 * (TS + 4) * 2), [2, TS + 4], BF16)
            XCb = av(o(2048), [2, 512], BF16)

            def two(a, b):
                return [av(a(4096), [2, 512], F32), av(b(4096), [2, 512], F32)]
            XC32 = two(o, o2)
            GL = two(o, o2)
            Rt = two(o, o2)
            It = two(o, o2)
            Aa = two(o, o)
            Mq = two(o, o)
            assert o2.o <= 32768
            t_DW = mk.tiles(2, 2)
            t_XR = mk.tiles(2, NG)
            t_XRh = mk.tiles(2)
            t_XCb = mk.tiles(2)
            t_XC32, t_GL, t_Rt, t_It, t_Aa, t_Mq = (mk.tiles(2, 2) for _ in range(6))
            its = [(n, g) for n in range(4) for g in range(NG)]
            blk = {}

            def front(i):
                n, g = its[i]
                p = i % 2
                dw, tdw = DW[n % 2], t_DW[n % 2]
                if g == 0:
                    sA = wslot()
                    wA = wview(sA, 8, 512)
                    wload(sA, [(wA[:, :, 0:256], wsrc(w_in_d[l], n * 256, 256)),
                               (wA[:, :, 256:512], wsrc(w_in_d[l], 1024 + n * 256, 256))])
                    sB = wslot()
                    rgv = wview(sB, 2, 256)
                    igv = wview(sB, 2, 256, off=512)
                    wload(sB, [(rgv, w_rg_d[l, n].rearrange("(k p) n -> p k n", p=128)),
                               (igv, w_ig_d[l, n].rearrange("(k p) n -> p k n", p=128))])
                    blk[n] = (sA, wA, sB, rgv, igv)
                    for cc in range(2):
                        c = 2 * n + cc
                        for tap in range(4):
                            ts(dw[:, cc, tap, :], ident[:, :], vecs[:, V_CONVW + tap * 8 + c:V_CONVW + tap * 8 + c + 1], None,
                               ALU.mult, None, [t_const, t_vecs], [tdw[cc]])
                        cp(XRb[:, cc, 0:3], CONVST[:, c, 0:3], [t_convst[c]], [t_XRh[cc]])
                sA, wA, sB, rgv, igv = blk[n]
                for cc in range(2):
                    bx = psum()[0]
                    for k in range(KC):
                        mm(bx, wA[:, k, cc * 128:(cc + 1) * 128], HT[:, k, g * 512:(g + 1) * 512], k == 0, k == KC - 1,
                           [t_w[sA], t_HT[k][g]])
                    cp(XRb[:, cc, 3 + g * 512:3 + (g + 1) * 512], PS[:, bx, :], [t_ps[bx]], [t_XR[cc][g]])
                    bg = psum()[0]
                    for k in range(KC):
                        mm(bg, wA[:, k, 256 + cc * 128:256 + (cc + 1) * 128], HT[:, k, g * 512:(g + 1) * 512], k == 0,
                           k == KC - 1, [t_w[sA], t_HT[k][g]])
                    act(GL[p][:, cc, :], PS[:, bg, :], AF.Gelu_apprx_tanh, [t_ps[bg]], [t_GL[p][cc]])
                for cc in range(2):
                    c = 2 * n + cc
                    bc = psum()[0]
                    rd = [tdw[cc], t_XR[cc][g], t_XR[cc][g - 1] if g > 0 else t_XRh[cc]]
                    for tap in range(4):
                        mm(bc, dw[:, cc, tap, :], XRb[:, cc, g * 512 + tap:g * 512 + tap + 512], tap == 0, tap == 3, rd)
                    act(XCb[:, cc, :], PS[:, bc, :], AF.Identity, [t_ps[bc], t_vecs], [t_XCb[cc]],
                        bias=vecs[:, V_CONVB + c:V_CONVB + c + 1], scale=1.0)
                    ts(XC32[p][:, cc, :], PS[:, bc, :], vecs[:, V_CONVB + c:V_CONVB + c + 1], None, ALU.add, None,
                       [t_ps[bc], t_vecs], [t_XC32[p][cc]])
                brs, bis = [], []
                for co in range(2):
                    br = psum()[0]
                    for kc in range(2):
                        mm(br, rgv[:, kc, co * 128:(co + 1) * 128], XCb[:, kc, :], kc == 0, kc == 1, [t_w[sB], t_XCb[kc]])
                    bi = psum()[0]
                    for kc in range(2):
                        mm(bi, igv[:, kc, co * 128:(co + 1) * 128], XCb[:, kc, :], kc == 0, kc == 1, [t_w[sB], t_XCb[kc]])
                    brs.append(br)
                    bis.append(bi)
                for co in range(2):
                    c = 2 * n + co
                    act(Rt[p][:, co, :], PS[:, brs[co], :], AF.Tanh, [t_ps[brs[co]], t_lv], [t_Rt[p][co]],
                        bias=lv[:, 8 + c:9 + c], scale=0.5)
                    act(It[p][:, co, :], PS[:, bis[co], :], AF.Tanh, [t_ps[bis[co]], t_lv], [t_It[p][co]],
                        bias=lv[:, 16 + c:17 + c], scale=0.5)
                for co in range(2):
                    c = 2 * n + co
                    act(Aa[p][:, co, :], Rt[p][:, co, :], AF.Exp, [t_Rt[p][co], t_lv], [t_Aa[p][co]], bias=lv[:, c:c + 1],
                        scale=lv[:, c:c + 1])
                for co in range(2):
                    act(Mq[p][:, co, :], Aa[p][:, co, :], AF.Square, [t_Aa[p][co]], [t_Mq[p][co]])
                for co in range(2):
                    act(Mq[p][:, co, :], Mq[p][:, co, :], AF.Ln, [t_Mq[p][co], t_const], [t_Mq[p][co]], bias=epsc[:, 1:2], scale=-1.0)
                for co in range(2):
                    act(Mq[p][:, co, :], Mq[p][:, co, :], AF.Exp, [t_Mq[p][co], t_const], [t_Mq[p][co]], bias=epsc[:, 2:3], scale=0.5)
                if g == NG - 1:
                    for cc in range(2):
                        c = 2 * n + cc
                        cp(CONVST[:, c, 0:3], XRb[:, cc, TS:TS + 3], [t_XR[cc][NG - 1]], [t_convst[c]])

            def tail(i):
                n, g = its[i]
                p = i % 2
                for co in range(2):
                    c = 2 * n + co
                    stt(It[p][:, co, :], It[p][:, co, :], 1.0, XC32[p][:, co, :], ALU.add, ALU.mult,
                        [t_It[p][co], t_XC32[p][co]], [t_It[p][co]])
                    tt(It[p][:, co, :], It[p][:, co, :], Mq[p][:, co, :], ALU.mult, [t_It[p][co], t_Mq[p][co]], [t_It[p][co]])
                    mk.op("dve", lambda e, co=co, c=c: e.tensor_tensor_scan(
                        out=XC32[p][:, co, :], data0=Aa[p][:, co, :], data1=It[p][:, co, :], initial=SCANST[:, c:c + 1],
                        op0=ALU.mult, op1=ALU.add), reads=[t_Aa[p][co], t_It[p][co], t_scanst[c]], writes=[t_XC32[p][co]])
                    cp(SCANST[:, c:c + 1], XC32[p][:, co, 511:512], [t_XC32[p][co]], [t_scanst[c]])
                    tt(YR[:, c, g * 512:(g + 1) * 512], XC32[p][:, co, :], GL[p][:, co, :], ALU.mult,
                       [t_XC32[p][co], t_GL[p][co]], [t_YR[c][g]])

            front(0)
            for i in range(1, len(its)):
                front(i)
                tail(i - 1)
            tail(len(its) - 1)

        def att_phase(l, seg, YA, t_YA):
            mk.mark('att')
            o = Bump(32768)
            ROPE = av(o(8192), [2, TS], F32)
            KA = av(o(2 * (128 + TS)), [128 + TS], BF16)
            KB = av(o(2 * (128 + TS)), [128 + TS], BF16)
            VA = av(o(2 * (1 + NT) * 256), [1 + NT, 2, 128], BF16)
            VB = av(o(2 * (1 + NT) * 256), [1 + NT, 2, 128], BF16)
            QT = av(o(8 * TS), [NT, 4, 128], BF16)
            QRAW = av(o(1024), [512], BF16)
            T1 = av(o(2048), [512], F32)
            T2 = av(o(2048), [512], F32)
            EE2 = [av(o(4096), [2, 2, 512], BF16), av(o(4096), [2, 2, 512], BF16)]
            LG = av(o(2048), [512], F32)
            t_rope, t_KA, t_KB, t_VA, t_VB, t_QRAW, t_T1, t_T2, t_LG = (mk.tile() for _ in range(9))
            t_QT = mk.tiles(NT)
            t_E2 = mk.tiles(2, 2)
            mk.dma("sp", ROPE, rope_d[:, :, seg * TS:(seg + 1) * TS].rearrange("a p s -> p a s"), writes=[t_rope])

            rp = [0]
            T12 = [T1, T2]
            t_T12 = [t_T1, t_T2]

            def roped(bk, dsts, g, rd_extra):
                p = rp[0] % 2
                rp[0] += 1
                Tp, t_Tp = T12[p], t_T12[p]
                act(QRAW, PS[:, bk, :], AF.Copy, [t_ps[bk]], [t_QRAW])
                bs = psum()[0]
                mm(bs, pmT[:, :], QRAW, True, True, [t_const, t_QRAW])
                tt(Tp, PS[:, bk, :], ROPE[:, 0, g * 512:(g + 1) * 512], ALU.mult, [t_ps[bk], t_rope], [t_Tp])
                tt(PS[:, bs, :], PS[:, bs, :], ROPE[:, 1, g * 512:(g + 1) * 512], ALU.mult, [t_ps[bs], t_rope], [t_ps[bs]])
                for (out_ap, p0, p1, shp, wr) in dsts:
                    i0, i1 = PS[p0:p1, bs, :], Tp[p0:p1, :]
                    if shp is not None:
                        i0 = i0.rearrange("p (a b) -> p a b", a=shp)
                        i1 = i1.rearrange("p (a b) -> p a b", a=shp)
                    tt(out_ap, i0, i1, ALU.add, [t_ps[bs], t_Tp], wr)

            memset(VA[:, :, :, :], 0.0, [t_VA])
            memset(VB[:, :, :, :], 0.0, [t_VB])
            for gk in range(2):
                memset(KA[64:128, :], 0.0, [t_KA])
                memset(KB[0:64, :], 0.0, [t_KB])
                cp(KA[0:64, 0:128], KPREV[0:64, gk, 0, :], [t_kprev[gk]], [t_KA])
                cp(KB[64:128, 0:128], KPREV[64:128, gk, 1, :], [t_kprev[gk]], [t_KB])
                sK = wslot()
                wK = wview(sK, 8, 128)
                wV = wview(sK, 8, 128, off=1024)
                pairs = [(wK, wsrc(wk2_d[l], gk * 128, 128))]
                if gk == 0:
                    pairs.append((wV, wsrc(w_in_d[l], 3200, 128)))
                wload(sK, pairs)
                for g in range(NG):
                    bk = psum()[0]
                    for k in range(KC):
                        mm(bk, wK[:, k, :], HT[:, k, g * 512:(g + 1) * 512], k == 0, k == KC - 1, [t_w[sK], t_HT[k][g]])
                    c0 = 128 + g * 512
                    roped(bk, [(KA[0:64, c0:c0 + 512], 0, 64, None, [t_KA]), (KB[64:128, c0:c0 + 512], 64, 128, None, [t_KB])], g, None)
                if _KCUT < 12:
                    continue
                if gk == 0:
                    for i in range(2):
                        cp(VA[:, 0, i, :], VPREV[:, i, 0, :], [t_vprev[i]], [t_VA])
                        cp(VB[:, 0, i, :], VPREV[:, i, 1, :], [t_vprev[i]], [t_VB])
                    for tk in range(NT):
                        bv = psum()[0]
                        for k in range(KC):
                            mm(bv, HT[:, k, tk * 128:(tk + 1) * 128], wV[:, k, :], k == 0, k == KC - 1,
                               [t_w[sK], t_HT[k][tk // 4]], cols=(0, 128))
                        src = PS[:, bv, 0:128].rearrange("p (g d) -> p g d", g=2)
                        act(VA[:, 1 + tk, :, 0:64], src, AF.Copy, [t_ps[bv]], [t_VA])
                        cp(VB[:, 1 + tk, :, 64:128], src, [t_ps[bv]], [t_VB])
                if _KCUT < 13:
                    continue
                sQ = wslot()
                wQ = wview(sQ, 8, 512)
                wload(sQ, [(wQ, wsrc(w_in_d[l], 2048 + gk * 512, 512))])
                jobs = [(cc, g) for cc in range(4) for g in range(NG)]

                def q_proj(j):
                    cc, g = jobs[j]
                    bq = psum()[0]
                    for k in range(KC):
                        mm(bq, wQ[:, k, cc * 128:(cc + 1) * 128], HT[:, k, g * 512:(g + 1) * 512], k == 0, k == KC - 1,
                           [t_w[sQ], t_HT[k][g]])
                    return bq
                qb_banks = {0: q_proj(0)}
                for j in range(len(jobs)):
                    if j + 1 < len(jobs):
                        qb_banks[j + 1] = q_proj(j + 1)
                    cc, g = jobs[j]
                    roped(qb_banks[j], [(QT[:, g * 4:(g + 1) * 4, cc, :], 0, 128, 4, t_QT[g * 4:(g + 1) * 4])], g, None)
                def kblocks(qb):
                    gb = seg * NT + qb
                    kb = []
                    if gb > 0:
                        kb.append((0, qb * 128, qb, maskp))
                    kb.append((1, (qb + 1) * 128, qb + 1, maskc))
                    return kb

                def emit_scores(qb):
                    qrhs = QT[:, qb, :, :].rearrange("p a b -> p (a b)")
                    EE, t_E = EE2[qb % 2], t_E2[qb % 2]
                    for (slot, kcol, vt, mask) in kblocks(qb):
                        sc = [2 * slot, 2 * slot + 1]
                        for hf in range(2):
                            mm(sc[hf], ident[:, :], mask[:, :], True, False, [t_const])
                            kk_ = KA if hf == 0 else KB
                            mm(sc[hf], kk_[:, kcol:kcol + 128], qrhs, False, True, [t_KA if hf == 0 else t_KB, t_QT[qb]])
                        act(EE[:, slot, :, :], PS[:, sc[0]:sc[0] + 2, :], AF.Exp, [t_ps[sc[0]], t_ps[sc[1]]], [t_E[slot]], scale=0.125)

                def emit_nd(qb):
                    EE, t_E = EE2[qb % 2], t_E2[qb % 2]
                    kb = kblocks(qb)
                    bnum, bden = (4, 5) if qb % 2 == 0 else (6, 7)
                    nmm = 2 * len(kb)
                    i = 0
                    for (slot, kcol, vt, mask) in kb:
                        for hf in range(2):
                            vv = VA if hf == 0 else VB
                            mm(bnum, vv[:, vt, gk, :], EE[:, slot, hf, :], i == 0, i == nmm - 1,
                               [t_VA if hf == 0 else t_VB, t_E[slot]])
                            i += 1
                    i = 0
                    for (slot, kcol, vt, mask) in kb:
                        for hf in range(2):
                            mm(bden, (onesA if hf == 0 else onesB)[:, :], EE[:, slot, hf, :], i == 0, i == nmm - 1,
                               [t_const, t_E[slot]])
                            i += 1
                    for cc in range(4):
                        j = 24 + gk * 4 + cc
                        act(LG[:, cc * 128:(cc + 1) * 128], PS[:, bden, cc * 128:(cc + 1) * 128], AF.Ln, [t_ps[bden], t_lv], [t_LG],
                            bias=lv[:, j:j + 1], scale=1.0)
                    act(LG, LG, AF.Exp, [t_LG], [t_LG], scale=-1.0)
                    tt(YA[:, gk * 4:(gk + 1) * 4, qb * 128:(qb + 1) * 128], PS[:, bnum, :].rearrange("p (a b) -> p a b", a=4),
                       LG.rearrange("p (a b) -> p a b", a=4), ALU.mult, [t_ps[bnum], t_LG],
                       [t_YA[gk * 4 + cc][qb // 4] for cc in range(4)])

                emit_scores(0)
                for qb in range(1, NT):
                    emit_scores(qb)
                    emit_nd(qb - 1)
                emit_nd(NT - 1)
                cp(KPREV[0:64, gk, 0, :], KA[0:64, TS:TS + 128], [t_KA], [t_kprev[gk]])
                cp(KPREV[64:128, gk, 1, :], KB[64:128, TS:TS + 128], [t_KB], [t_kprev[gk]])
            for i in range(2):
                cp(VPREV[:, i, 0, :], VA[:, NT, i, :], [t_VA], [t_vprev[i]])
                cp(VPREV[:, i, 1, :], VB[:, NT, i, :], [t_VB], [t_vprev[i]])

        def merge_phase(l, seg, YR, t_YR, YA, t_YA, M, t_M):
            mk.mark('merge')
            TM = av(49152, [4, TS], F32)
            SG = av(65536, [2, 512], F32)
            V2 = av(69632, [512], F32)
            t_TM = mk.tiles(4, NG)
            t_SG = mk.tiles(2)
            t_V2 = mk.tile()
            r = 0
            for ob in range(2):
                for br in range(2):
                    Ysrc, t_Ysrc = (YR, t_YR) if br == 0 else (YA, t_YA)
                    w1 = w_brr_d[l] if br == 0 else w_bra_d[l]
                    gcol = (3328 if br == 0 else 4352) + ob * 512
                    s1 = wslot()
                    wload(s1, [(wview(s1, 8, 512), wsrc(w1, ob * 512, 512))])
                    s2 = wslot()
                    wload(s2, [(wview(s2, 8, 512), wsrc(w_in_d[l], gcol, 512))])
                    for cc in range(4):
                        c = ob * 4 + cc
                        for g in range(NG):
                            bz = psum()[0]
                            for k in range(KC):
                                mm(bz, wview(s1, 8, 512)[:, k, cc * 128:(cc + 1) * 128], Ysrc[:, k, g * 512:(g + 1) * 512],
                                   k == 0, k == KC - 1, [t_w[s1], t_Ysrc[k][g]])
                            bg = psum()[0]
                            for k in range(KC):
                                mm(bg, wview(s2, 8, 512)[:, k, cc * 128:(cc + 1) * 128], HT[:, k, g * 512:(g + 1) * 512],
                                   k == 0, k == KC - 1, [t_w[s2], t_HT[k][g]])
                            b = r % 2
                            r += 1
                            act(SG[:, b, :], PS[:, bg, :], AF.Sigmoid, [t_ps[bg]], [t_SG[b]])
                            if br == 0:
                                tt(TM[:, cc, g * 512:(g + 1) * 512], PS[:, bz, :], SG[:, b, :], ALU.mult,
                                   [t_ps[bz], t_SG[b]], [t_TM[cc][g]])
                            else:
                                tt(V2, PS[:, bz, :], SG[:, b, :], ALU.mult, [t_ps[bz], t_SG[b]], [t_V2])
                                tt(M[:, c, g * 512:(g + 1) * 512], V2, TM[:, cc, g * 512:(g + 1) * 512], ALU.add,
                                   [t_V2, t_TM[cc][g]], [t_M[c][g]])

        def mixer(l, seg):
            mk.set_fence()
            YR = av(0, [KC, TS], BF16)
            YA = av(16384, [KC, TS], BF16)
            t_YR = mk.tiles(KC, NG)
            t_YA = mk.tiles(KC, NG)
            if mix_stage & 1:
                rnn_phase(l, seg, YR, t_YR)
            mk.set_fence()
            if mix_stage & 2:
                att_phase(l, seg, YA, t_YA)
            mk.set_fence()
            if not (mix_stage & 4):
                return
            M = av(32768, [KC, TS], BF16)
            t_M = mk.tiles(KC, NG)
            merge_phase(l, seg, YR, t_YR, YA, t_YA, M, t_M)
            mk.set_fence()
            out_proj_ln(l, seg, w_out_d[l], M, t_M, KC, 0, 49152, rebuild=True)

        if "cross" in subs:
            prep_mem()
        for l in range(L):
            mk.dma("sp", vecs[:, :], vecs_d[l], writes=[t_vecs])
            if "cross" in subs:
                cross_kv(l)
            if "mixer" in subs:
                layer_prep(l)
            for seg in range(NSEG):
                mk.set_fence()
                if (l == 0 and seg == 0) or "ffn" not in subs:
                    make_HT(seg, 0)
                    mk.set_fence()
                if "mixer" in subs:
                    mixer(l, seg)
                if "cross" in subs:
                    cross(l, seg)
                if "ffn" in subs:
                    ffn(l, seg)

        yv = y_d.rearrange("(t p) d -> p t d", p=128)
        t_out = mk.tile()
        for t0 in range(0, NTT, 4):
            mk.dma("sp", yv[:, t0:t0 + 4, :], H[:, t0:t0 + 4, :], reads=t_H[t0:t0 + 4], writes=[t_out])
        sp = mk.E["sp"]
        for i in range(NDMA):
            if mk.dtot[i] > 0 and sp.waited.get(("d", i), 0) < mk.dtot[i]:
                sp.h.wait_ge(mk.dsem[i], mk.dtot[i])
    return nc, mk


def prep_inputs(inputs, S, L):
    f32 = np.float32
    g = {k: np.asarray(v, dtype=f32) for k, v in inputs.items() if k not in ("x", "mem")}
    w_in = g["w_in"][:L]
    kcols = w_in[:, :, 3072:3200]
    wk2 = np.concatenate([kcols[:, :, 0:64], kcols[:, :, 0:64], kcols[:, :, 64:128], kcols[:, :, 64:128]], axis=2)
    vecs = np.zeros((L, 128, NV), f32)
    for l in range(L):
        for tap in range(4):
            vecs[l, :, V_CONVW + tap * 8:V_CONVW + tap * 8 + 8] = _fm(g["conv_w"][l, tap])
        vecs[l, :, V_CONVB:V_CONVB + 8] = _fm(g["conv_b"][l])
        vecs[l, :, V_BRG:V_BRG + 8] = _fm(g["b_rg"][l])
        vecs[l, :, V_BIG:V_BIG + 8] = _fm(g["b_ig"][l])
        vecs[l, :, V_LAM:V_LAM + 8] = _fm(g["lru_lambda"][l])
    lnrow = np.stack([g["ln1_g"][:L], g["ln1_b"][:L], g["ln2_g"][:L], g["ln2_b"][:L], g["ln3_g"][:L], g["ln3_b"][:L]], axis=1)
    sinkcol = np.zeros((L, 128, 8), f32)
    for l in range(L):
        for j in range(8):
            sinkcol[l, 0:64, j] = g["sinks"][l, 2 * j]
            sinkcol[l, 64:128, j] = g["sinks"][l, 2 * j + 1]
    consts, rope = _consts(S)
    shared = {
        "w_in": np.ascontiguousarray(w_in), "wk2": np.ascontiguousarray(wk2),
        "w_rg": g["w_rg"][:L], "w_ig": g["w_ig"][:L], "w_br_rnn": g["w_br_rnn"][:L], "w_br_attn": g["w_br_attn"][:L],
        "w_out": g["w_out"][:L], "cq_w": g["cq_w"][:L], "ckv_w": g["ckv_w"][:L], "co_w": g["co_w"][:L],
        "ffn_wi": g["ffn_wi"][:L], "ffn_wo": g["ffn_wo"][:L], "vecs": vecs, "lnrow": np.ascontiguousarray(lnrow),
        "sinkcol": sinkcol, "consts": consts, "rope": rope,
    }
    return shared


_CACHE = {}


def kernel(**inputs):
    x = np.asarray(inputs["x"], dtype=np.float32)
    mem = np.asarray(inputs["mem"], dtype=np.float32)
    B, S, _ = x.shape
    L = inputs["w_in"].shape[0]
    key = (S, L)
    if key not in _CACHE:
        _CACHE[key] = build_program(S=S, DEPTH=L)[0]
    nc = _CACHE[key]
    shared = prep_inputs(inputs, S, L)
    in_maps = []
    for b in range(B):
        m = dict(shared)
        m["x"] = np.ascontiguousarray(x[b])
        m["mem"] = np.ascontiguousarray(mem[b])
        in_maps.append(m)
    res = run_bass_kernel_spmd(nc, in_maps, core_ids=list(range(B)))
    return np.stack([r["y"] for r in res.results], axis=0).astype(np.float32)
```
